# Optimizing a Trainium2 kernel written in Bass

```python
import jax, jax.numpy as jnp
from jax import lax
import numpy as np

D_MODEL = 2048
BATCH = 2
SEQ = 8192
DEPTH = 1

GDN_HEADS = 8
GDN_DK = 128
GDN_DV = 128
GDN_CONV = 4
GDN_CHUNK = 64
FOX_HEADS = 8
FOX_DH = 128
FOX_BLOCK = 128
MEM_LEN = 256
MEM_HEADS = 4
MEM_DH = 256
D_FF = 4 * D_MODEL
N_BRANCH = 3
EPS = 1e-6

GDN_QK = GDN_HEADS * GDN_DK
GDN_V = GDN_HEADS * GDN_DV
GDN_QKV = 2 * GDN_QK + GDN_V
FOX_W = FOX_HEADS * FOX_DH
MEM_W = MEM_HEADS * MEM_DH
IN_SPLITS = (GDN_QKV, GDN_V, GDN_HEADS, GDN_HEADS, FOX_W, FOX_W, FOX_W, FOX_HEADS, MEM_W, N_BRANCH * D_MODEL)
D_IN = 2 * GDN_QK + 2 * GDN_V + 2 * GDN_HEADS + 3 * FOX_W + FOX_HEADS + MEM_W + N_BRANCH * D_MODEL

kernel_name = "hybrid_gdn_fox_memory_block"


def rms_norm(x, g):
    xf = x.astype(jnp.float32)
    y = xf * lax.rsqrt(jnp.mean(xf * xf, axis=-1, keepdims=True) + EPS)
    return (y * g.astype(jnp.float32)).astype(x.dtype)


def l2_norm(x):
    return x * lax.rsqrt(jnp.sum(x * x, axis=-1, keepdims=True) + EPS)


def to_heads(t, n_heads):
    b, s, _ = t.shape
    return t.reshape(b, s, n_heads, -1).transpose(0, 2, 1, 3)


def causal_conv_silu(x, w):
    k_w = w.shape[0]
    s = x.shape[1]
    xp = jnp.pad(x, ((0, 0), (k_w - 1, 0), (0, 0)))
    y = xp[:, 0:s] * w[0]
    for i in range(1, k_w):
        y = y + xp[:, i:i + s] * w[i]
    return jax.nn.silu(y)


def gated_delta_rule(q, k, v, g, beta):
    b, h, s, dk = q.shape
    dv = v.shape[-1]
    c = GDN_CHUNK
    n = s // c
    q = q.reshape(b, h, n, c, dk)
    k = k.reshape(b, h, n, c, dk)
    v = v.reshape(b, h, n, c, dv)
    beta = beta.reshape(b, h, n, c)
    gam = jnp.cumsum(g.reshape(b, h, n, c), axis=-1)
    idx = jnp.arange(c)
    strict = idx[:, None] > idx[None, :]
    incl = idx[:, None] >= idx[None, :]
    diff = gam[..., :, None] - gam[..., None, :]
    dec_strict = jnp.where(strict, jnp.exp(jnp.where(strict, diff, 0.0)), 0.0)
    dec_incl = jnp.where(incl, jnp.exp(jnp.where(incl, diff, 0.0)), 0.0)
    m_low = beta[..., :, None] * jnp.einsum('bhnid,bhnjd->bhnij', k, k) * dec_strict
    a_mat = jnp.eye(c, dtype=jnp.float32) + m_low
    rhs = jnp.concatenate([(beta * jnp.exp(gam))[..., None] * k, beta[..., None] * v], axis=-1)
    sol = lax.linalg.triangular_solve(a_mat, rhs, left_side=True, lower=True, unit_diagonal=True)
    w_c, u_c = sol[..., :dk], sol[..., dk:]
    qk = jnp.einsum('bhnid,bhnjd->bhnij', q, k) * dec_incl
    q_dec = q * jnp.exp(gam)[..., None]
    k_dec = k * jnp.exp(gam[..., -1:] - gam)[..., None]
    chunk_dec = jnp.exp(gam[..., -1])

    def step(state, xs):
        w_i, u_i, qk_i, qd_i, kd_i, cd_i = xs
        u = u_i - jnp.einsum('bhid,bhde->bhie', w_i, state)
        o = jnp.einsum('bhid,bhde->bhie', qd_i, state) + jnp.einsum('bhij,bhje->bhie', qk_i, u)
        state = cd_i[..., None, None] * state + jnp.einsum('bhid,bhie->bhde', kd_i, u)
        return state, o

    xs = tuple(jnp.moveaxis(t, 2, 0) for t in (w_c, u_c, qk, q_dec, k_dec, chunk_dec))
    _, o = lax.scan(step, jnp.zeros((b, h, dk, dv), jnp.float32), xs)
    return jnp.moveaxis(o, 0, 2).reshape(b, h, s, dv)


def forgetting_attention(q, k, v, log_f):
    _, _, s, d = q.shape
    cum = jnp.cumsum(log_f, axis=-1)
    scale = d ** -0.5
    outs = []
    for start in range(0, s, FOX_BLOCK):
        end = start + FOX_BLOCK
        logits = jnp.einsum('bhqd,bhkd->bhqk', q[:, :, start:end], k[:, :, :end]).astype(jnp.float32) * scale
        logits = logits + cum[:, :, start:end, None] - cum[:, :, None, :end]
        mask = (start + jnp.arange(FOX_BLOCK))[:, None] >= jnp.arange(end)[None, :]
        p = jax.nn.softmax(jnp.where(mask, logits, -jnp.inf), axis=-1)
        outs.append(jnp.einsum('bhqk,bhkd->bhqd', p.astype(v.dtype), v[:, :, :end]))
    return jnp.concatenate(outs, axis=2)


def setup_inputs(seed: int = 0) -> dict:
    key = jax.random.key(seed)
    ks = jax.random.split(key, 24)
    L, D = DEPTH, D_MODEL
    nrm = lambda k, shape, fan_in: jax.random.normal(k, shape, jnp.float32) * (fan_in ** -0.5)
    gain = lambda k, shape: 1.0 + 0.02 * jax.random.normal(k, shape, jnp.float32)
    a_log = jnp.log(jax.random.uniform(ks[5], (L, GDN_HEADS), jnp.float32, 1.0, 16.0))
    dt = jnp.exp(jax.random.uniform(ks[6], (L, GDN_HEADS), jnp.float32, np.log(1e-3), np.log(1e-1)))
    dt_bias = dt + jnp.log(-jnp.expm1(-dt))
    return {
        "x": jax.random.normal(ks[0], (BATCH, SEQ, D), jnp.float32),
        "mem": jax.random.normal(ks[1], (BATCH, MEM_LEN, D), jnp.float32),
        "g_mix": gain(ks[2], (L, D)),
        "w_in": nrm(ks[3], (L, D, D_IN), D),
        "conv_w": nrm(ks[4], (L, GDN_CONV, GDN_QKV), GDN_CONV),
        "a_log": a_log,
        "dt_bias": dt_bias,
        "gdn_norm_g": gain(ks[7], (L, GDN_DV)),
        "fox_b_f": jax.random.uniform(ks[8], (L, FOX_HEADS), jnp.float32, 1.0, 4.0),
        "fox_q_norm": gain(ks[9], (L, FOX_DH)),
        "fox_k_norm": gain(ks[10], (L, FOX_DH)),
        "g_mem": gain(ks[11], (L, D)),
        "w_mem_kv": nrm(ks[12], (L, D, 2 * MEM_W), D),
        "mem_q_norm": gain(ks[13], (L, MEM_DH)),
        "mem_k_norm": gain(ks[14], (L, MEM_DH)),
        "w_up_gdn": nrm(ks[15], (L, GDN_V, D), GDN_V),
        "w_up_fox": nrm(ks[16], (L, FOX_W, D), FOX_W),
        "w_up_mem": nrm(ks[17], (L, MEM_W, D), MEM_W),
        "w_out": nrm(ks[18], (L, D, D), D),
        "g_mlp": gain(ks[19], (L, D)),
        "w_ff1": nrm(ks[20], (L, D, D_FF), D),
        "w_ff2": nrm(ks[21], (L, D_FF, D), D_FF),
    }


def reference(x, mem, g_mix, w_in, conv_w, a_log, dt_bias, gdn_norm_g, fox_b_f, fox_q_norm, fox_k_norm,
              g_mem, w_mem_kv, mem_q_norm, mem_k_norm, w_up_gdn, w_up_fox, w_up_mem, w_out, g_mlp, w_ff1, w_ff2):
    b, s, _ = x.shape
    splits = np.cumsum(IN_SPLITS)[:-1].tolist()
    f32 = jnp.float32
    for l in range(DEPTH):
        h = rms_norm(x, g_mix[l])
        proj = h @ w_in[l]
        qkv_a, z_a, b_a, a_a, q_b, k_b, v_b, f_b, q_m, gates = jnp.split(proj, splits, axis=-1)

        qkv_a = causal_conv_silu(qkv_a, conv_w[l])
        q_a, k_a, v_a = jnp.split(qkv_a, [GDN_QK, 2 * GDN_QK], axis=-1)
        q_a = l2_norm(to_heads(q_a, GDN_HEADS).astype(f32)) * (GDN_DK ** -0.5)
        k_a = l2_norm(to_heads(k_a, GDN_HEADS).astype(f32))
        v_a = to_heads(v_a, GDN_HEADS).astype(f32)
        beta = jax.nn.sigmoid(b_a.astype(f32)).transpose(0, 2, 1)
        g_dec = (-jnp.exp(a_log[l].astype(f32)) * jax.nn.softplus(a_a.astype(f32) + dt_bias[l].astype(f32))).transpose(0, 2, 1)
        o_a = gated_delta_rule(q_a, k_a, v_a, g_dec, beta).transpose(0, 2, 1, 3)
        o_a = rms_norm(o_a, gdn_norm_g[l]) * jax.nn.silu(z_a.astype(f32).reshape(b, s, GDN_HEADS, GDN_DV))
        o_a = o_a.reshape(b, s, GDN_V).astype(x.dtype)

        q_bh = rms_norm(to_heads(q_b, FOX_HEADS), fox_q_norm[l])
        k_bh = rms_norm(to_heads(k_b, FOX_HEADS), fox_k_norm[l])
        v_bh = to_heads(v_b, FOX_HEADS)
        log_f = jax.nn.log_sigmoid(f_b.astype(f32) + fox_b_f[l].astype(f32)).transpose(0, 2, 1)
        o_b = forgetting_attention(q_bh, k_bh, v_bh, log_f)
        o_b = o_b.transpose(0, 2, 1, 3).reshape(b, s, FOX_W)

        kv_m = rms_norm(mem, g_mem[l]) @ w_mem_kv[l]
        k_m, v_m = jnp.split(kv_m, 2, axis=-1)
        q_mh = rms_norm(to_heads(q_m, MEM_HEADS), mem_q_norm[l])
        k_mh = rms_norm(to_heads(k_m, MEM_HEADS), mem_k_norm[l])
        v_mh = to_heads(v_m, MEM_HEADS)
        logits_m = jnp.einsum('bhqd,bhkd->bhqk', q_mh, k_mh).astype(f32) * (MEM_DH ** -0.5)
        p_m = jax.nn.softmax(logits_m, axis=-1).astype(v_mh.dtype)
        o_m = jnp.einsum('bhqk,bhkd->bhqd', p_m, v_mh).transpose(0, 2, 1, 3).reshape(b, s, MEM_W)

        gate_a, gate_b, gate_m = jnp.split(jax.nn.sigmoid(gates), N_BRANCH, axis=-1)
        y = gate_a * (o_a @ w_up_gdn[l]) + gate_b * (o_b @ w_up_fox[l]) + gate_m * (o_m @ w_up_mem[l])
        x = x + y @ w_out[l]

        h2 = rms_norm(x, g_mlp[l])
        x = x + jnp.square(jax.nn.relu(h2 @ w_ff1[l])) @ w_ff2[l]
    return x
```

```python
from contextlib import ExitStack, contextmanager
import types
import os
import numpy as np
import concourse.bass as bass
import concourse.mybir as mybir
from concourse.bass_utils import run_bass_kernel_spmd

F32 = mybir.dt.float32
BF16 = mybir.dt.bfloat16
AF = mybir.ActivationFunctionType
ALU = mybir.AluOpType

ENGS = ("pe", "act", "dve", "pool", "sp")
SAME_ENG_SYNC = bool(int(os.environ.get("KSES", "1")))
NDMA_SEM = 10
EPS = 1e-6
D = 2048
DFF = 8192
NEG = -30000.0
ARENA_F32 = 47 * 1024


class T:
    __slots__ = ("t", "name", "lws", "rd", "excl")

    def __init__(self, t, name, fence=None):
        self.t = t
        self.name = name
        self.excl = False
        self.lws = {}
        self.rd = dict(fence) if fence else {}

    def __getitem__(self, idx):
        return self.t[idx]


class Ev:
    __slots__ = ("kind", "key", "idx", "op")

    def __init__(self, kind, key, idx, op=None):
        self.kind = kind; self.key = key; self.idx = idx; self.op = op


class Op:
    __slots__ = ("eng", "fn", "waits", "mark", "cnt")

    def __init__(self, eng, fn):
        self.eng = eng; self.fn = fn; self.waits = []; self.mark = False; self.cnt = 0


def _freeze(fn):
    if fn is None or fn.__closure__ is None:
        return fn
    cells = []
    for c in fn.__closure__:
        try:
            cells.append(types.CellType(c.cell_contents))
        except ValueError:
            cells.append(c)
    return types.FunctionType(fn.__code__, fn.__globals__, fn.__name__, fn.__defaults__, tuple(cells))


def _merge(d, ev):
    o = d.get(ev.key)
    if o is None or o.idx < ev.idx:
        d[ev.key] = ev


class Prog:
    def __init__(self, nc):
        self.nc = nc
        self.es = ExitStack()
        self.ops = {e: [] for e in ENGS}
        self.seen = {e: {} for e in ENGS}
        self.esem = {e: self.es.enter_context(nc.semaphore("s_" + e)) for e in ENGS}
        self.dsem = {}
        self.dcount = {}
        self.dnext = {}
        for q in ("sp", "pool", "act"):
            self.dsem[q] = [self.es.enter_context(nc.semaphore(f"d_{q}{i}")) for i in range(NDMA_SEM)]
            self.dcount[q] = [0] * NDMA_SEM
            self.dnext[q] = 0
        self.csem = {}
        self.arena = None
        self.sp = 0
        self.sp_max = 0
        self.last_ev = {}
        self.ntile = 0
        self.fence = {}
        self.scopes = []

    def sbuf(self, shape, dt, name=None):
        self.ntile += 1
        name = (name or "t") + f"_{self.ntile}"
        if self.arena is None:
            self.arena = self.es.enter_context(self.nc.sbuf_tensor("arena", [128, ARENA_F32], F32))
        shape = list(shape)
        esz = 4 if dt == F32 else 2
        nfree = 1
        for d in shape[1:]:
            nfree *= d
        nbytes = (nfree * esz + 31) // 32 * 32
        off = self.sp
        self.sp += nbytes
        assert self.sp <= ARENA_F32 * 4, f"SBUF arena overflow: {self.sp} > {ARENA_F32 * 4} ({name})"
        self.sp_max = max(self.sp_max, self.sp)
        v = self.arena[0:shape[0], off // 4:(off + nbytes) // 4]
        if dt != F32:
            v = v.bitcast(dt)
        v = v[:, 0:nfree]
        if len(shape) == 3:
            v = v.rearrange("p (a b) -> p a b", b=shape[2])
        t = T(v, name, self.fence)
        if self.scopes:
            self.scopes[-1][1].append(t)
        return t

    def psum(self, shape, dt, name=None):
        self.ntile += 1
        name = (name or "p") + f"_{self.ntile}"
        t = T(self.es.enter_context(self.nc.psum_tensor(name, list(shape), dt)), name)
        t.excl = True
        return t

    def dram(self, name, shape, dt, kind="Internal"):
        return T(self.nc.dram_tensor(name, list(shape), dt, kind=kind), name)

    @contextmanager
    def scope(self):
        tiles = []
        self.scopes.append((None, tiles))
        sp0 = self.sp
        try:
            yield
        finally:
            self.scopes.pop()
            f = dict(self.fence)
            for t in tiles:
                for ev in t.lws.values():
                    _merge(f, ev)
                for ev in t.rd.values():
                    _merge(f, ev)
            self.fence = f
            self.sp = sp0

    def _need(self, eng, ev, op):
        if ev.kind == "e":
            if ev.key[1] == eng and (eng == "pe" or not SAME_ENG_SYNC):
                return
            if self.seen[eng].get(ev.key, -1) >= ev.idx:
                return
            self.seen[eng][ev.key] = ev.idx
            ev.op.mark = True
            op.waits.append(ev)
        else:
            if self.seen[eng].get(ev.key, -1) >= ev.idx:
                return
            self.seen[eng][ev.key] = ev.idx
            op.waits.append(ev)

    def _deps(self, eng, op, reads, writes, pw):
        need = {}
        for t in reads:
            for ev in t.lws.values():
                _merge(need, ev)
            if t.excl:
                for ev in t.rd.values():
                    if ev.kind != "e" or ev.key[1] != eng:
                        _merge(need, ev)
        for t in writes:
            for ev in t.lws.values():
                _merge(need, ev)
            for ev in t.rd.values():
                _merge(need, ev)
        for t in pw:
            for ev in t.rd.values():
                _merge(need, ev)
        for ev in need.values():
            self._need(eng, ev, op)

    def _record(self, ev, reads, writes, pw):
        for t in reads:
            _merge(t.rd, ev)
        for t in writes:
            t.lws = {ev.key: ev}
            t.rd = {}
        for t in pw:
            _merge(t.lws, ev)

    def _cut(self):
        self.nops = getattr(self, "nops", 0) + 1
        if os.environ.get("KLIST"):
            import inspect
            fr = inspect.stack()[2]
            print("OP", self.nops, inspect.stack()[1].function, fr.lineno, (fr.code_context or [""])[0].strip()[:110])
        return self.nops > int(os.environ.get("KCUT", "1000000000"))

    def op(self, eng, fn, reads=(), writes=(), pw=()):
        if self._cut():
            return
        o = Op(eng, _freeze(fn))
        self._deps(eng, o, reads, writes, pw)
        idx = len(self.ops[eng])
        self.ops[eng].append(o)
        ev = Ev("e", ("e", eng), idx, o)
        self.last_ev[eng] = ev
        self._record(ev, reads, writes, pw)
        return o

    def dma(self, q, out_ap, in_ap, reads=(), writes=(), pw=(), **kw):
        if self._cut():
            return
        o = Op(q, None)
        self._deps(q, o, reads, writes, pw)
        si = self.dnext[q]
        self.dnext[q] = (si + 1) % NDMA_SEM
        prev = self.dcount[q][si]
        key = ("d", q, si)
        if prev > 0:
            self._need(q, Ev("d", key, prev), o)
        val = prev + 1
        self.dcount[q][si] = val
        sem = self.dsem[q][si]
        o.fn = lambda e: e.dma_start(out=out_ap, in_=in_ap, **kw).then_inc(sem, 16)
        self.ops[q].append(o)
        self._record(Ev("d", key, val), reads, writes, pw)

    def custom(self, q, fn, reads=(), writes=(), pw=()):
        o = Op(q, None)
        self._deps(q, o, reads, writes, pw)
        sem = self.es.enter_context(self.nc.semaphore(f"c_{len(self.csem)}"))
        key = ("c", len(self.csem))
        self.csem[key] = sem
        fn = _freeze(fn)
        o.fn = lambda e: fn(e, sem)
        self.ops[q].append(o)
        self._record(Ev("d", key, 1), reads, writes, pw)

    def wait_all(self, eng, tiles):
        o = Op(eng, None)
        need = {}
        for t in tiles:
            for ev in t.lws.values():
                _merge(need, ev)
            for ev in t.rd.values():
                _merge(need, ev)
        for ev in self.last_ev.values():
            _merge(need, ev)
        for q in self.dcount:
            for si, cnt in enumerate(self.dcount[q]):
                if cnt > 0:
                    _merge(need, Ev("d", ("d", q, si), cnt))
        for ev in need.values():
            self._need(eng, ev, o)
        self.ops[eng].append(o)

    def emit(self):
        nc = self.nc
        for e in ENGS:
            c = 0
            for o in self.ops[e]:
                if o.mark:
                    c += 1
                    o.cnt = c
        P = self

        def run(eng_name, e):
            for o in P.ops[eng_name]:
                for ev in o.waits:
                    if ev.kind == "e":
                        e.wait_ge(P.esem[ev.key[1]], ev.op.cnt)
                    elif ev.key[0] == "c":
                        e.wait_ge(P.csem[ev.key], 1)
                    else:
                        e.wait_ge(P.dsem[ev.key[1]][ev.key[2]], 16 * ev.idx)
                if o.fn is None:
                    continue
                r = o.fn(e)
                if o.mark:
                    r.then_inc(P.esem[eng_name], 1)

        with nc.Block() as block:
            @block.tensor
            def _(e):
                run("pe", e)

            @block.scalar
            def _(e):
                run("act", e)

            @block.vector
            def _(e):
                run("dve", e)

            @block.gpsimd
            def _(e):
                run("pool", e)

            @block.sync
            def _(e):
                run("sp", e)
        self.es.close()


class Rot:
    def __init__(self, P, n, shape, dt, name):
        self.b = [P.sbuf(shape, dt, name) for _ in range(n)]
        self.i = 0

    def get(self):
        t = self.b[self.i]
        self.i = (self.i + 1) % len(self.b)
        return t


PR_GMIX, PR_GMLP, PR_GMEM = 0, 16, 32
PR_CONV = 48
PR_GDNG = 72
PR_FQN, PR_FKN = 73, 74
PR_MQN, PR_MKN = 75, 77
PR_ALOG, PR_DTB, PR_BF = 79, 81, 83
NPRM = 85

C_ID, C_TRI, C_TRIS, C_ONES, C_SEL, C_NM = 0, 128, 256, 384, 512, 640
NCST = 768


def make_consts():
    c = np.zeros((128, NCST), np.float32)
    i = np.arange(128)
    c[:, C_ID:C_ID + 128] = np.eye(128)
    c[:, C_TRI:C_TRI + 128] = (i[:, None] <= i[None, :])
    c[:, C_TRIS:C_TRIS + 128] = (i[:, None] < i[None, :])
    c[:, C_ONES:C_ONES + 128] = 1.0
    c[64, C_SEL:C_SEL + 128] = 1.0
    j = np.arange(64)
    nm = np.zeros((128, 128), np.float32)
    nm[:64, 0:64] = np.where(j[None, :] > j[:, None], 0.0, NEG)
    nm[:64, 64:128] = np.where(j[None, :] >= j[:, None], 0.0, NEG)
    c[:, C_NM:C_NM + 128] = nm
    return c


def relay(W):
    K, N = W.shape
    return np.ascontiguousarray(W.reshape(K // 128, 128, N // 128, 128).transpose(2, 1, 0, 3)).reshape(N // 128, 128, K)


def relay_tok(W):
    K, n = W.shape
    return np.ascontiguousarray(W.reshape(K // 128, 128, n).transpose(1, 0, 2)).reshape(128, (K // 128) * n)


def build(S, dbg=False, PH=(0, 1, 2, 3, 4, 5)):
    nc = bass.Bass("TRN2", target_bir_lowering=False)
    NTILE = S // 512
    J = S // 128
    NCH = S // 64
    TQ = S // 4
    NT4 = min(512, TQ)
    NSUB = TQ // NT4

    def din(name, shape, dt=F32):
        return T(nc.dram_tensor(name, list(shape), dt, kind="ExternalInput"), name)

    xT = din("xT", [D, S]); xq = din("xq", [D, TQ]); memT = din("memT", [D, 256])
    prm_d = din("prm", [128, NPRM]); cst_d = din("cst", [128, NCST])
    w1 = din("w1", [14, 128, D]); w1s = din("w1s", [128, 16 * 262])
    wmk = din("wmk", [2, 128, D]); wmv = din("wmv", [128, 16 * 256])
    if 5 in PH:
        wg = din("wg", [48, 128, D]); wup = din("wup", [48, 128, 1024])
        wout = din("wout", [16, 128, D]); wff1 = din("wff1", [64, 128, D]); wff2 = din("wff2", [16, 128, DFF])
    outT = T(nc.dram_tensor("outT", [D, TQ], F32, kind="ExternalOutput"), "outT")

    P = Prog(nc)
    skind = "ExternalOutput" if dbg else "Internal"
    graw = [P.dram(f"graw{i}", [128, S], F32, skind) for i in range(8)]
    foxq = [P.dram(f"foxq{i}", [128, S], BF16, skind) for i in range(2)]
    foxk = [P.dram(f"foxk{i}", [128, S], BF16, skind) for i in range(2)]
    foxv = P.dram("foxv", [S, 256], BF16, skind)
    smallT = P.dram("smallT", [6, S], F32, skind)
    NCT = S // NT4
    omine = P.dram("omine_i", [NCT, 768, NT4], BF16)
    gath = P.dram("gath", [NCT, 4 * 768, NT4], BF16)

    def omine_write(row0, t0, ob):
        for o in range(0, 512, NT4):
            ct = (t0 + o) // NT4
            P.dma("sp", omine[ct, row0:row0 + 128, :], ob[:, o:o + NT4], reads=[ob], pw=[omine])

    cst = P.sbuf([128, NCST], F32, "cst")
    prm = P.sbuf([128, NPRM], F32, "prm")
    cbf = P.sbuf([128, 512], BF16, "cbf")
    epsc = P.sbuf([128, 2], F32, "epsc")
    kTm = P.sbuf([128, 2, 256], BF16, "kTm")
    vm = P.sbuf([128, 2, 256], BF16, "vm")
    PS = [P.psum([128, 512], F32, f"ps{i}") for i in range(7)]
    PSB = P.psum([128, 1024], BF16, "psb")

    P.dma("sp", cst[:, :], cst_d[:, :], writes=[cst])
    P.dma("sp", prm[:, :], prm_d[:, :], writes=[prm])
    P.op("dve", lambda e: e.tensor_copy(cbf[:, :], cst[:, 0:512]), reads=[cst], writes=[cbf])
    P.op("pool", lambda e: e.memset(epsc[:, 0:1], EPS), writes=[epsc])
    P.op("pool", lambda e: e.memset(epsc[:, 1:2], 1.0), pw=[epsc])
    ident_bf = cbf[:, 0:128]; tri_bf = cbf[:, 128:256]; ones_bf = cbf[:, 384:512]

    def rstd_from_ss(ss_ps, n, inv_d, out_t, np_=128):
        P.op("act", lambda e: e.activation(out_t[0:np_, 0:n], ss_ps[0:np_, 0:n], AF.Sqrt, bias=epsc[0:np_, 0:1], scale=inv_d),
             reads=[ss_ps, epsc], writes=[out_t])
        P.op("dve", lambda e: e.reciprocal(out_t[0:np_, 0:n], out_t[0:np_, 0:n]), reads=[out_t], writes=[out_t])

    tmp2 = Rot(P, 3, [128, 512], F32, "tmp2")

    def stt2(out_t, out_ap, in_t, in_ap, gc, rs_tile, rs_ap, n, first):
        tm = tmp2.get()
        P.op("pool", lambda e: e.tensor_tensor(tm[:, 0:n], in_ap, rs_ap, ALU.mult), reads=[in_t, rs_tile], writes=[tm])
        P.op("dve", lambda e: e.tensor_scalar(out_ap, tm[:, 0:n], prm[:, gc:gc + 1], None, ALU.mult), reads=[tm, prm],
             writes=[out_t] if first else (), pw=() if first else [out_t])

    def rmsnorm_tile(xt, n, gcol, hT, sqrot, rs_t):
        ss = PS[0]
        for c in range(16):
            sq = sqrot.get()
            P.op("act", lambda e, c=c, sq=sq: e.activation(sq[:, 0:n], xt[:, c, 0:n], AF.Square), reads=[xt], writes=[sq])
            P.op("pe", lambda e, c=c, sq=sq: e.matmul(ss[:, 0:n], ones_bf, sq[:, 0:n], start=(c == 0), stop=(c == 15)),
                 reads=[cbf, sq], writes=[ss] if c == 0 else (), pw=() if c == 0 else [ss])
        rstd_from_ss(ss, n, 1.0 / D, rs_t)
        for c in range(16):
            stt2(hT, hT[:, c, 0:n], xt, xt[:, c, 0:n], gcol + c, rs_t, rs_t[:, 0:n], n, c == 0)

    cast_rr = [0]

    def wload(dst, src2d, n, stg):
        for o in range(0, n, 2048):
            m = min(2048, n - o)
            st = stg.get()
            P.dma("sp", st[:, 0:m], src2d[:, o:o + m], writes=[st])
            first = (o == 0)
            if cast_rr[0] % 2 == 0:
                P.op("act", lambda e, st=st, o=o, m=m: e.copy(dst[:, o:o + m], st[:, 0:m]), reads=[st], writes=[dst] if first else (), pw=() if first else [dst])
            else:
                P.op("pool", lambda e, st=st, o=o, m=m: e.tensor_copy(dst[:, o:o + m], st[:, 0:m]), reads=[st], writes=[dst] if first else (), pw=() if first else [dst])
        cast_rr[0] += 1

    def phase(n):
        if n in PH:
            with P.scope():
                yield

    for _ in phase(0):
        mt = P.sbuf([128, 16, 256], F32, "mt")
        mh = P.sbuf([128, 16, 256], BF16, "mh")
        sqrot = Rot(P, 3, [128, 512], BF16, "sq")
        rs_t = P.sbuf([128, 512], F32, "rs")
        wk_b = [P.sbuf([128, 2048], BF16, "wk") for _ in range(2)]
        wv_b = P.sbuf([128, 4096], BF16, "wv")
        stg = Rot(P, 2, [128, 2048], F32, "stg")
        kraw = P.sbuf([128, 2, 256], F32, "kraw")
        P.dma("sp", mt[:, :, :], memT[:, :].rearrange("(c p) n -> p c n", p=128), writes=[mt])
        for dc in range(2):
            wload(wk_b[dc], wmk[dc], 2048, stg)
        wload(wv_b, wmv, 4096, stg)
        rmsnorm_tile(mt, 256, PR_GMEM, mh, sqrot, rs_t)
        ss2 = PS[3]
        for dc in range(2):
            ps = PS[1 + dc]
            for k in range(16):
                P.op("pe", lambda e, k=k, dc=dc, ps=ps: e.matmul(ps[:, 0:256], wk_b[dc][:, k * 128:(k + 1) * 128], mh[:, k, :], start=(k == 0), stop=(k == 15)),
                     reads=[wk_b[dc], mh], writes=[ps] if k == 0 else (), pw=() if k == 0 else [ps])
            P.op("dve", lambda e, dc=dc, ps=ps: e.tensor_copy(kraw[:, dc, :], ps[:, 0:256]), reads=[ps], writes=[kraw] if dc == 0 else (), pw=() if dc == 0 else [kraw])
            sq = sqrot.get()
            P.op("act", lambda e, sq=sq, dc=dc: e.activation(sq[:, 0:256], kraw[:, dc, :], AF.Square), reads=[kraw], writes=[sq])
            P.op("pe", lambda e, sq=sq, dc=dc: e.matmul(ss2[:, 0:256], ones_bf, sq[:, 0:256], start=(dc == 0), stop=(dc == 1)),
                 reads=[cbf, sq], writes=[ss2] if dc == 0 else (), pw=() if dc == 0 else [ss2])
        rstd_from_ss(ss2, 256, 1.0 / 256, rs_t)
        for dc in range(2):
            stt2(kTm, kTm[:, dc, :], kraw, kraw[:, dc, :], PR_MKN + dc, rs_t, rs_t[:, 0:256], 256, dc == 0)
        for kc in range(2):
            ps = PS[4 + kc]
            for k in range(16):
                P.op("pe", lambda e, k=k, kc=kc, ps=ps: e.matmul(ps[:, 0:256], mh[:, k, kc * 128:(kc + 1) * 128], wv_b[:, k * 256:(k + 1) * 256], start=(k == 0), stop=(k == 15)),
                     reads=[wv_b, mh], writes=[ps] if k == 0 else (), pw=() if k == 0 else [ps])
            P.op("act", lambda e, kc=kc, ps=ps: e.copy(vm[:, kc, :], ps[:, 0:256]), reads=[ps], writes=[vm] if kc == 0 else (), pw=() if kc == 0 else [vm])

    for _ in phase(1):
        w1b = [P.sbuf([128, 2048], BF16, "w1b") for _ in range(14)]
        w1sb = P.sbuf([128, 16 * 262], BF16, "w1sb")
        with P.scope():
            stg = Rot(P, 2, [128, 2048], F32, "stg")
            for cc in range(14):
                wload(w1b[cc], w1[cc], 2048, stg)
            wload(w1sb, w1s, 16 * 262, stg)
        xrot = Rot(P, 2, [128, 16, 512], F32, "xt")
        hT = P.sbuf([128, 16, 512], BF16, "hT")
        sqrot = Rot(P, 3, [128, 512], BF16, "sq")
        rs_t = P.sbuf([128, 512], F32, "rs")
        rs2 = P.sbuf([128, 512], F32, "rs2")
        rawrot = Rot(P, 3, [128, 512], F32, "raw")
        mqraw = P.sbuf([128, 2, 512], F32, "mqraw")
        mqn = P.sbuf([128, 2, 512], BF16, "mqn")
        obrot = Rot(P, 3, [128, 512], BF16, "ob")
        pTm = [P.sbuf([128, 512], BF16, "pTm") for _ in range(2)]
        vst = Rot(P, 2, [128, 256], BF16, "vst")
        sst = Rot(P, 2, [128, 6], F32, "sst")
        rden = P.sbuf([128, 512], F32, "rden")
        for t in range(NTILE):
            t0 = t * 512
            xt = xrot.get()
            P.dma("sp", xt[:, :, :], xT[:, t0:t0 + 512].rearrange("(c p) n -> p c n", p=128), writes=[xt])
            rmsnorm_tile(xt, 512, PR_GMIX, hT, sqrot, rs_t)
            for cc in range(14):
                ps = PS[1 + (cc % 2)]
                for k in range(16):
                    P.op("pe", lambda e, k=k, cc=cc, ps=ps: e.matmul(ps[:, :], w1b[cc][:, k * 128:(k + 1) * 128], hT[:, k, :], start=(k == 0), stop=(k == 15)),
                         reads=[w1b[cc], hT], writes=[ps] if k == 0 else (), pw=() if k == 0 else [ps])
                if cc < 8:
                    raw = rawrot.get()
                    P.op("act", lambda e, raw=raw, ps=ps: e.copy(raw[:, :], ps[:, :]), reads=[ps], writes=[raw])
                    P.dma("sp", graw[cc][:, t0:t0 + 512], raw[:, :], reads=[raw], pw=[graw[cc]])
                elif cc < 12:
                    raw = rawrot.get()
                    sq = sqrot.get()
                    P.op("dve", lambda e, raw=raw, ps=ps: e.tensor_copy(raw[:, :], ps[:, :]), reads=[ps], writes=[raw])
                    P.op("act", lambda e, sq=sq, ps=ps: e.activation(sq[:, :], ps[:, :], AF.Square), reads=[ps], writes=[sq])
                    P.op("pe", lambda e, sq=sq: e.matmul(PS[3][:, :], ones_bf, sq[:, :], start=True, stop=True), reads=[cbf, sq], writes=[PS[3]])
                    rstd_from_ss(PS[3], 512, 1.0 / 128, rs2)
                    ob = obrot.get()
                    gc = PR_FQN if cc < 10 else PR_FKN
                    stt2(ob, ob[:, :], raw, raw[:, :], gc, rs2, rs2[:, :], 512, True)
                    dst = (foxq if cc < 10 else foxk)[cc % 2]
                    P.dma("sp", dst[:, t0:t0 + 512], ob[:, :], reads=[ob], pw=[dst])
                else:
                    dc = cc - 12
                    sq = sqrot.get()
                    P.op("dve", lambda e, dc=dc, ps=ps: e.tensor_copy(mqraw[:, dc, :], ps[:, :]), reads=[ps], writes=[mqraw] if dc == 0 else (), pw=() if dc == 0 else [mqraw])
                    P.op("act", lambda e, sq=sq, ps=ps: e.activation(sq[:, :], ps[:, :], AF.Square), reads=[ps], writes=[sq])
                    P.op("pe", lambda e, sq=sq, dc=dc: e.matmul(PS[3][:, :], ones_bf, sq[:, :], start=(dc == 0), stop=(dc == 1)),
                         reads=[cbf, sq], writes=[PS[3]] if dc == 0 else (), pw=() if dc == 0 else [PS[3]])
            rstd_from_ss(PS[3], 512, 1.0 / 256, rs2)
            for dc in range(2):
                stt2(mqn, mqn[:, dc, :], mqraw, mqraw[:, dc, :], PR_MQN + dc, rs2, rs2[:, :], 512, dc == 0)
            for kc in range(2):
                ps = PS[1 + kc]
                for dc in range(2):
                    P.op("pe", lambda e, kc=kc, dc=dc, ps=ps: e.matmul(ps[:, :], kTm[:, dc, kc * 128:(kc + 1) * 128], mqn[:, dc, :], start=(dc == 0), stop=(dc == 1)),
                         reads=[kTm, mqn], writes=[ps] if dc == 0 else (), pw=() if dc == 0 else [ps])
                P.op("act", lambda e, kc=kc, ps=ps: e.activation(pTm[kc][:, :], ps[:, :], AF.Exp, scale=1.0 / 16.0), reads=[ps], writes=[pTm[kc]])
            for kc in range(2):
                P.op("pe", lambda e, kc=kc: e.matmul(PS[3][:, :], ones_bf, pTm[kc][:, :], start=(kc == 0), stop=(kc == 1)),
                     reads=[cbf, pTm[kc]], writes=[PS[3]] if kc == 0 else (), pw=() if kc == 0 else [PS[3]])
            P.op("dve", lambda e: e.reciprocal(rden[:, :], PS[3][:, :]), reads=[PS[3]], writes=[rden])
            for dvc in range(2):
                ps = PS[4 + dvc]
                for kc in range(2):
                    P.op("pe", lambda e, kc=kc, dvc=dvc, ps=ps: e.matmul(ps[:, :], vm[:, kc, dvc * 128:(dvc + 1) * 128], pTm[kc][:, :], start=(kc == 0), stop=(kc == 1)),
                         reads=[vm, pTm[kc]], writes=[ps] if kc == 0 else (), pw=() if kc == 0 else [ps])
                ob = obrot.get()
                P.op("dve", lambda e, ob=ob, ps=ps: e.tensor_tensor(ob[:, :], ps[:, :], rden[:, :], ALU.mult), reads=[ps, rden], writes=[ob])
                omine_write(512 + dvc * 128, t0, ob)
            for blk in range(4):
                ps = PS[6]
                for k in range(16):
                    P.op("pe", lambda e, k=k, blk=blk: e.matmul(ps[:, 0:262], hT[:, k, blk * 128:(blk + 1) * 128], w1sb[:, k * 262:(k + 1) * 262], start=(k == 0), stop=(k == 15)),
                         reads=[w1sb, hT], writes=[ps] if k == 0 else (), pw=() if k == 0 else [ps])
                v_s = vst.get(); s_s = sst.get()
                P.op("act", lambda e, v_s=v_s: e.copy(v_s[:, :], ps[:, 0:256]), reads=[ps], writes=[v_s])
                P.op("dve", lambda e, s_s=s_s: e.tensor_copy(s_s[:, :], ps[:, 256:262]), reads=[ps], writes=[s_s])
                r0 = t0 + blk * 128
                P.dma("sp", foxv[r0:r0 + 128, :], v_s[:, :], reads=[v_s], pw=[foxv])
                P.dma("sp", smallT[:, r0:r0 + 128].rearrange("c p -> p c"), s_s[:, :], reads=[s_s], pw=[smallT], allow_slow_non_contiguous=True)

    for _ in phase(2):
        kT = P.sbuf([128, S], BF16, "kT")
        V = P.sbuf([128, J, 128], BF16, "V")
        lf = P.sbuf([128, J], F32, "lf")
        lfJ = P.sbuf([J, 128], F32, "lfJ")
        totc = P.sbuf([J, 1], F32, "totc")
        totB = P.sbuf([J, 128], F32, "totB")
        cT = P.sbuf([128, J], F32, "cT")
        negc = P.sbuf([128, J], F32, "negc")
        cmidB = P.sbuf([128, J], F32, "cmidB")
        nbf = P.sbuf([128, 1], F32, "nbf")
        brot = Rot(P, 8, [128, J], F32, "bias")
        qrot = Rot(P, 2, [128, 512], BF16, "qg")
        prot = Rot(P, 3, [128, 512], BF16, "pT")
        rden = P.sbuf([128, 512], F32, "rden")
        obrot = Rot(P, 2, [128, 512], BF16, "ob")
        strot = [PS[2], PS[3], PS[4]]
        sti = 0
        for h in range(2):
            P.dma("sp", kT[:, :], foxk[h][:, :], reads=[foxk[h]], writes=[kT])
            P.dma("sp", V[:, :, :], foxv[:, h * 128:(h + 1) * 128].rearrange("(j p) c -> p j c", p=128), reads=[foxv], writes=[V])
            P.dma("sp", lf[:, :], smallT[h, :].rearrange("(j p) -> p j", p=128), reads=[smallT], writes=[lf], allow_slow_non_contiguous=True)
            P.dma("sp", lfJ[:, :], smallT[h, :].rearrange("(j p) -> j p", p=128), reads=[smallT], writes=[lfJ])
            P.op("dve", lambda e, h=h: e.tensor_scalar(nbf[:, :], prm[:, PR_BF + h:PR_BF + h + 1], -1.0, None, ALU.mult), reads=[prm], writes=[nbf])
            for tl, npart in ((lf, 128), (lfJ, J)):
                P.op("act", lambda e, tl=tl, npart=npart: e.activation(tl[:, :], tl[:, :], AF.Exp, bias=nbf[0:npart, 0:1], scale=-1.0), reads=[tl, nbf], writes=[tl])
                P.op("act", lambda e, tl=tl, npart=npart: e.activation(tl[:, :], tl[:, :], AF.Ln, bias=epsc[0:npart, 1:2], scale=1.0), reads=[tl, epsc], writes=[tl])
            P.op("dve", lambda e: e.reduce_sum(totc[:, :], lfJ[:, :], mybir.AxisListType.X), reads=[lfJ], writes=[totc])
            P.op("dve", lambda e: e.tensor_scalar(totB[:, :], cst[0:J, C_ONES:C_ONES + 128], totc[:, 0:1], None, ALU.mult), reads=[cst, totc], writes=[totB])
            ps = PS[5]
            P.op("pe", lambda e: e.matmul(ps[:, 0:J], cst[:, C_TRI:C_TRI + 128], lf[:, :], start=True, stop=False), reads=[cst, lf], writes=[ps])
            P.op("pe", lambda e: e.matmul(ps[:, 0:J], totB[:, :], cst[0:J, C_TRIS:C_TRIS + J], start=False, stop=True), reads=[cst, totB], pw=[ps])
            P.op("dve", lambda e: e.tensor_copy(negc[:, :], ps[:, 0:J]), reads=[ps], writes=[negc])
            P.op("dve", lambda e: e.tensor_scalar(cT[:, :], ps[:, 0:J], -1.0, None, ALU.mult), reads=[ps], writes=[cT])
            P.op("pe", lambda e: e.matmul(ps[:, 0:J], cst[:, C_SEL:C_SEL + 128], cT[:, :], start=True, stop=True), reads=[cst, cT], writes=[ps])
            P.op("dve", lambda e: e.tensor_copy(cmidB[:, :], ps[:, 0:J]), reads=[ps], writes=[cmidB])
            for G in range(NTILE):
                qg = qrot.get()
                P.dma("sp", qg[:, :], foxq[h][:, G * 512:(G + 1) * 512], reads=[foxq[h]], writes=[qg])
                biases = []
                for ib in range(4):
                    i = 4 * G + ib
                    bt = brot.get()
                    P.op("dve", lambda e, bt=bt, i=i: e.tensor_scalar(bt[:, 0:i + 1], negc[:, 0:i + 1], cmidB[:, i:i + 1], None, ALU.add), reads=[negc, cmidB], writes=[bt])
                    biases.append(bt)
                o_ps = PS[0]; d_ps = PS[1]
                order = list(range(4 * G, 4 * G + 4)) + list(range(0, 4 * G))
                for n, j in enumerate(order):
                    qlo = max(0, j - 4 * G)
                    c0 = qlo * 128
                    st = strot[sti]; sti = (sti + 1) % 3
                    P.op("pe", lambda e, j=j, c0=c0, st=st, qg=qg: e.matmul(st[:, c0:512], kT[:, j * 128:(j + 1) * 128], qg[:, c0:512], start=True, stop=True),
                         reads=[kT, qg], writes=[st])
                    pT = prot.get()
                    for ib in range(qlo, 4):
                        bt = biases[ib]
                        P.op("act", lambda e, ib=ib, bt=bt, pT=pT, st=st, j=j: e.activation(pT[:, ib * 128:(ib + 1) * 128], st[:, ib * 128:(ib + 1) * 128], AF.Exp,
                                                                                             bias=bt[:, j:j + 1], scale=128.0 ** -0.5),
                             reads=[st, bt], writes=[pT] if ib == qlo else (), pw=() if ib == qlo else [pT])
                    if j >= 4 * G:
                        P.op("pool", lambda e, pT=pT, c0=c0: e.tensor_tensor(pT[:, c0:c0 + 128], pT[:, c0:c0 + 128], tri_bf, ALU.mult), reads=[pT, cbf], writes=[pT])
                    first = (n == 0); last = (n == len(order) - 1)
                    P.op("pe", lambda e, j=j, c0=c0, pT=pT, first=first, last=last: e.matmul(o_ps[:, c0:512], V[:, j, :], pT[:, c0:512], start=first, stop=last, skip_group_check=True),
                         reads=[V, pT], writes=[o_ps] if first else (), pw=() if first else [o_ps])
                    P.op("pe", lambda e, c0=c0, pT=pT, first=first, last=last: e.matmul(d_ps[:, c0:512], ones_bf, pT[:, c0:512], start=first, stop=last, skip_group_check=True),
                         reads=[cbf, pT], writes=[d_ps] if first else (), pw=() if first else [d_ps])
                P.op("dve", lambda e: e.reciprocal(rden[:, :], d_ps[:, :]), reads=[d_ps], writes=[rden])
                ob = obrot.get()
                P.op("dve", lambda e, ob=ob: e.tensor_tensor(ob[:, :], o_ps[:, :], rden[:, :], ALU.mult), reads=[o_ps, rden], writes=[ob])
                omine_write(256 + h * 128, G * 512, ob)

    if 3 in PH:
      gdn_phase(P, nc, S, NTILE, NCH, PS, PSB, cst, cbf, prm, epsc, graw, smallT, omine_write)

    if 4 in PH:
        for ct in range(NCT):
            P.custom("pool", lambda e, sem, ct=ct: e.collective_compute("AllGather", ALU.bypass, replica_groups=[[0, 1, 2, 3], [4, 5, 6, 7]],
                                                                       ins=[omine.t[ct].opt()], outs=[gath.t[ct].opt()]).then_inc(sem, 1),
                     reads=[omine], writes=[gath] if ct == 0 else (), pw=() if ct == 0 else [gath])

    pid = nc.partition_id()
    qid = pid % 4
    for _ in phase(5):
        xt = P.sbuf([128, 16, NT4], F32, "xt4")
        sqrot = Rot(P, 3, [128, 512], BF16, "sq")
        rs_t = P.sbuf([128, 512], F32, "rs")
        wrot = Rot(P, 3, [128, 2048], BF16, "w4")
        wurot = Rot(P, 2, [128, 1024], BF16, "wu4")
        w2rot = Rot(P, 2, [128, 4096], BF16, "wf2")
        stg = Rot(P, 3, [128, 2048], F32, "stg")
        sgrot = Rot(P, 2, [128, 512], F32, "sg")
        tmrot = Rot(P, 2, [128, 512], F32, "tm")
        yacc = P.sbuf([128, 512], F32, "yacc")
        orot = Rot(P, 2, [128, 512], F32, "o4")
        for sub in range(NSUB):
            c0 = sub * NT4
            P.dma("sp", xt[:, :, :], xq[:, c0:c0 + NT4].rearrange("(c p) n -> p c n", p=128), writes=[xt])
            with P.scope():
                hT = P.sbuf([128, 16, NT4], BF16, "h4")
                oT = P.sbuf([128, 24, NT4], BF16, "o4T")
                yT = P.sbuf([128, 16, NT4], BF16, "y4")
                rmsnorm_tile(xt, NT4, PR_GMIX, hT, sqrot, rs_t)
                src = gath.t[bass.ds(qid * NSUB + sub, 1), :, :].rearrange("o (r p) n -> p (o r) n", p=128)
                P.dma("sp", oT[:, :, :], src, reads=[gath], writes=[oT])
                for c in range(16):
                    for br in range(3):
                        wgb = wrot.get(); wub = wurot.get()
                        wload(wgb, wg[br * 16 + c], 2048, stg)
                        wload(wub, wup[br * 16 + c], 1024, stg)
                        gps = PS[1 + (br % 2)]; ups = PS[3 + (br % 2)]
                        for k in range(16):
                            P.op("pe", lambda e, k=k, wgb=wgb, gps=gps: e.matmul(gps[:, 0:NT4], wgb[:, k * 128:(k + 1) * 128], hT[:, k, :], start=(k == 0), stop=(k == 15)),
                                 reads=[wgb, hT], writes=[gps] if k == 0 else (), pw=() if k == 0 else [gps])
                        for k in range(8):
                            rk = (k // 2) * 6 + br * 2 + (k % 2)
                            P.op("pe", lambda e, k=k, rk=rk, wub=wub, ups=ups: e.matmul(ups[:, 0:NT4], wub[:, k * 128:(k + 1) * 128], oT[:, rk, :], start=(k == 0), stop=(k == 7)),
                                 reads=[wub, oT], writes=[ups] if k == 0 else (), pw=() if k == 0 else [ups])
                        sg = sgrot.get()
                        P.op("act", lambda e, sg=sg, gps=gps: e.activation(sg[:, 0:NT4], gps[:, 0:NT4], AF.Sigmoid), reads=[gps], writes=[sg])
                        if br == 0:
                            P.op("dve", lambda e, sg=sg, ups=ups: e.tensor_tensor(yacc[:, 0:NT4], sg[:, 0:NT4], ups[:, 0:NT4], ALU.mult), reads=[sg, ups], writes=[yacc])
                        else:
                            tm = tmrot.get()
                            P.op("dve", lambda e, sg=sg, ups=ups, tm=tm: e.tensor_tensor(tm[:, 0:NT4], sg[:, 0:NT4], ups[:, 0:NT4], ALU.mult), reads=[sg, ups], writes=[tm])
                            if br == 1:
                                P.op("pool", lambda e, tm=tm: e.tensor_tensor(yacc[:, 0:NT4], yacc[:, 0:NT4], tm[:, 0:NT4], ALU.add), reads=[tm, yacc], writes=[yacc])
                            else:
                                P.op("pool", lambda e, tm=tm, c=c: e.tensor_tensor(yT[:, c, :], yacc[:, 0:NT4], tm[:, 0:NT4], ALU.add), reads=[tm, yacc],
                                     writes=[yT] if c == 0 else (), pw=() if c == 0 else [yT])
                for c in range(16):
                    wb = wrot.get()
                    wload(wb, wout[c], 2048, stg)
                    ps = PS[1 + (c % 2)]
                    for k in range(16):
                        P.op("pe", lambda e, k=k, wb=wb, ps=ps: e.matmul(ps[:, 0:NT4], wb[:, k * 128:(k + 1) * 128], yT[:, k, :], start=(k == 0), stop=(k == 15)),
                             reads=[wb, yT], writes=[ps] if k == 0 else (), pw=() if k == 0 else [ps])
                    P.op("dve", lambda e, c=c, ps=ps: e.tensor_tensor(xt[:, c, :], xt[:, c, :], ps[:, 0:NT4], ALU.add), reads=[ps, xt], pw=[xt])
            with P.scope():
                h2 = P.sbuf([128, 16, NT4], BF16, "h24")
                uT = P.sbuf([128, 32, NT4], BF16, "u4")
                rrot = Rot(P, 2, [128, 512], F32, "r4")
                rmsnorm_tile(xt, NT4, PR_GMLP, h2, sqrot, rs_t)
                for half in range(2):
                    for fi in range(32):
                        f = half * 32 + fi
                        wb = wrot.get()
                        wload(wb, wff1[f], 2048, stg)
                        ps = PS[1 + (f % 2)]
                        for k in range(16):
                            P.op("pe", lambda e, k=k, wb=wb, ps=ps: e.matmul(ps[:, 0:NT4], wb[:, k * 128:(k + 1) * 128], h2[:, k, :], start=(k == 0), stop=(k == 15)),
                                 reads=[wb, h2], writes=[ps] if k == 0 else (), pw=() if k == 0 else [ps])
                        r = rrot.get()
                        P.op("act", lambda e, r=r, ps=ps: e.activation(r[:, 0:NT4], ps[:, 0:NT4], AF.Relu), reads=[ps], writes=[r])
                        P.op("pool", lambda e, r=r, fi=fi: e.tensor_tensor(uT[:, fi, :], r[:, 0:NT4], r[:, 0:NT4], ALU.mult), reads=[r],
                             writes=[uT] if fi == 0 else (), pw=() if fi == 0 else [uT])
                    for c in range(16):
                        wb = w2rot.get()
                        wload(wb, wff2[c][:, half * 4096:(half + 1) * 4096], 4096, stg)
                        ps = PS[3 + (c % 2)]
                        for fi in range(32):
                            P.op("pe", lambda e, fi=fi, wb=wb, ps=ps: e.matmul(ps[:, 0:NT4], wb[:, fi * 128:(fi + 1) * 128], uT[:, fi, :], start=(fi == 0), stop=(fi == 31)),
                                 reads=[wb, uT], writes=[ps] if fi == 0 else (), pw=() if fi == 0 else [ps])
                        if half == 0:
                            P.op("dve", lambda e, c=c, ps=ps: e.tensor_tensor(xt[:, c, :], xt[:, c, :], ps[:, 0:NT4], ALU.add), reads=[ps, xt], pw=[xt])
                        else:
                            o = orot.get()
                            P.op("dve", lambda e, c=c, ps=ps, o=o: e.tensor_tensor(o[:, 0:NT4], xt[:, c, :], ps[:, 0:NT4], ALU.add), reads=[ps, xt], writes=[o])
                            P.dma("sp", outT[c * 128:(c + 1) * 128, c0:c0 + NT4], o[:, 0:NT4], reads=[o], pw=[outT])
    fin = [outT]
    if dbg:
        fin += [foxv, smallT] + graw + foxq + foxk
    P.wait_all("sp", fin)
    P.emit()
    return nc


def gdn_phase(P, nc, S, NTILE, NCH, PS, PSB, cst, cbf, prm, epsc, graw, smallT, omine_write):
    ident_bf = cbf[:, 0:128]; ones_bf = cbf[:, 384:512]
    I64 = cbf[0:64, 0:64]
    UT64 = cst[0:64, C_TRI:C_TRI + 64]
    ID64 = cst[0:64, C_ID:C_ID + 64]
    with P.scope():
        gall = P.sbuf([64, NCH], F32, "gall"); ball = P.sbuf([64, NCH], F32, "ball")
        nea = P.sbuf([64, 1], F32, "nea")
        xb = [P.sbuf([128, 515], F32, f"xb{i}") for i in range(4)]
        y = P.sbuf([128, 512], F32, "gy"); tmp = P.sbuf([128, 512], F32, "gtmp"); sg = P.sbuf([128, 512], F32, "gsg")
        yc = P.sbuf([128, 512], F32, "gyc")
        sq = P.sbuf([128, 512], BF16, "gsq"); rs = P.sbuf([128, 512], F32, "grs")
        qh = P.sbuf([128, 512], BF16, "qh"); kh = P.sbuf([128, 512], BF16, "kh"); vb = P.sbuf([128, 512], BF16, "vb")
        zs = P.sbuf([128, 512], F32, "zs"); oraw = P.sbuf([128, 512], F32, "oraw"); on = P.sbuf([128, 512], F32, "on")
        ob = P.sbuf([128, 512], BF16, "gob")
        gB = P.sbuf([64, 128], F32, "gB"); bB = P.sbuf([64, 128], F32, "bB")
        gamRow = P.sbuf([128, 64], F32, "gamRow"); betaRow = P.sbuf([128, 64], F32, "betaRow"); aRow = P.sbuf([128, 64], F32, "aRow")
        gamCol = P.sbuf([64, 1], F32, "gamCol"); acol = P.sbuf([64, 1], F32, "acol"); rcol = P.sbuf([64, 1], F32, "rcol"); bacol = P.sbuf([64, 1], F32, "bacol")
        t0d = P.sbuf([64, 64], F32, "t0d"); d2 = P.sbuf([64, 128], F32, "d2")
        kbq = P.sbuf([128, 128], BF16, "kbq"); NQ = P.sbuf([64, 128], BF16, "NQ")
        PP = [P.sbuf([64, 128], BF16, f"PP{i}") for i in range(2)]
        Y = [P.sbuf([64, 64], BF16, f"Y{i}") for i in range(2)]
        RHSw = P.sbuf([64, 128], BF16, "RHSw"); kdec = P.sbuf([64, 128], BF16, "kdec"); RHSv = P.sbuf([64, 128], BF16, "RHSv")
        nwT = P.sbuf([128, 64], BF16, "nwT"); uc = P.sbuf([64, 128], F32, "uc"); qdT = P.sbuf([128, 64], BF16, "qdT")
        u = P.sbuf([64, 128], BF16, "u"); S32 = P.sbuf([128, 128], F32, "S32"); Sbf = P.sbuf([128, 128], BF16, "Sbf"); St = P.sbuf([128, 128], F32, "St")
        for hi in range(2):
            P.dma("sp", gall[:, :], smallT[4 + hi, :].rearrange("(n p) -> p n", p=64), reads=[smallT], writes=[gall], allow_slow_non_contiguous=True)
            P.dma("sp", ball[:, :], smallT[2 + hi, :].rearrange("(n p) -> p n", p=64), reads=[smallT], writes=[ball], allow_slow_non_contiguous=True)
            P.op("act", lambda e, hi=hi: e.activation(nea[:, :], prm[0:64, PR_ALOG + hi:PR_ALOG + hi + 1], AF.Exp), reads=[prm], writes=[nea])
            P.op("dve", lambda e: e.tensor_scalar(nea[:, :], nea[:, :], -1.0, None, ALU.mult), reads=[nea], writes=[nea])
            P.op("act", lambda e, hi=hi: e.activation(gall[:, :], gall[:, :], AF.Exp, bias=prm[0:64, PR_DTB + hi:PR_DTB + hi + 1], scale=1.0), reads=[gall, prm], writes=[gall])
            P.op("act", lambda e: e.activation(gall[:, :], gall[:, :], AF.Ln, bias=epsc[0:64, 1:2], scale=1.0), reads=[gall, epsc], writes=[gall])
            P.op("dve", lambda e: e.tensor_scalar(gall[:, :], gall[:, :], nea[:, 0:1], None, ALU.mult), reads=[gall, nea], writes=[gall])
            P.op("act", lambda e: e.activation(ball[:, :], ball[:, :], AF.Sigmoid), reads=[ball], writes=[ball])
            P.op("pool", lambda e: e.memset(S32[:, :], 0.0), writes=[S32])
            P.op("pool", lambda e: e.memset(Sbf[:, :], 0.0), writes=[Sbf])
            for t in range(NTILE):
                c0t = t * 512
                for gi in range(4):
                    src = graw[gi * 2 + hi]
                    if t == 0:
                        P.op("pool", lambda e, gi=gi: e.memset(xb[gi][:, 0:3], 0.0), writes=[xb[gi]])
                        P.dma("sp", xb[gi][:, 3:515], src[:, 0:512], reads=[src], pw=[xb[gi]])
                    else:
                        P.dma("sp", xb[gi][:, 0:515], src[:, c0t - 3:c0t + 512], reads=[src], writes=[xb[gi]])
                for gi in range(3):
                    wc = PR_CONV + (gi * 2 + hi) * 4
                    P.op("dve", lambda e, gi=gi, wc=wc: e.tensor_scalar(y[:, :], xb[gi][:, 0:512], prm[:, wc:wc + 1], None, ALU.mult), reads=[xb[gi], prm], writes=[y])
                    for i in range(1, 4):
                        P.op("pool", lambda e, gi=gi, wc=wc, i=i: e.tensor_scalar(tmp[:, :], xb[gi][:, i:i + 512], prm[:, wc + i:wc + i + 1], None, ALU.mult), reads=[xb[gi], prm], writes=[tmp])
                        P.op("dve", lambda e: e.tensor_tensor(y[:, :], y[:, :], tmp[:, :], ALU.add), reads=[y, tmp], writes=[y])
                    P.op("act", lambda e: e.activation(sg[:, :], y[:, :], AF.Sigmoid), reads=[y], writes=[sg])
                    if gi == 2:
                        P.op("pool", lambda e: e.tensor_tensor(vb[:, :], y[:, :], sg[:, :], ALU.mult), reads=[y, sg], writes=[vb])
                    else:
                        P.op("pool", lambda e: e.tensor_tensor(yc[:, :], y[:, :], sg[:, :], ALU.mult), reads=[y, sg], writes=[yc])
                        P.op("act", lambda e: e.activation(sq[:, :], yc[:, :], AF.Square), reads=[yc], writes=[sq])
                        P.op("pe", lambda e: e.matmul(PS[0][:, :], ones_bf, sq[:, :], start=True, stop=True), reads=[cbf, sq], writes=[PS[0]])
                        P.op("act", lambda e: e.activation(rs[:, :], PS[0][:, :], AF.Sqrt, bias=epsc[:, 0:1], scale=1.0), reads=[PS[0], epsc], writes=[rs])
                        P.op("dve", lambda e: e.reciprocal(rs[:, :], rs[:, :]), reads=[rs], writes=[rs])
                        if gi == 0:
                            P.op("dve", lambda e: e.tensor_scalar(rs[:, :], rs[:, :], 128.0 ** -0.5, None, ALU.mult), reads=[rs], writes=[rs])
                        dst = qh if gi == 0 else kh
                        P.op("dve", lambda e, dst=dst: e.tensor_tensor(dst[:, :], yc[:, :], rs[:, :], ALU.mult), reads=[yc, rs], writes=[dst])
                P.op("act", lambda e: e.activation(sg[:, :], xb[3][:, 3:515], AF.Sigmoid), reads=[xb[3]], writes=[sg])
                P.op("pool", lambda e: e.tensor_tensor(zs[:, :], xb[3][:, 3:515], sg[:, :], ALU.mult), reads=[xb[3], sg], writes=[zs])
                for ci in range(8):
                    n = t * 8 + ci
                    c0 = ci * 64
                    A = PS[1]
                    P.op("dve", lambda e, n=n: e.tensor_scalar(gB[:, :], cst[0:64, C_ONES:C_ONES + 128], gall[:, n:n + 1], None, ALU.mult), reads=[cst, gall], writes=[gB])
                    P.op("dve", lambda e, n=n: e.tensor_scalar(bB[:, :], cst[0:64, C_ONES:C_ONES + 128], ball[:, n:n + 1], None, ALU.mult), reads=[cst, ball], writes=[bB])
                    P.op("pe", lambda e: e.matmul(A[:, 0:64], gB[:, :], UT64, start=True, stop=True), reads=[gB, cst], writes=[A])
                    P.op("pe", lambda e: e.matmul(A[:, 64:128], bB[:, :], ID64, start=True, stop=True), reads=[bB, cst], pw=[A])
                    P.op("pe", lambda e, n=n: e.matmul(A[0:64, 128:129], UT64, gall[:, n:n + 1], start=True, stop=True), reads=[gall, cst], pw=[A])
                    P.op("dve", lambda e: e.tensor_copy(gamRow[:, :], A[:, 0:64]), reads=[A], writes=[gamRow])
                    P.op("dve", lambda e: e.tensor_copy(betaRow[:, :], A[:, 64:128]), reads=[A], writes=[betaRow])
                    P.op("dve", lambda e: e.tensor_copy(gamCol[:, :], A[0:64, 128:129]), reads=[A], writes=[gamCol])
                    P.op("act", lambda e: e.activation(aRow[:, :], gamRow[:, :], AF.Exp), reads=[gamRow], writes=[aRow])
                    P.op("act", lambda e: e.activation(acol[:, :], gamCol[:, :], AF.Exp), reads=[gamCol], writes=[acol])
                    P.op("act", lambda e: e.activation(rcol[:, :], gamCol[:, :], AF.Exp, bias=gamRow[0:64, 63:64], scale=-1.0), reads=[gamCol, gamRow], writes=[rcol])
                    P.op("dve", lambda e, n=n: e.tensor_tensor(bacol[:, :], acol[:, :], ball[:, n:n + 1], ALU.mult), reads=[acol, ball], writes=[bacol])
                    P.op("dve", lambda e: e.tensor_scalar(t0d[:, :], gamRow[0:64, :], gamCol[:, 0:1], None, ALU.subtract), reads=[gamRow, gamCol], writes=[t0d])
                    P.op("dve", lambda e: e.tensor_tensor(d2[:, 0:64], t0d[:, :], cst[0:64, C_NM:C_NM + 64], ALU.add), reads=[t0d, cst], writes=[d2])
                    P.op("dve", lambda e: e.tensor_tensor(d2[:, 64:128], t0d[:, :], cst[0:64, C_NM + 64:C_NM + 128], ALU.add), reads=[t0d, cst], pw=[d2])
                    P.op("act", lambda e: e.activation(d2[:, :], d2[:, :], AF.Exp), reads=[d2], writes=[d2])
                    P.op("dve", lambda e, c0=c0: e.tensor_tensor(kbq[:, 0:64], kh[:, c0:c0 + 64], betaRow[:, :], ALU.mult), reads=[kh, betaRow], writes=[kbq])
                    P.op("dve", lambda e, c0=c0: e.tensor_copy(kbq[:, 64:128], qh[:, c0:c0 + 64]), reads=[qh], pw=[kbq])
                    B = PS[2]
                    P.op("pe", lambda e, c0=c0: e.matmul(B[0:64, 0:128], kh[:, c0:c0 + 64], kbq[:, :], start=True, stop=True), reads=[kh, kbq], writes=[B])
                    P.op("dve", lambda e: e.tensor_tensor(NQ[:, :], B[0:64, 0:128], d2[:, :], ALU.mult), reads=[B, d2], writes=[NQ])
                    P.op("pe", lambda e: e.transpose(PSB[0:64, 0:64], NQ[:, 0:64], I64), reads=[NQ, cbf], writes=[PSB])
                    P.op("dve", lambda e: e.tensor_copy(PP[0][:, 64:128], PSB[0:64, 0:64]), reads=[PSB], writes=[PP[0]])
                    P.op("dve", lambda e: e.tensor_copy(PP[0][:, 0:64], NQ[:, 0:64]), reads=[NQ], pw=[PP[0]])
                    P.op("dve", lambda e: e.tensor_tensor(Y[0][:, :], I64, NQ[:, 0:64], ALU.subtract), reads=[cbf, NQ], writes=[Y[0]])
                    pk = 0; yk = 0
                    for lvl in range(5):
                        C = PS[3]
                        P.op("pe", lambda e, pk=pk: e.matmul(C[0:64, 0:64], PP[pk][:, 64:128], PP[pk][:, 0:64], start=True, stop=True), reads=[PP[pk]], writes=[C])
                        P.op("pe", lambda e, pk=pk: e.matmul(C[0:64, 64:128], PP[pk][:, 0:64], PP[pk][:, 64:128], start=True, stop=True), reads=[PP[pk]], pw=[C])
                        P.op("dve", lambda e, pk=pk: e.tensor_copy(PP[1 - pk][:, :], C[0:64, 0:128]), reads=[C], writes=[PP[1 - pk]])
                        pk = 1 - pk
                        Dp = PS[4]
                        P.op("pe", lambda e, pk=pk, yk=yk: e.matmul(Dp[0:64, 0:64], PP[pk][:, 64:128], Y[yk][:, :], start=True, stop=True), reads=[PP[pk], Y[yk]], writes=[Dp])
                        P.op("dve", lambda e, yk=yk: e.tensor_tensor(Y[1 - yk][:, :], Dp[0:64, 0:64], Y[yk][:, :], ALU.add), reads=[Dp, Y[yk]], writes=[Y[1 - yk]])
                        yk = 1 - yk
                    Tt = Y[yk]
                    P.op("pe", lambda e, c0=c0: e.transpose(PSB[0:64, 128:256], kh[:, c0:c0 + 64], ident_bf), reads=[kh, cbf], writes=[PSB])
                    P.op("pe", lambda e, c0=c0: e.transpose(PSB[0:64, 256:384], vb[:, c0:c0 + 64], ident_bf), reads=[vb, cbf], pw=[PSB])
                    P.op("dve", lambda e: e.tensor_scalar(RHSw[:, :], PSB[0:64, 128:256], bacol[:, 0:1], None, ALU.mult), reads=[PSB, bacol], writes=[RHSw])
                    P.op("dve", lambda e: e.tensor_scalar(kdec[:, :], PSB[0:64, 128:256], rcol[:, 0:1], None, ALU.mult), reads=[PSB, rcol], writes=[kdec])
                    P.op("dve", lambda e, n=n: e.tensor_scalar(RHSv[:, :], PSB[0:64, 256:384], ball[:, n:n + 1], None, ALU.mult), reads=[PSB, ball], writes=[RHSv])
                    E = PS[5]
                    P.op("pe", lambda e, Tt=Tt: e.matmul(E[:, 0:64], RHSw[:, :], Tt[:, :], start=True, stop=True), reads=[RHSw, Tt], writes=[E])
                    P.op("dve", lambda e: e.tensor_scalar(nwT[:, :], E[:, 0:64], -1.0, None, ALU.mult), reads=[E], writes=[nwT])
                    F = PS[6]
                    P.op("pe", lambda e, Tt=Tt: e.matmul(F[0:64, 0:128], Tt[:, :], RHSv[:, :], start=True, stop=True), reads=[RHSv, Tt], writes=[F])
                    P.op("dve", lambda e: e.tensor_copy(uc[:, :], F[0:64, 0:128]), reads=[F], writes=[uc])
                    P.op("dve", lambda e, c0=c0: e.tensor_tensor(qdT[:, :], qh[:, c0:c0 + 64], aRow[:, :], ALU.mult), reads=[qh, aRow], writes=[qdT])
                    P.op("pe", lambda e: e.matmul(F[0:64, 128:256], nwT[:, :], Sbf[:, :], start=True, stop=True), reads=[nwT, Sbf], writes=[F])
                    P.op("dve", lambda e: e.tensor_tensor(u[:, :], F[0:64, 128:256], uc[:, :], ALU.add), reads=[F, uc], writes=[u])
                    P.op("pe", lambda e: e.matmul(E[:, 64:128], Sbf[:, :], qdT[:, :], start=True, stop=False), reads=[Sbf, qdT], writes=[E])
                    P.op("pe", lambda e: e.matmul(E[:, 64:128], u[:, :], NQ[:, 64:128], start=False, stop=True), reads=[u, NQ], pw=[E])
                    P.op("dve", lambda e, c0=c0: e.tensor_copy(oraw[:, c0:c0 + 64], E[:, 64:128]), reads=[E], writes=[oraw] if ci == 0 else (), pw=() if ci == 0 else [oraw])
                    P.op("pe", lambda e: e.matmul(E[:, 128:256], kdec[:, :], u[:, :], start=True, stop=True), reads=[kdec, u], writes=[E])
                    P.op("dve", lambda e: e.tensor_scalar(St[:, :], S32[:, :], aRow[:, 63:64], None, ALU.mult), reads=[S32, aRow], writes=[St])
                    P.op("dve", lambda e: e.tensor_tensor(S32[:, :], St[:, :], E[:, 128:256], ALU.add), reads=[St, E], writes=[S32])
                    P.op("dve", lambda e: e.tensor_copy(Sbf[:, :], S32[:, :]), reads=[S32], writes=[Sbf])
                P.op("act", lambda e: e.activation(sq[:, :], oraw[:, :], AF.Square), reads=[oraw], writes=[sq])
                P.op("pe", lambda e: e.matmul(PS[0][:, :], ones_bf, sq[:, :], start=True, stop=True), reads=[cbf, sq], writes=[PS[0]])
                P.op("act", lambda e: e.activation(rs[:, :], PS[0][:, :], AF.Sqrt, bias=epsc[:, 0:1], scale=1.0 / 128), reads=[PS[0], epsc], writes=[rs])
                P.op("dve", lambda e: e.reciprocal(rs[:, :], rs[:, :]), reads=[rs], writes=[rs])
                P.op("pool", lambda e: e.tensor_tensor(on[:, :], oraw[:, :], rs[:, :], ALU.mult), reads=[oraw, rs], writes=[on])
                P.op("dve", lambda e: e.tensor_scalar(on[:, :], on[:, :], prm[:, PR_GDNG:PR_GDNG + 1], None, ALU.mult), reads=[on, prm], writes=[on])
                P.op("pool", lambda e: e.tensor_tensor(ob[:, :], on[:, :], zs[:, :], ALU.mult), reads=[on, zs], writes=[ob])
                omine_write(hi * 128, c0t, ob)


_CACHE = {}


def prep_inputs(I, S, PH=(0, 1, 2, 3, 4, 5)):
    w_in = I["w_in"][0]
    TQ = S // 4
    cst = make_consts()
    gates0 = 8216
    big = {}
    if 5 in PH:
        big["wg"] = np.concatenate([relay(w_in[:, gates0 + br * D:gates0 + (br + 1) * D]) for br in range(3)], axis=0)
        big["wup"] = np.concatenate([relay(I[n][0]) for n in ("w_up_gdn", "w_up_fox", "w_up_mem")], axis=0)
        big["wout"] = relay(I["w_out"][0]); big["wff1"] = relay(I["w_ff1"][0]); big["wff2"] = relay(I["w_ff2"][0])
    maps = []
    for core in range(8):
        b, g = core // 4, core % 4
        hs = [2 * g, 2 * g + 1]
        xTb = np.ascontiguousarray(I["x"][b, :S].T)
        cols = []
        for base in (0, 1024, 2048, 3072, 4112, 5136):
            for h in hs:
                cols.append(np.arange(base + h * 128, base + (h + 1) * 128))
        cols.append(np.arange(7192 + g * 256, 7192 + (g + 1) * 256))
        w1 = relay(w_in[:, np.concatenate(cols)])
        scol = np.concatenate([np.arange(6160 + h * 128, 6160 + (h + 1) * 128) for h in hs] + [np.array([7184 + h for h in hs]),
                              np.array([4096 + h for h in hs]), np.array([4104 + h for h in hs])])
        w1s = relay_tok(w_in[:, scol])
        wmkv = I["w_mem_kv"][0]
        wmk = relay(wmkv[:, g * 256:(g + 1) * 256])
        wmv = relay_tok(wmkv[:, 1024 + g * 256:1024 + (g + 1) * 256])
        prm = np.zeros((128, NPRM), np.float32)
        prm[:, PR_GMIX:PR_GMIX + 16] = I["g_mix"][0].reshape(16, 128).T
        prm[:, PR_GMLP:PR_GMLP + 16] = I["g_mlp"][0].reshape(16, 128).T
        prm[:, PR_GMEM:PR_GMEM + 16] = I["g_mem"][0].reshape(16, 128).T
        cw = I["conv_w"][0]
        for gi, base in enumerate((0, 1024, 2048)):
            for hi, h in enumerate(hs):
                prm[:, PR_CONV + (gi * 2 + hi) * 4:PR_CONV + (gi * 2 + hi) * 4 + 4] = cw[:, base + h * 128:base + (h + 1) * 128].T
        prm[:, PR_GDNG] = I["gdn_norm_g"][0]
        prm[:, PR_FQN] = I["fox_q_norm"][0]; prm[:, PR_FKN] = I["fox_k_norm"][0]
        prm[:, PR_MQN:PR_MQN + 2] = I["mem_q_norm"][0].reshape(2, 128).T
        prm[:, PR_MKN:PR_MKN + 2] = I["mem_k_norm"][0].reshape(2, 128).T
        for hi, h in enumerate(hs):
            prm[:, PR_ALOG + hi] = I["a_log"][0, h]; prm[:, PR_DTB + hi] = I["dt_bias"][0, h]; prm[:, PR_BF + hi] = I["fox_b_f"][0, h]
        maps.append({
            "xT": xTb, "xq": np.ascontiguousarray(xTb[:, g * TQ:(g + 1) * TQ]), "memT": np.ascontiguousarray(I["mem"][b].T),
            "prm": prm, "cst": cst, "w1": w1, "w1s": w1s, "wmk": wmk, "wmv": wmv, **big,
        })
    return maps


def run(I, S, dbg=False, PH=(0, 1, 2, 3, 4, 5)):
    key = (S, dbg, PH)
    if key not in _CACHE:
        _CACHE[key] = build(S, dbg, PH)
    nc = _CACHE[key]
    maps = prep_inputs(I, S, PH)
    res = run_bass_kernel_spmd(nc, maps, core_ids=list(range(8)))
    return res


def kernel(**inputs):
    I = {k: np.asarray(v) for k, v in inputs.items()}
    S = I["x"].shape[1]
    res = run(I, S)
    TQ = S // 4
    out = np.empty((2, S, D), np.float32)
    for core in range(8):
        b, g = core // 4, core % 4
        out[b, g * TQ:(g + 1) * TQ, :] = res.results[core]["outT"].T
    return out
```

```python
from contextlib import ExitStack, contextmanager
import types
import os
import numpy as np
import concourse.bass as bass
import concourse.mybir as mybir
from concourse.bass_utils import run_bass_kernel_spmd

F32 = mybir.dt.float32
BF16 = mybir.dt.bfloat16
AF = mybir.ActivationFunctionType
ALU = mybir.AluOpType

ENGS = ("pe", "act", "dve", "pool", "sp")
SAME_ENG_SYNC = bool(int(os.environ.get("KSES", "1")))
NDMA_SEM = 10
EPS = 1e-6
D = 2048
DFF = 8192
NEG = -30000.0
ARENA_F32 = 47 * 1024


class T:
    __slots__ = ("t", "name", "lws", "rd", "excl")

    def __init__(self, t, name, fence=None):
        self.t = t
        self.name = name
        self.excl = False
        self.lws = {}
        self.rd = dict(fence) if fence else {}

    def __getitem__(self, idx):
        return self.t[idx]


class Ev:
    __slots__ = ("kind", "key", "idx", "op")

    def __init__(self, kind, key, idx, op=None):
        self.kind = kind; self.key = key; self.idx = idx; self.op = op


class Op:
    __slots__ = ("eng", "fn", "waits", "mark", "cnt")

    def __init__(self, eng, fn):
        self.eng = eng; self.fn = fn; self.waits = []; self.mark = False; self.cnt = 0


def _freeze(fn):
    if fn is None or fn.__closure__ is None:
        return fn
    cells = []
    for c in fn.__closure__:
        try:
            cells.append(types.CellType(c.cell_contents))
        except ValueError:
            cells.append(c)
    return types.FunctionType(fn.__code__, fn.__globals__, fn.__name__, fn.__defaults__, tuple(cells))


def _merge(d, ev):
    o = d.get(ev.key)
    if o is None or o.idx < ev.idx:
        d[ev.key] = ev


class Prog:
    def __init__(self, nc):
        self.nc = nc
        self.es = ExitStack()
        self.ops = {e: [] for e in ENGS}
        self.seen = {e: {} for e in ENGS}
        self.esem = {e: self.es.enter_context(nc.semaphore("s_" + e)) for e in ENGS}
        self.dsem = {}
        self.dcount = {}
        self.dnext = {}
        for q in ("sp", "pool", "act"):
            self.dsem[q] = [self.es.enter_context(nc.semaphore(f"d_{q}{i}")) for i in range(NDMA_SEM)]
            self.dcount[q] = [0] * NDMA_SEM
            self.dnext[q] = 0
        self.csem = {}
        self.arena = None
        self.sp = 0
        self.sp_max = 0
        self.last_ev = {}
        self.ntile = 0
        self.fence = {}
        self.scopes = []

    def sbuf(self, shape, dt, name=None):
        self.ntile += 1
        name = (name or "t") + f"_{self.ntile}"
        if self.arena is None:
            self.arena = self.es.enter_context(self.nc.sbuf_tensor("arena", [128, ARENA_F32], F32))
        shape = list(shape)
        esz = 4 if dt == F32 else 2
        nfree = 1
        for d in shape[1:]:
            nfree *= d
        nbytes = (nfree * esz + 31) // 32 * 32
        off = self.sp
        self.sp += nbytes
        assert self.sp <= ARENA_F32 * 4, f"SBUF arena overflow: {self.sp} > {ARENA_F32 * 4} ({name})"
        self.sp_max = max(self.sp_max, self.sp)
        v = self.arena[0:shape[0], off // 4:(off + nbytes) // 4]
        if dt != F32:
            v = v.bitcast(dt)
        v = v[:, 0:nfree]
        if len(shape) == 3:
            v = v.rearrange("p (a b) -> p a b", b=shape[2])
        t = T(v, name, self.fence)
        if self.scopes:
            self.scopes[-1][1].append(t)
        return t

    def psum(self, shape, dt, name=None):
        self.ntile += 1
        name = (name or "p") + f"_{self.ntile}"
        t = T(self.es.enter_context(self.nc.psum_tensor(name, list(shape), dt)), name)
        t.excl = True
        return t

    def dram(self, name, shape, dt, kind="Internal"):
        return T(self.nc.dram_tensor(name, list(shape), dt, kind=kind), name)

    @contextmanager
    def scope(self):
        tiles = []
        self.scopes.append((None, tiles))
        sp0 = self.sp
        try:
            yield
        finally:
            self.scopes.pop()
            f = dict(self.fence)
            for t in tiles:
                for ev in t.lws.values():
                    _merge(f, ev)
                for ev in t.rd.values():
                    _merge(f, ev)
            self.fence = f
            self.sp = sp0

    def _need(self, eng, ev, op):
        if ev.kind == "e":
            if ev.key[1] == eng and (eng == "pe" or not SAME_ENG_SYNC):
                return
            if self.seen[eng].get(ev.key, -1) >= ev.idx:
                return
            self.seen[eng][ev.key] = ev.idx
            ev.op.mark = True
            op.waits.append(ev)
        else:
            if self.seen[eng].get(ev.key, -1) >= ev.idx:
                return
            self.seen[eng][ev.key] = ev.idx
            op.waits.append(ev)

    def _deps(self, eng, op, reads, writes, pw):
        reads = [getattr(t, "b", t) for t in reads]; writes = [getattr(t, "b", t) for t in writes]; pw = [getattr(t, "b", t) for t in pw]
        need = {}
        for t in reads:
            for ev in t.lws.values():
                _merge(need, ev)
            if t.excl:
                for ev in t.rd.values():
                    if ev.kind != "e" or ev.key[1] != eng:
                        _merge(need, ev)
        for t in writes:
            for ev in t.lws.values():
                _merge(need, ev)
            for ev in t.rd.values():
                _merge(need, ev)
        for t in pw:
            for ev in t.rd.values():
                _merge(need, ev)
        for ev in need.values():
            self._need(eng, ev, op)

    def _record(self, ev, reads, writes, pw):
        reads = [getattr(t, "b", t) for t in reads]; writes = [getattr(t, "b", t) for t in writes]; pw = [getattr(t, "b", t) for t in pw]
        for t in reads:
            _merge(t.rd, ev)
        for t in writes:
            t.lws = {ev.key: ev}
            t.rd = {}
        for t in pw:
            _merge(t.lws, ev)

    def _cut(self):
        self.nops = getattr(self, "nops", 0) + 1
        if os.environ.get("KLIST"):
            import inspect
            fr = inspect.stack()[2]
            print("OP", self.nops, inspect.stack()[1].function, fr.lineno, (fr.code_context or [""])[0].strip()[:110])
        return self.nops > int(os.environ.get("KCUT", "1000000000"))

    def op(self, eng, fn, reads=(), writes=(), pw=()):
        if self._cut():
            return
        o = Op(eng, _freeze(fn))
        self._deps(eng, o, reads, writes, pw)
        idx = len(self.ops[eng])
        self.ops[eng].append(o)
        ev = Ev("e", ("e", eng), idx, o)
        self.last_ev[eng] = ev
        self._record(ev, reads, writes, pw)
        return o

    def dma(self, q, out_ap, in_ap, reads=(), writes=(), pw=(), **kw):
        if self._cut():
            return
        o = Op(q, None)
        self._deps(q, o, reads, writes, pw)
        si = self.dnext[q]
        self.dnext[q] = (si + 1) % NDMA_SEM
        prev = self.dcount[q][si]
        key = ("d", q, si)
        if prev > 0:
            self._need(q, Ev("d", key, prev), o)
        val = prev + 1
        self.dcount[q][si] = val
        sem = self.dsem[q][si]
        o.fn = lambda e: e.dma_start(out=out_ap, in_=in_ap, **kw).then_inc(sem, 16)
        self.ops[q].append(o)
        self._record(Ev("d", key, val), reads, writes, pw)

    def custom(self, q, fn, reads=(), writes=(), pw=()):
        o = Op(q, None)
        self._deps(q, o, reads, writes, pw)
        sem = self.es.enter_context(self.nc.semaphore(f"c_{len(self.csem)}"))
        key = ("c", len(self.csem))
        self.csem[key] = sem
        fn = _freeze(fn)
        o.fn = lambda e: fn(e, sem)
        self.ops[q].append(o)
        self._record(Ev("d", key, 1), reads, writes, pw)

    def wait_all(self, eng, tiles):
        o = Op(eng, None)
        need = {}
        for t in tiles:
            for ev in t.lws.values():
                _merge(need, ev)
            for ev in t.rd.values():
                _merge(need, ev)
        for ev in self.last_ev.values():
            _merge(need, ev)
        for q in self.dcount:
            for si, cnt in enumerate(self.dcount[q]):
                if cnt > 0:
                    _merge(need, Ev("d", ("d", q, si), cnt))
        for ev in need.values():
            self._need(eng, ev, o)
        self.ops[eng].append(o)

    def emit(self):
        nc = self.nc
        for e in ENGS:
            c = 0
            for o in self.ops[e]:
                if o.mark:
                    c += 1
                    o.cnt = c
        P = self

        def run(eng_name, e):
            for o in P.ops[eng_name]:
                for ev in o.waits:
                    if ev.kind == "e":
                        e.wait_ge(P.esem[ev.key[1]], ev.op.cnt)
                    elif ev.key[0] == "c":
                        e.wait_ge(P.csem[ev.key], 1)
                    else:
                        e.wait_ge(P.dsem[ev.key[1]][ev.key[2]], 16 * ev.idx)
                if o.fn is None:
                    continue
                r = o.fn(e)
                if o.mark:
                    r.then_inc(P.esem[eng_name], 1)

        with nc.Block() as block:
            @block.tensor
            def _(e):
                run("pe", e)

            @block.scalar
            def _(e):
                run("act", e)

            @block.vector
            def _(e):
                run("dve", e)

            @block.gpsimd
            def _(e):
                run("pool", e)

            @block.sync
            def _(e):
                run("sp", e)
        self.es.close()


class Rot:
    def __init__(self, P, n, shape, dt, name):
        self.b = [P.sbuf(shape, dt, name) for _ in range(n)]
        self.i = 0

    def get(self):
        t = self.b[self.i]
        self.i = (self.i + 1) % len(self.b)
        return t


PR_GMIX, PR_GMLP, PR_GMEM = 0, 16, 32
PR_CONV = 48
PR_GDNG = 72
PR_FQN, PR_FKN = 73, 74
PR_MQN, PR_MKN = 75, 77
PR_ALOG, PR_DTB, PR_BF = 79, 81, 83
NPRM = 85

C_ID, C_TRI, C_TRIS, C_ONES, C_SEL, C_NM = 0, 128, 256, 384, 512, 640
NCST = 768


def make_consts():
    c = np.zeros((128, NCST), np.float32)
    i = np.arange(128)
    c[:, C_ID:C_ID + 128] = np.eye(128)
    c[:, C_TRI:C_TRI + 128] = (i[:, None] <= i[None, :])
    c[:, C_TRIS:C_TRIS + 128] = (i[:, None] < i[None, :])
    c[:, C_ONES:C_ONES + 128] = 1.0
    c[64, C_SEL:C_SEL + 128] = 1.0
    j = np.arange(64)
    nm = np.zeros((128, 128), np.float32)
    nm[:64, 0:64] = np.where(j[None, :] > j[:, None], 0.0, NEG)
    nm[:64, 64:128] = np.where(j[None, :] >= j[:, None], 0.0, NEG)
    c[:, C_NM:C_NM + 128] = nm
    return c


def relay(W):
    K, N = W.shape
    return np.ascontiguousarray(W.reshape(K // 128, 128, N // 128, 128).transpose(2, 1, 0, 3)).reshape(N // 128, 128, K)


def relay_tok(W):
    K, n = W.shape
    return np.ascontiguousarray(W.reshape(K // 128, 128, n).transpose(1, 0, 2)).reshape(128, (K // 128) * n)


def build(S, dbg=False, PH=(0, 1, 2, 3, 4, 5)):
    nc = bass.Bass("TRN2", target_bir_lowering=False)
    NTILE = S // 512
    J = S // 128
    NCH = S // 64
    TQ = S // 4
    NT4 = min(512, TQ)
    NSUB = TQ // NT4

    def din(name, shape, dt=F32):
        return T(nc.dram_tensor(name, list(shape), dt, kind="ExternalInput"), name)

    xT = din("xT", [D, S]); xq = din("xq", [D, TQ]); memT = din("memT", [D, 256])
    prm_d = din("prm", [128, NPRM]); cst_d = din("cst", [128, NCST])
    w1 = din("w1", [14, 128, D]); w1s = din("w1s", [128, 16 * 262])
    wmk = din("wmk", [2, 128, D]); wmv = din("wmv", [128, 16 * 256])
    if 5 in PH:
        wg = din("wg", [48, 128, D]); wup = din("wup", [48, 128, 1024])
        wout = din("wout", [16, 128, D]); wff1 = din("wff1", [64, 128, D]); wff2 = din("wff2", [16, 128, DFF])
    outT = T(nc.dram_tensor("outT", [D, TQ], F32, kind="ExternalOutput"), "outT")

    P = Prog(nc)
    skind = "ExternalOutput" if dbg else "Internal"
    graw = [P.dram(f"graw{i}", [128, S], F32, skind) for i in range(8)]
    foxq = [P.dram(f"foxq{i}", [128, S], BF16, skind) for i in range(2)]
    foxk = [P.dram(f"foxk{i}", [128, S], BF16, skind) for i in range(2)]
    foxv = P.dram("foxv", [S, 256], BF16, skind)
    smallT = P.dram("smallT", [6, S], F32, skind)
    NCT = S // NT4
    omine = P.dram("omine_i", [NCT, 768, NT4], BF16)
    gath = P.dram("gath", [NCT, 4 * 768, NT4], BF16)

    def omine_write(row0, t0, ob, Dq=None):
        Dq = Dq or P
        for o in range(0, 512, NT4):
            ct = (t0 + o) // NT4
            Dq.dma("sp", omine[ct, row0:row0 + 128, :], ob[:, o:o + NT4], reads=[ob], pw=[omine])

    cst = P.sbuf([128, NCST], F32, "cst")
    prm = P.sbuf([128, NPRM], F32, "prm")
    cbf = P.sbuf([128, 512], BF16, "cbf")
    epsc = P.sbuf([128, 2], F32, "epsc")
    kTm = P.sbuf([128, 2, 256], BF16, "kTm")
    vm = P.sbuf([128, 2, 256], BF16, "vm")
    PS = [P.psum([128, 512], F32, f"ps{i}") for i in range(6)]
    PSB = [P.psum([128, 1024], BF16, f"psb{i}") for i in range(2)]

    P.dma("sp", cst[:, :], cst_d[:, :], writes=[cst])
    P.dma("sp", prm[:, :], prm_d[:, :], writes=[prm])
    P.op("dve", lambda e: e.tensor_copy(cbf[:, :], cst[:, 0:512]), reads=[cst], writes=[cbf])
    P.op("pool", lambda e: e.memset(epsc[:, 0:1], EPS), writes=[epsc])
    P.op("pool", lambda e: e.memset(epsc[:, 1:2], 1.0), pw=[epsc])
    ident_bf = cbf[:, 0:128]; tri_bf = cbf[:, 128:256]; ones_bf = cbf[:, 384:512]

    def rstd_from_ss(ss_ps, n, inv_d, out_t, np_=128):
        P.op("act", lambda e: e.activation(out_t[0:np_, 0:n], ss_ps[0:np_, 0:n], AF.Sqrt, bias=epsc[0:np_, 0:1], scale=inv_d),
             reads=[ss_ps, epsc], writes=[out_t])
        P.op("dve", lambda e: e.reciprocal(out_t[0:np_, 0:n], out_t[0:np_, 0:n]), reads=[out_t], writes=[out_t])

    tmp2 = Rot(P, 3, [128, 512], F32, "tmp2")

    def stt2(out_t, out_ap, in_t, in_ap, gc, rs_tile, rs_ap, n, first):
        tm = tmp2.get()
        P.op("act", lambda e: e.activation(tm[:, 0:n], in_ap, AF.Copy, scale=prm[:, gc:gc + 1]), reads=[in_t, prm], writes=[tm])
        P.op("dve", lambda e: e.tensor_tensor(out_ap, tm[:, 0:n], rs_ap, ALU.mult), reads=[tm, rs_tile],
             writes=[out_t] if first else (), pw=() if first else [out_t])

    def rmsnorm_tile(xt, n, gcol, hT, sqrot, rs_t):
        ss = PS[0]
        for c in range(16):
            sq = sqrot.get()
            P.op("act", lambda e, c=c, sq=sq: e.activation(sq[:, 0:n], xt[:, c, 0:n], AF.Square), reads=[xt], writes=[sq])
            P.op("pe", lambda e, c=c, sq=sq: e.matmul(ss[:, 0:n], ones_bf, sq[:, 0:n], start=(c == 0), stop=(c == 15)),
                 reads=[cbf, sq], writes=[ss] if c == 0 else (), pw=() if c == 0 else [ss])
        rstd_from_ss(ss, n, 1.0 / D, rs_t)
        for c in range(16):
            stt2(hT, hT[:, c, 0:n], xt, xt[:, c, 0:n], gcol + c, rs_t, rs_t[:, 0:n], n, c == 0)

    def wload(dst, src2d, n, stg=None):
        for o in range(0, n, 4096):
            m = min(4096, n - o)
            first = (o == 0)
            P.dma("pool", dst[:, o:o + m], src2d[:, o:o + m], writes=[dst] if first else (), pw=() if first else [dst])

    def phase(n):
        if n in PH:
            with P.scope():
                yield

    for _ in phase(0):
        mt = P.sbuf([128, 16, 256], F32, "mt")
        mh = P.sbuf([128, 16, 256], BF16, "mh")
        sqrot = Rot(P, 3, [128, 512], BF16, "sq")
        rs_t = P.sbuf([128, 512], F32, "rs")
        wk_b = [P.sbuf([128, 2048], BF16, "wk") for _ in range(2)]
        wv_b = P.sbuf([128, 4096], BF16, "wv")
        stg = Rot(P, 2, [128, 2048], F32, "stg")
        kraw = P.sbuf([128, 2, 256], F32, "kraw")
        P.dma("sp", mt[:, :, :], memT[:, :].rearrange("(c p) n -> p c n", p=128), writes=[mt])
        for dc in range(2):
            wload(wk_b[dc], wmk[dc], 2048, stg)
        wload(wv_b, wmv, 4096, stg)
        rmsnorm_tile(mt, 256, PR_GMEM, mh, sqrot, rs_t)
        ss2 = PS[3]
        for dc in range(2):
            ps = PS[1 + dc]
            for k in range(16):
                P.op("pe", lambda e, k=k, dc=dc, ps=ps: e.matmul(ps[:, 0:256], wk_b[dc][:, k * 128:(k + 1) * 128], mh[:, k, :], start=(k == 0), stop=(k == 15)),
                     reads=[wk_b[dc], mh], writes=[ps] if k == 0 else (), pw=() if k == 0 else [ps])
            P.op("dve", lambda e, dc=dc, ps=ps: e.tensor_copy(kraw[:, dc, :], ps[:, 0:256]), reads=[ps], writes=[kraw] if dc == 0 else (), pw=() if dc == 0 else [kraw])
            sq = sqrot.get()
            P.op("act", lambda e, sq=sq, dc=dc: e.activation(sq[:, 0:256], kraw[:, dc, :], AF.Square), reads=[kraw], writes=[sq])
            P.op("pe", lambda e, sq=sq, dc=dc: e.matmul(ss2[:, 0:256], ones_bf, sq[:, 0:256], start=(dc == 0), stop=(dc == 1)),
                 reads=[cbf, sq], writes=[ss2] if dc == 0 else (), pw=() if dc == 0 else [ss2])
        rstd_from_ss(ss2, 256, 1.0 / 256, rs_t)
        for dc in range(2):
            stt2(kTm, kTm[:, dc, :], kraw, kraw[:, dc, :], PR_MKN + dc, rs_t, rs_t[:, 0:256], 256, dc == 0)
        for kc in range(2):
            ps = PS[4 + kc]
            for k in range(16):
                P.op("pe", lambda e, k=k, kc=kc, ps=ps: e.matmul(ps[:, 0:256], mh[:, k, kc * 128:(kc + 1) * 128], wv_b[:, k * 256:(k + 1) * 256], start=(k == 0), stop=(k == 15)),
                     reads=[wv_b, mh], writes=[ps] if k == 0 else (), pw=() if k == 0 else [ps])
            P.op("act", lambda e, kc=kc, ps=ps: e.copy(vm[:, kc, :], ps[:, 0:256]), reads=[ps], writes=[vm] if kc == 0 else (), pw=() if kc == 0 else [vm])

    for _ in phase(1):
        w1b = [P.sbuf([128, 2048], BF16, "w1b") for _ in range(14)]
        w1sb = P.sbuf([128, 16 * 262], BF16, "w1sb")
        with P.scope():
            stg = None
            for cc in range(14):
                wload(w1b[cc], w1[cc], 2048, stg)
            wload(w1sb, w1s, 16 * 262, stg)
        xrot = Rot(P, 2, [128, 16, 512], F32, "xt")
        hT = P.sbuf([128, 16, 512], BF16, "hT")
        sqrot = Rot(P, 3, [128, 512], BF16, "sq")
        rs_t = P.sbuf([128, 512], F32, "rs")
        rs2 = P.sbuf([128, 512], F32, "rs2")
        rawrot = Rot(P, 3, [128, 512], F32, "raw")
        mqraw = P.sbuf([128, 2, 512], F32, "mqraw")
        mqn = P.sbuf([128, 2, 512], BF16, "mqn")
        obrot = Rot(P, 3, [128, 512], BF16, "ob")
        pTm = [P.sbuf([128, 512], BF16, "pTm") for _ in range(2)]
        vst = Rot(P, 2, [128, 256], BF16, "vst")
        sst = Rot(P, 2, [128, 6], F32, "sst")
        rden = P.sbuf([128, 512], F32, "rden")
        for t in range(NTILE):
            t0 = t * 512
            xt = xrot.get()
            P.dma("sp", xt[:, :, :], xT[:, t0:t0 + 512].rearrange("(c p) n -> p c n", p=128), writes=[xt])
            rmsnorm_tile(xt, 512, PR_GMIX, hT, sqrot, rs_t)
            for cc in range(14):
                ps = PS[1 + (cc % 2)]
                for k in range(16):
                    P.op("pe", lambda e, k=k, cc=cc, ps=ps: e.matmul(ps[:, :], w1b[cc][:, k * 128:(k + 1) * 128], hT[:, k, :], start=(k == 0), stop=(k == 15)),
                         reads=[w1b[cc], hT], writes=[ps] if k == 0 else (), pw=() if k == 0 else [ps])
                if cc < 8:
                    raw = rawrot.get()
                    P.op("act", lambda e, raw=raw, ps=ps: e.copy(raw[:, :], ps[:, :]), reads=[ps], writes=[raw])
                    P.dma("sp", graw[cc][:, t0:t0 + 512], raw[:, :], reads=[raw], pw=[graw[cc]])
                elif cc < 12:
                    raw = rawrot.get()
                    sq = sqrot.get()
                    P.op("dve", lambda e, raw=raw, ps=ps: e.tensor_copy(raw[:, :], ps[:, :]), reads=[ps], writes=[raw])
                    P.op("act", lambda e, sq=sq, ps=ps: e.activation(sq[:, :], ps[:, :], AF.Square), reads=[ps], writes=[sq])
                    P.op("pe", lambda e, sq=sq: e.matmul(PS[3][:, :], ones_bf, sq[:, :], start=True, stop=True), reads=[cbf, sq], writes=[PS[3]])
                    rstd_from_ss(PS[3], 512, 1.0 / 128, rs2)
                    ob = obrot.get()
                    gc = PR_FQN if cc < 10 else PR_FKN
                    stt2(ob, ob[:, :], raw, raw[:, :], gc, rs2, rs2[:, :], 512, True)
                    dst = (foxq if cc < 10 else foxk)[cc % 2]
                    P.dma("sp", dst[:, t0:t0 + 512], ob[:, :], reads=[ob], pw=[dst])
                else:
                    dc = cc - 12
                    sq = sqrot.get()
                    P.op("dve", lambda e, dc=dc, ps=ps: e.tensor_copy(mqraw[:, dc, :], ps[:, :]), reads=[ps], writes=[mqraw] if dc == 0 else (), pw=() if dc == 0 else [mqraw])
                    P.op("act", lambda e, sq=sq, ps=ps: e.activation(sq[:, :], ps[:, :], AF.Square), reads=[ps], writes=[sq])
                    P.op("pe", lambda e, sq=sq, dc=dc: e.matmul(PS[3][:, :], ones_bf, sq[:, :], start=(dc == 0), stop=(dc == 1)),
                         reads=[cbf, sq], writes=[PS[3]] if dc == 0 else (), pw=() if dc == 0 else [PS[3]])
            rstd_from_ss(PS[3], 512, 1.0 / 256, rs2)
            for dc in range(2):
                stt2(mqn, mqn[:, dc, :], mqraw, mqraw[:, dc, :], PR_MQN + dc, rs2, rs2[:, :], 512, dc == 0)
            for kc in range(2):
                ps = PS[1 + kc]
                for dc in range(2):
                    P.op("pe", lambda e, kc=kc, dc=dc, ps=ps: e.matmul(ps[:, :], kTm[:, dc, kc * 128:(kc + 1) * 128], mqn[:, dc, :], start=(dc == 0), stop=(dc == 1)),
                         reads=[kTm, mqn], writes=[ps] if dc == 0 else (), pw=() if dc == 0 else [ps])
                P.op("act", lambda e, kc=kc, ps=ps: e.activation(pTm[kc][:, :], ps[:, :], AF.Exp, scale=1.0 / 16.0), reads=[ps], writes=[pTm[kc]])
            for kc in range(2):
                P.op("pe", lambda e, kc=kc: e.matmul(PS[3][:, :], ones_bf, pTm[kc][:, :], start=(kc == 0), stop=(kc == 1)),
                     reads=[cbf, pTm[kc]], writes=[PS[3]] if kc == 0 else (), pw=() if kc == 0 else [PS[3]])
            P.op("dve", lambda e: e.reciprocal(rden[:, :], PS[3][:, :]), reads=[PS[3]], writes=[rden])
            for dvc in range(2):
                ps = PS[4 + dvc]
                for kc in range(2):
                    P.op("pe", lambda e, kc=kc, dvc=dvc, ps=ps: e.matmul(ps[:, :], vm[:, kc, dvc * 128:(dvc + 1) * 128], pTm[kc][:, :], start=(kc == 0), stop=(kc == 1)),
                         reads=[vm, pTm[kc]], writes=[ps] if kc == 0 else (), pw=() if kc == 0 else [ps])
                ob = obrot.get()
                P.op("dve", lambda e, ob=ob, ps=ps: e.tensor_tensor(ob[:, :], ps[:, :], rden[:, :], ALU.mult), reads=[ps, rden], writes=[ob])
                omine_write(512 + dvc * 128, t0, ob)
            for blk in range(4):
                ps = PS[0]
                for k in range(16):
                    P.op("pe", lambda e, k=k, blk=blk: e.matmul(ps[:, 0:262], hT[:, k, blk * 128:(blk + 1) * 128], w1sb[:, k * 262:(k + 1) * 262], start=(k == 0), stop=(k == 15)),
                         reads=[w1sb, hT], writes=[ps] if k == 0 else (), pw=() if k == 0 else [ps])
                v_s = vst.get(); s_s = sst.get()
                P.op("act", lambda e, v_s=v_s: e.copy(v_s[:, :], ps[:, 0:256]), reads=[ps], writes=[v_s])
                P.op("dve", lambda e, s_s=s_s: e.tensor_copy(s_s[:, :], ps[:, 256:262]), reads=[ps], writes=[s_s])
                r0 = t0 + blk * 128
                P.dma("sp", foxv[r0:r0 + 128, :], v_s[:, :], reads=[v_s], pw=[foxv])
                P.dma("sp", smallT[:, r0:r0 + 128].rearrange("c p -> p c"), s_s[:, :], reads=[s_s], pw=[smallT], allow_slow_non_contiguous=True)

    for _ in phase(2):
        kT = P.sbuf([128, S], BF16, "kT")
        V = P.sbuf([128, J, 128], BF16, "V")
        lf = P.sbuf([128, J], F32, "lf")
        lfJ = P.sbuf([J, 128], F32, "lfJ")
        totc = P.sbuf([J, 1], F32, "totc")
        totB = P.sbuf([J, 128], F32, "totB")
        cT = P.sbuf([128, J], F32, "cT")
        negc = P.sbuf([128, J], F32, "negc")
        cmidB = P.sbuf([128, J], F32, "cmidB")
        nbf = P.sbuf([128, 1], F32, "nbf")
        brot = Rot(P, 8, [128, J], F32, "bias")
        qrot = Rot(P, 2, [128, 512], BF16, "qg")
        prot = Rot(P, 3, [128, 512], BF16, "pT")
        rden = P.sbuf([128, 512], F32, "rden")
        obrot = Rot(P, 2, [128, 512], BF16, "ob")
        strot = [PS[2], PS[3], PS[4]]
        sti = 0
        for h in range(2):
            P.dma("sp", kT[:, :], foxk[h][:, :], reads=[foxk[h]], writes=[kT])
            P.dma("sp", V[:, :, :], foxv[:, h * 128:(h + 1) * 128].rearrange("(j p) c -> p j c", p=128), reads=[foxv], writes=[V])
            P.dma("sp", lf[:, :], smallT[h, :].rearrange("(j p) -> p j", p=128), reads=[smallT], writes=[lf], allow_slow_non_contiguous=True)
            P.dma("sp", lfJ[:, :], smallT[h, :].rearrange("(j p) -> j p", p=128), reads=[smallT], writes=[lfJ])
            P.op("dve", lambda e, h=h: e.tensor_scalar(nbf[:, :], prm[:, PR_BF + h:PR_BF + h + 1], -1.0, None, ALU.mult), reads=[prm], writes=[nbf])
            for tl, npart in ((lf, 128), (lfJ, J)):
                P.op("act", lambda e, tl=tl, npart=npart: e.activation(tl[:, :], tl[:, :], AF.Exp, bias=nbf[0:npart, 0:1], scale=-1.0), reads=[tl, nbf], writes=[tl])
                P.op("act", lambda e, tl=tl, npart=npart: e.activation(tl[:, :], tl[:, :], AF.Ln, bias=epsc[0:npart, 1:2], scale=1.0), reads=[tl, epsc], writes=[tl])
            P.op("dve", lambda e: e.reduce_sum(totc[:, :], lfJ[:, :], mybir.AxisListType.X), reads=[lfJ], writes=[totc])
            P.op("dve", lambda e: e.tensor_scalar(totB[:, :], cst[0:J, C_ONES:C_ONES + 128], totc[:, 0:1], None, ALU.mult), reads=[cst, totc], writes=[totB])
            ps = PS[5]
            P.op("pe", lambda e: e.matmul(ps[:, 0:J], cst[:, C_TRI:C_TRI + 128], lf[:, :], start=True, stop=False), reads=[cst, lf], writes=[ps])
            P.op("pe", lambda e: e.matmul(ps[:, 0:J], totB[:, :], cst[0:J, C_TRIS:C_TRIS + J], start=False, stop=True), reads=[cst, totB], pw=[ps])
            P.op("dve", lambda e: e.tensor_copy(negc[:, :], ps[:, 0:J]), reads=[ps], writes=[negc])
            P.op("dve", lambda e: e.tensor_scalar(cT[:, :], ps[:, 0:J], -1.0, None, ALU.mult), reads=[ps], writes=[cT])
            P.op("pe", lambda e: e.matmul(ps[:, 0:J], cst[:, C_SEL:C_SEL + 128], cT[:, :], start=True, stop=True), reads=[cst, cT], writes=[ps])
            P.op("dve", lambda e: e.tensor_copy(cmidB[:, :], ps[:, 0:J]), reads=[ps], writes=[cmidB])
            for G in range(NTILE):
                qg = qrot.get()
                P.dma("sp", qg[:, :], foxq[h][:, G * 512:(G + 1) * 512], reads=[foxq[h]], writes=[qg])
                biases = []
                for ib in range(4):
                    i = 4 * G + ib
                    bt = brot.get()
                    P.op("dve", lambda e, bt=bt, i=i: e.tensor_scalar(bt[:, 0:i + 1], negc[:, 0:i + 1], cmidB[:, i:i + 1], None, ALU.add), reads=[negc, cmidB], writes=[bt])
                    biases.append(bt)
                o_ps = PS[0]; d_ps = PS[1]
                order = list(range(4 * G, 4 * G + 4)) + list(range(0, 4 * G))
                for n, j in enumerate(order):
                    qlo = max(0, j - 4 * G)
                    c0 = qlo * 128
                    st = strot[sti]; sti = (sti + 1) % 3
                    P.op("pe", lambda e, j=j, c0=c0, st=st, qg=qg: e.matmul(st[:, c0:512], kT[:, j * 128:(j + 1) * 128], qg[:, c0:512], start=True, stop=True),
                         reads=[kT, qg], writes=[st])
                    pT = prot.get()
                    for ib in range(qlo, 4):
                        bt = biases[ib]
                        P.op("act", lambda e, ib=ib, bt=bt, pT=pT, st=st, j=j: e.activation(pT[:, ib * 128:(ib + 1) * 128], st[:, ib * 128:(ib + 1) * 128], AF.Exp,
                                                                                             bias=bt[:, j:j + 1], scale=128.0 ** -0.5),
                             reads=[st, bt], writes=[pT] if ib == qlo else (), pw=() if ib == qlo else [pT])
                    if j >= 4 * G:
                        P.op("pool", lambda e, pT=pT, c0=c0: e.tensor_tensor(pT[:, c0:c0 + 128], pT[:, c0:c0 + 128], tri_bf, ALU.mult), reads=[pT, cbf], writes=[pT])
                    first = (n == 0); last = (n == len(order) - 1)
                    P.op("pe", lambda e, j=j, c0=c0, pT=pT, first=first, last=last: e.matmul(o_ps[:, c0:512], V[:, j, :], pT[:, c0:512], start=first, stop=last, skip_group_check=True),
                         reads=[V, pT], writes=[o_ps] if first else (), pw=() if first else [o_ps])
                    P.op("pe", lambda e, c0=c0, pT=pT, first=first, last=last: e.matmul(d_ps[:, c0:512], ones_bf, pT[:, c0:512], start=first, stop=last, skip_group_check=True),
                         reads=[cbf, pT], writes=[d_ps] if first else (), pw=() if first else [d_ps])
                P.op("dve", lambda e: e.reciprocal(rden[:, :], d_ps[:, :]), reads=[d_ps], writes=[rden])
                ob = obrot.get()
                P.op("dve", lambda e, ob=ob: e.tensor_tensor(ob[:, :], o_ps[:, :], rden[:, :], ALU.mult), reads=[o_ps, rden], writes=[ob])
                omine_write(256 + h * 128, G * 512, ob)

    if 3 in PH:
      gdn_phase(P, nc, S, NTILE, NCH, PS, PSB, cst, cbf, prm, epsc, graw, smallT, omine_write)

    if 4 in PH:
        for ct in range(NCT):
            P.custom("pool", lambda e, sem, ct=ct: e.collective_compute("AllGather", ALU.bypass, replica_groups=[[0, 1, 2, 3], [4, 5, 6, 7]],
                                                                       ins=[omine.t[ct].opt()], outs=[gath.t[ct].opt()]).then_inc(sem, 1),
                     reads=[omine], writes=[gath] if ct == 0 else (), pw=() if ct == 0 else [gath])

    pid = nc.partition_id()
    qid = pid % 4
    for _ in phase(5):
        xt = P.sbuf([128, 16, NT4], F32, "xt4")
        sqrot = Rot(P, 3, [128, 512], BF16, "sq")
        rs_t = P.sbuf([128, 512], F32, "rs")
        wrot = Rot(P, 6, [128, 2048], BF16, "w4")
        wurot = Rot(P, 4, [128, 1024], BF16, "wu4")
        w2rot = Rot(P, 3, [128, 4096], BF16, "wf2")
        stg = None
        sgrot = Rot(P, 2, [128, 512], F32, "sg")
        tmrot = Rot(P, 2, [128, 512], F32, "tm")
        yacc = P.sbuf([128, 512], F32, "yacc")
        orot = Rot(P, 2, [128, 512], F32, "o4")
        for sub in range(NSUB):
            c0 = sub * NT4
            P.dma("sp", xt[:, :, :], xq[:, c0:c0 + NT4].rearrange("(c p) n -> p c n", p=128), writes=[xt])
            with P.scope():
                hT = P.sbuf([128, 16, NT4], BF16, "h4")
                oT = P.sbuf([128, 24, NT4], BF16, "o4T")
                yT = P.sbuf([128, 16, NT4], BF16, "y4")
                rmsnorm_tile(xt, NT4, PR_GMIX, hT, sqrot, rs_t)
                src = gath.t[bass.ds(qid * NSUB + sub, 1), :, :].rearrange("o (r p) n -> p (o r) n", p=128)
                P.dma("sp", oT[:, :, :], src, reads=[gath], writes=[oT])
                for c in range(16):
                    for br in range(3):
                        wgb = wrot.get(); wub = wurot.get()
                        wload(wgb, wg[br * 16 + c], 2048, stg)
                        wload(wub, wup[br * 16 + c], 1024, stg)
                        gps = PS[1 + (br % 2)]; ups = PS[3 + (br % 2)]
                        for k in range(16):
                            P.op("pe", lambda e, k=k, wgb=wgb, gps=gps: e.matmul(gps[:, 0:NT4], wgb[:, k * 128:(k + 1) * 128], hT[:, k, :], start=(k == 0), stop=(k == 15)),
                                 reads=[wgb, hT], writes=[gps] if k == 0 else (), pw=() if k == 0 else [gps])
                        for k in range(8):
                            rk = (k // 2) * 6 + br * 2 + (k % 2)
                            P.op("pe", lambda e, k=k, rk=rk, wub=wub, ups=ups: e.matmul(ups[:, 0:NT4], wub[:, k * 128:(k + 1) * 128], oT[:, rk, :], start=(k == 0), stop=(k == 7)),
                                 reads=[wub, oT], writes=[ups] if k == 0 else (), pw=() if k == 0 else [ups])
                        sg = sgrot.get()
                        P.op("act", lambda e, sg=sg, gps=gps: e.activation(sg[:, 0:NT4], gps[:, 0:NT4], AF.Sigmoid), reads=[gps], writes=[sg])
                        if br == 0:
                            P.op("dve", lambda e, sg=sg, ups=ups: e.tensor_tensor(yacc[:, 0:NT4], sg[:, 0:NT4], ups[:, 0:NT4], ALU.mult), reads=[sg, ups], writes=[yacc])
                        else:
                            tm = tmrot.get()
                            P.op("dve", lambda e, sg=sg, ups=ups, tm=tm: e.tensor_tensor(tm[:, 0:NT4], sg[:, 0:NT4], ups[:, 0:NT4], ALU.mult), reads=[sg, ups], writes=[tm])
                            if br == 1:
                                P.op("dve", lambda e, tm=tm: e.tensor_tensor(yacc[:, 0:NT4], yacc[:, 0:NT4], tm[:, 0:NT4], ALU.add), reads=[tm, yacc], writes=[yacc])
                            else:
                                P.op("dve", lambda e, tm=tm, c=c: e.tensor_tensor(yT[:, c, :], yacc[:, 0:NT4], tm[:, 0:NT4], ALU.add), reads=[tm, yacc],
                                     writes=[yT] if c == 0 else (), pw=() if c == 0 else [yT])
                for c in range(16):
                    wb = wrot.get()
                    wload(wb, wout[c], 2048, stg)
                    ps = PS[1 + (c % 2)]
                    for k in range(16):
                        P.op("pe", lambda e, k=k, wb=wb, ps=ps: e.matmul(ps[:, 0:NT4], wb[:, k * 128:(k + 1) * 128], yT[:, k, :], start=(k == 0), stop=(k == 15)),
                             reads=[wb, yT], writes=[ps] if k == 0 else (), pw=() if k == 0 else [ps])
                    P.op("dve", lambda e, c=c, ps=ps: e.tensor_tensor(xt[:, c, :], xt[:, c, :], ps[:, 0:NT4], ALU.add), reads=[ps, xt], pw=[xt])
            with P.scope():
                h2 = P.sbuf([128, 16, NT4], BF16, "h24")
                uT = P.sbuf([128, 32, NT4], BF16, "u4")
                rrot = Rot(P, 2, [128, 512], F32, "r4")
                rmsnorm_tile(xt, NT4, PR_GMLP, h2, sqrot, rs_t)
                for half in range(2):
                    for fi in range(32):
                        f = half * 32 + fi
                        wb = wrot.get()
                        wload(wb, wff1[f], 2048, stg)
                        ps = PS[1 + (f % 2)]
                        for k in range(16):
                            P.op("pe", lambda e, k=k, wb=wb, ps=ps: e.matmul(ps[:, 0:NT4], wb[:, k * 128:(k + 1) * 128], h2[:, k, :], start=(k == 0), stop=(k == 15)),
                                 reads=[wb, h2], writes=[ps] if k == 0 else (), pw=() if k == 0 else [ps])
                        r = rrot.get()
                        P.op("act", lambda e, r=r, ps=ps: e.activation(r[:, 0:NT4], ps[:, 0:NT4], AF.Relu), reads=[ps], writes=[r])
                        P.op("dve", lambda e, r=r, fi=fi: e.tensor_tensor(uT[:, fi, :], r[:, 0:NT4], r[:, 0:NT4], ALU.mult), reads=[r],
                             writes=[uT] if fi == 0 else (), pw=() if fi == 0 else [uT])
                    for c in range(16):
                        wb = w2rot.get()
                        wload(wb, wff2[c][:, half * 4096:(half + 1) * 4096], 4096, stg)
                        ps = PS[3 + (c % 2)]
                        for fi in range(32):
                            P.op("pe", lambda e, fi=fi, wb=wb, ps=ps: e.matmul(ps[:, 0:NT4], wb[:, fi * 128:(fi + 1) * 128], uT[:, fi, :], start=(fi == 0), stop=(fi == 31)),
                                 reads=[wb, uT], writes=[ps] if fi == 0 else (), pw=() if fi == 0 else [ps])
                        if half == 0:
                            P.op("dve", lambda e, c=c, ps=ps: e.tensor_tensor(xt[:, c, :], xt[:, c, :], ps[:, 0:NT4], ALU.add), reads=[ps, xt], pw=[xt])
                        else:
                            o = orot.get()
                            P.op("dve", lambda e, c=c, ps=ps, o=o: e.tensor_tensor(o[:, 0:NT4], xt[:, c, :], ps[:, 0:NT4], ALU.add), reads=[ps, xt], writes=[o])
                            P.dma("sp", outT[c * 128:(c + 1) * 128, c0:c0 + NT4], o[:, 0:NT4], reads=[o], pw=[outT])
    fin = [outT]
    if dbg:
        fin += [foxv, smallT] + graw + foxq + foxk
    P.wait_all("sp", fin)
    P.emit()
    return nc


class View:
    __slots__ = ("ap", "b")

    def __init__(self, b, ap):
        self.b = b; self.ap = ap

    def __getitem__(self, idx):
        return self.ap[idx]


class Rec:
    def __init__(self):
        self.l = []

    def op(self, eng, fn, reads=(), writes=(), pw=()):
        self.l.append(("op", eng, _freeze(fn), tuple(reads), tuple(writes), tuple(pw)))

    def dma(self, q, out_ap, in_ap, reads=(), writes=(), pw=(), **kw):
        self.l.append(("dma", q, out_ap, in_ap, tuple(reads), tuple(writes), tuple(pw), kw))


def replay(P, recs):
    n = max(len(r.l) for r in recs)
    for k in range(n):
        for r in recs:
            if k < len(r.l):
                it = r.l[k]
                if it[0] == "op":
                    P.op(it[1], it[2], reads=it[3], writes=it[4], pw=it[5])
                else:
                    P.dma(it[1], it[2], it[3], reads=it[4], writes=it[5], pw=it[6], **it[7])


def gdn_phase(P, nc, S, NTILE, NCH, PS, PSB, cst, cbf, prm, epsc, graw, smallT, omine_write):
    ident_bf = cbf[:, 0:128]; ones_bf = cbf[:, 384:512]
    I64 = cbf[0:64, 0:64]
    UT64 = cst[0:64, C_TRI:C_TRI + 64]
    ID64 = cst[0:64, C_ID:C_ID + 64]

    def head(R, hi):
        ss = PS[hi]
        X = PS[2 + 2 * hi]; Yb = PS[3 + 2 * hi]; PB = PSB[hi]
        A = View(X, X[:, 0:129]); B = View(X, X[0:64, 129:257]); C = View(X, X[0:64, 257:385]); Dp = View(X, X[0:64, 385:449])
        E1 = View(Yb, Yb[:, 0:64]); E2 = View(Yb, Yb[:, 64:128]); E3 = View(Yb, Yb[:, 128:256])
        F1 = View(Yb, Yb[0:64, 256:384]); F2 = View(Yb, Yb[0:64, 384:512])
        PB1 = View(PB, PB[0:64, 0:64]); PB2 = View(PB, PB[0:64, 128:384])
        gall = P.sbuf([64, NCH], F32, "gall"); ball = P.sbuf([64, NCH], F32, "ball")
        nea = P.sbuf([64, 1], F32, "nea")
        xb = [P.sbuf([128, 515], F32, f"xb{i}") for i in range(4)]
        y = P.sbuf([128, 512], F32, "gy"); tmp = [P.sbuf([128, 512], F32, "gtmp") for _ in range(2)]; sg = P.sbuf([128, 512], F32, "gsg")
        yc = P.sbuf([128, 512], F32, "gyc")
        sq = P.sbuf([128, 512], BF16, "gsq"); rs = P.sbuf([128, 512], F32, "grs")
        qh = P.sbuf([128, 512], BF16, "qh"); kh = P.sbuf([128, 512], BF16, "kh"); vb = P.sbuf([128, 512], BF16, "vb")
        zs = P.sbuf([128, 512], F32, "zs"); oraw = P.sbuf([128, 512], F32, "oraw"); on = P.sbuf([128, 512], F32, "on")
        ob = P.sbuf([128, 512], BF16, "gob")
        gB = P.sbuf([64, 128], F32, "gB"); bB = P.sbuf([64, 128], F32, "bB")
        gamRow = P.sbuf([128, 64], F32, "gamRow"); betaRow = P.sbuf([128, 64], F32, "betaRow"); aRow = P.sbuf([128, 64], F32, "aRow")
        gamCol = P.sbuf([64, 1], F32, "gamCol"); acol = P.sbuf([64, 1], F32, "acol"); rcol = P.sbuf([64, 1], F32, "rcol"); bacol = P.sbuf([64, 1], F32, "bacol")
        t0d = P.sbuf([64, 64], F32, "t0d"); d2 = P.sbuf([64, 128], F32, "d2")
        kbq = P.sbuf([128, 128], BF16, "kbq"); NQ = P.sbuf([64, 128], BF16, "NQ")
        PP = [P.sbuf([64, 128], BF16, f"PP{i}") for i in range(2)]
        Y = [P.sbuf([64, 64], BF16, f"Y{i}") for i in range(2)]
        RHSw = P.sbuf([64, 128], BF16, "RHSw"); kdec = P.sbuf([64, 128], BF16, "kdec"); RHSv = P.sbuf([64, 128], BF16, "RHSv")
        nwT = P.sbuf([128, 64], BF16, "nwT"); uc = P.sbuf([64, 128], F32, "uc"); qdT = P.sbuf([128, 64], BF16, "qdT")
        u = P.sbuf([64, 128], BF16, "u"); S32 = P.sbuf([128, 128], F32, "S32"); Sbf = P.sbuf([128, 128], BF16, "Sbf"); St = P.sbuf([128, 128], F32, "St")
        R.dma("sp", gall[:, :], smallT[4 + hi, :].rearrange("(n p) -> p n", p=64), reads=[smallT], writes=[gall], allow_slow_non_contiguous=True)
        R.dma("sp", ball[:, :], smallT[2 + hi, :].rearrange("(n p) -> p n", p=64), reads=[smallT], writes=[ball], allow_slow_non_contiguous=True)
        R.op("act", lambda e: e.activation(nea[:, :], prm[0:64, PR_ALOG + hi:PR_ALOG + hi + 1], AF.Exp), reads=[prm], writes=[nea])
        R.op("dve", lambda e: e.tensor_scalar(nea[:, :], nea[:, :], -1.0, None, ALU.mult), reads=[nea], writes=[nea])
        R.op("act", lambda e: e.activation(gall[:, :], gall[:, :], AF.Exp, bias=prm[0:64, PR_DTB + hi:PR_DTB + hi + 1], scale=1.0), reads=[gall, prm], writes=[gall])
        R.op("act", lambda e: e.activation(gall[:, :], gall[:, :], AF.Ln, bias=epsc[0:64, 1:2], scale=1.0), reads=[gall, epsc], writes=[gall])
        R.op("dve", lambda e: e.tensor_scalar(gall[:, :], gall[:, :], nea[:, 0:1], None, ALU.mult), reads=[gall, nea], writes=[gall])
        R.op("act", lambda e: e.activation(ball[:, :], ball[:, :], AF.Sigmoid), reads=[ball], writes=[ball])
        R.op("pool", lambda e: e.memset(S32[:, :], 0.0), writes=[S32])
        R.op("pool", lambda e: e.memset(Sbf[:, :], 0.0), writes=[Sbf])
        for t in range(NTILE):
            c0t = t * 512
            for gi in range(4):
                src = graw[gi * 2 + hi]
                if t == 0:
                    R.op("pool", lambda e, gi=gi: e.memset(xb[gi][:, 0:3], 0.0), writes=[xb[gi]])
                    R.dma("sp", xb[gi][:, 3:515], src[:, 0:512], reads=[src], pw=[xb[gi]])
                else:
                    R.dma("sp", xb[gi][:, 0:515], src[:, c0t - 3:c0t + 512], reads=[src], writes=[xb[gi]])
            for gi in range(3):
                wc = PR_CONV + (gi * 2 + hi) * 4
                R.op("act", lambda e, gi=gi, wc=wc: e.activation(y[:, :], xb[gi][:, 0:512], AF.Copy, scale=prm[:, wc:wc + 1]), reads=[xb[gi], prm], writes=[y])
                for i in range(1, 4):
                    tm = tmp[i % 2]
                    R.op("act", lambda e, gi=gi, wc=wc, i=i, tm=tm: e.activation(tm[:, :], xb[gi][:, i:i + 512], AF.Copy, scale=prm[:, wc + i:wc + i + 1]), reads=[xb[gi], prm], writes=[tm])
                    R.op("dve", lambda e, tm=tm: e.tensor_tensor(y[:, :], y[:, :], tm[:, :], ALU.add), reads=[y, tm], writes=[y])
                R.op("act", lambda e: e.activation(sg[:, :], y[:, :], AF.Sigmoid), reads=[y], writes=[sg])
                if gi == 2:
                    R.op("dve", lambda e: e.tensor_tensor(vb[:, :], y[:, :], sg[:, :], ALU.mult), reads=[y, sg], writes=[vb])
                else:
                    R.op("dve", lambda e: e.tensor_tensor(yc[:, :], y[:, :], sg[:, :], ALU.mult), reads=[y, sg], writes=[yc])
                    R.op("act", lambda e: e.activation(sq[:, :], yc[:, :], AF.Square), reads=[yc], writes=[sq])
                    R.op("pe", lambda e: e.matmul(ss[:, :], ones_bf, sq[:, :], start=True, stop=True), reads=[cbf, sq], writes=[ss])
                    R.op("act", lambda e: e.activation(rs[:, :], ss[:, :], AF.Sqrt, bias=epsc[:, 0:1], scale=1.0), reads=[ss, epsc], writes=[rs])
                    R.op("dve", lambda e: e.reciprocal(rs[:, :], rs[:, :]), reads=[rs], writes=[rs])
                    if gi == 0:
                        R.op("dve", lambda e: e.tensor_scalar(rs[:, :], rs[:, :], 128.0 ** -0.5, None, ALU.mult), reads=[rs], writes=[rs])
                    dst = qh if gi == 0 else kh
                    R.op("dve", lambda e, dst=dst: e.tensor_tensor(dst[:, :], yc[:, :], rs[:, :], ALU.mult), reads=[yc, rs], writes=[dst])
            R.op("act", lambda e: e.activation(sg[:, :], xb[3][:, 3:515], AF.Sigmoid), reads=[xb[3]], writes=[sg])
            R.op("dve", lambda e: e.tensor_tensor(zs[:, :], xb[3][:, 3:515], sg[:, :], ALU.mult), reads=[xb[3], sg], writes=[zs])
            for ci in range(8):
                n = t * 8 + ci
                c0 = ci * 64
                R.op("act", lambda e, n=n: e.activation(gB[:, :], cst[0:64, C_ONES:C_ONES + 128], AF.Copy, scale=gall[:, n:n + 1]), reads=[cst, gall], writes=[gB])
                R.op("act", lambda e, n=n: e.activation(bB[:, :], cst[0:64, C_ONES:C_ONES + 128], AF.Copy, scale=ball[:, n:n + 1]), reads=[cst, ball], writes=[bB])
                R.op("pe", lambda e: e.matmul(A[:, 0:64], gB[:, :], UT64, start=True, stop=True), reads=[gB, cst], writes=[A])
                R.op("pe", lambda e: e.matmul(A[:, 64:128], bB[:, :], ID64, start=True, stop=True), reads=[bB, cst], pw=[A])
                R.op("pe", lambda e, n=n: e.matmul(A[0:64, 128:129], UT64, gall[:, n:n + 1], start=True, stop=True), reads=[gall, cst], pw=[A])
                R.op("dve", lambda e: e.tensor_copy(gamRow[:, :], A[:, 0:64]), reads=[A], writes=[gamRow])
                R.op("dve", lambda e: e.tensor_copy(betaRow[:, :], A[:, 64:128]), reads=[A], writes=[betaRow])
                R.op("dve", lambda e: e.tensor_copy(gamCol[:, :], A[0:64, 128:129]), reads=[A], writes=[gamCol])
                R.op("act", lambda e: e.activation(aRow[:, :], gamRow[:, :], AF.Exp), reads=[gamRow], writes=[aRow])
                R.op("act", lambda e: e.activation(acol[:, :], gamCol[:, :], AF.Exp), reads=[gamCol], writes=[acol])
                R.op("act", lambda e: e.activation(rcol[:, :], gamCol[:, :], AF.Exp, bias=gamRow[0:64, 63:64], scale=-1.0), reads=[gamCol, gamRow], writes=[rcol])
                R.op("dve", lambda e, n=n: e.tensor_tensor(bacol[:, :], acol[:, :], ball[:, n:n + 1], ALU.mult), reads=[acol, ball], writes=[bacol])
                R.op("dve", lambda e: e.tensor_scalar(t0d[:, :], gamRow[0:64, :], gamCol[:, 0:1], None, ALU.subtract), reads=[gamRow, gamCol], writes=[t0d])
                R.op("dve", lambda e: e.tensor_tensor(d2[:, 0:64], t0d[:, :], cst[0:64, C_NM:C_NM + 64], ALU.add), reads=[t0d, cst], writes=[d2])
                R.op("dve", lambda e: e.tensor_tensor(d2[:, 64:128], t0d[:, :], cst[0:64, C_NM + 64:C_NM + 128], ALU.add), reads=[t0d, cst], pw=[d2])
                R.op("act", lambda e: e.activation(d2[:, :], d2[:, :], AF.Exp), reads=[d2], writes=[d2])
                R.op("dve", lambda e, c0=c0: e.tensor_tensor(kbq[:, 0:64], kh[:, c0:c0 + 64], betaRow[:, :], ALU.mult), reads=[kh, betaRow], writes=[kbq])
                R.op("act", lambda e, c0=c0: e.copy(kbq[:, 64:128], qh[:, c0:c0 + 64]), reads=[qh], pw=[kbq])
                R.op("pe", lambda e, c0=c0: e.matmul(B[:, :], kh[:, c0:c0 + 64], kbq[:, :], start=True, stop=True), reads=[kh, kbq], writes=[B])
                R.op("dve", lambda e: e.tensor_tensor(NQ[:, :], B[:, :], d2[:, :], ALU.mult), reads=[B, d2], writes=[NQ])
                R.op("pe", lambda e: e.transpose(PB1[:, :], NQ[:, 0:64], I64), reads=[NQ, cbf], writes=[PB1])
                R.op("dve", lambda e: e.tensor_copy(PP[0][:, 64:128], PB1[:, :]), reads=[PB1], writes=[PP[0]])
                R.op("act", lambda e: e.copy(PP[0][:, 0:64], NQ[:, 0:64]), reads=[NQ], pw=[PP[0]])
                R.op("dve", lambda e: e.tensor_tensor(Y[0][:, :], I64, NQ[:, 0:64], ALU.subtract), reads=[cbf, NQ], writes=[Y[0]])
                pk = 0; yk = 0
                for lvl in range(5):
                    R.op("pe", lambda e, pk=pk: e.matmul(C[:, 0:64], PP[pk][:, 64:128], PP[pk][:, 0:64], start=True, stop=True), reads=[PP[pk]], writes=[C])
                    R.op("pe", lambda e, pk=pk: e.matmul(C[:, 64:128], PP[pk][:, 0:64], PP[pk][:, 64:128], start=True, stop=True), reads=[PP[pk]], pw=[C])
                    R.op("dve", lambda e, pk=pk: e.tensor_copy(PP[1 - pk][:, :], C[:, :]), reads=[C], writes=[PP[1 - pk]])
                    pk = 1 - pk
                    R.op("pe", lambda e, pk=pk, yk=yk: e.matmul(Dp[:, :], PP[pk][:, 64:128], Y[yk][:, :], start=True, stop=True), reads=[PP[pk], Y[yk]], writes=[Dp])
                    R.op("dve", lambda e, yk=yk: e.tensor_tensor(Y[1 - yk][:, :], Dp[:, :], Y[yk][:, :], ALU.add), reads=[Dp, Y[yk]], writes=[Y[1 - yk]])
                    yk = 1 - yk
                Tt = Y[yk]
                R.op("pe", lambda e, c0=c0: e.transpose(PB2[:, 0:128], kh[:, c0:c0 + 64], ident_bf), reads=[kh, cbf], writes=[PB2])
                R.op("pe", lambda e, c0=c0: e.transpose(PB2[:, 128:256], vb[:, c0:c0 + 64], ident_bf), reads=[vb, cbf], pw=[PB2])
                R.op("dve", lambda e: e.tensor_scalar(RHSw[:, :], PB2[:, 0:128], bacol[:, 0:1], None, ALU.mult), reads=[PB2, bacol], writes=[RHSw])
                R.op("dve", lambda e: e.tensor_scalar(kdec[:, :], PB2[:, 0:128], rcol[:, 0:1], None, ALU.mult), reads=[PB2, rcol], writes=[kdec])
                R.op("dve", lambda e, n=n: e.tensor_scalar(RHSv[:, :], PB2[:, 128:256], ball[:, n:n + 1], None, ALU.mult), reads=[PB2, ball], writes=[RHSv])
                R.op("pe", lambda e, Tt=Tt: e.matmul(E1[:, :], RHSw[:, :], Tt[:, :], start=True, stop=True), reads=[RHSw, Tt], writes=[E1])
                R.op("dve", lambda e: e.tensor_scalar(nwT[:, :], E1[:, :], -1.0, None, ALU.mult), reads=[E1], writes=[nwT])
                R.op("pe", lambda e, Tt=Tt: e.matmul(F1[:, :], Tt[:, :], RHSv[:, :], start=True, stop=True), reads=[RHSv, Tt], writes=[F1])
                R.op("dve", lambda e: e.tensor_copy(uc[:, :], F1[:, :]), reads=[F1], writes=[uc])
                R.op("dve", lambda e, c0=c0: e.tensor_tensor(qdT[:, :], qh[:, c0:c0 + 64], aRow[:, :], ALU.mult), reads=[qh, aRow], writes=[qdT])
                R.op("pe", lambda e: e.matmul(F2[:, :], nwT[:, :], Sbf[:, :], start=True, stop=True), reads=[nwT, Sbf], writes=[F2])
                R.op("dve", lambda e: e.tensor_tensor(u[:, :], F2[:, :], uc[:, :], ALU.add), reads=[F2, uc], writes=[u])
                R.op("pe", lambda e: e.matmul(E2[:, :], Sbf[:, :], qdT[:, :], start=True, stop=False), reads=[Sbf, qdT], writes=[E2])
                R.op("pe", lambda e: e.matmul(E2[:, :], u[:, :], NQ[:, 64:128], start=False, stop=True), reads=[u, NQ], pw=[E2])
                R.op("dve", lambda e, c0=c0: e.tensor_copy(oraw[:, c0:c0 + 64], E2[:, :]), reads=[E2], writes=[oraw] if ci == 0 else (), pw=() if ci == 0 else [oraw])
                R.op("pe", lambda e: e.matmul(E3[:, :], kdec[:, :], u[:, :], start=True, stop=True), reads=[kdec, u], writes=[E3])
                R.op("act", lambda e: e.activation(St[:, :], S32[:, :], AF.Copy, scale=aRow[:, 63:64]), reads=[S32, aRow], writes=[St])
                R.op("dve", lambda e: e.tensor_tensor(S32[:, :], St[:, :], E3[:, :], ALU.add), reads=[St, E3], writes=[S32])
                R.op("act", lambda e: e.copy(Sbf[:, :], S32[:, :]), reads=[S32], writes=[Sbf])
            R.op("act", lambda e: e.activation(sq[:, :], oraw[:, :], AF.Square), reads=[oraw], writes=[sq])
            R.op("pe", lambda e: e.matmul(ss[:, :], ones_bf, sq[:, :], start=True, stop=True), reads=[cbf, sq], writes=[ss])
            R.op("act", lambda e: e.activation(rs[:, :], ss[:, :], AF.Sqrt, bias=epsc[:, 0:1], scale=1.0 / 128), reads=[ss, epsc], writes=[rs])
            R.op("dve", lambda e: e.reciprocal(rs[:, :], rs[:, :]), reads=[rs], writes=[rs])
            R.op("dve", lambda e: e.tensor_tensor(on[:, :], oraw[:, :], rs[:, :], ALU.mult), reads=[oraw, rs], writes=[on])
            R.op("dve", lambda e: e.tensor_scalar(on[:, :], on[:, :], prm[:, PR_GDNG:PR_GDNG + 1], None, ALU.mult), reads=[on, prm], writes=[on])
            R.op("dve", lambda e: e.tensor_tensor(ob[:, :], on[:, :], zs[:, :], ALU.mult), reads=[on, zs], writes=[ob])
            omine_write(hi * 128, c0t, ob, R)

    with P.scope():
        recs = []
        for hi in range(2):
            R = Rec()
            head(R, hi)
            recs.append(R)
        replay(P, recs)


_CACHE = {}


def prep_inputs(I, S, PH=(0, 1, 2, 3, 4, 5)):
    w_in = I["w_in"][0]
    TQ = S // 4
    cst = make_consts()
    gates0 = 8216
    big = {}
    if 5 in PH:
        big["wg"] = np.concatenate([relay(w_in[:, gates0 + br * D:gates0 + (br + 1) * D]) for br in range(3)], axis=0)
        big["wup"] = np.concatenate([relay(I[n][0]) for n in ("w_up_gdn", "w_up_fox", "w_up_mem")], axis=0)
        big["wout"] = relay(I["w_out"][0]); big["wff1"] = relay(I["w_ff1"][0]); big["wff2"] = relay(I["w_ff2"][0])
    maps = []
    for core in range(8):
        b, g = core // 4, core % 4
        hs = [2 * g, 2 * g + 1]
        xTb = np.ascontiguousarray(I["x"][b, :S].T)
        cols = []
        for base in (0, 1024, 2048, 3072, 4112, 5136):
            for h in hs:
                cols.append(np.arange(base + h * 128, base + (h + 1) * 128))
        cols.append(np.arange(7192 + g * 256, 7192 + (g + 1) * 256))
        w1 = relay(w_in[:, np.concatenate(cols)])
        scol = np.concatenate([np.arange(6160 + h * 128, 6160 + (h + 1) * 128) for h in hs] + [np.array([7184 + h for h in hs]),
                              np.array([4096 + h for h in hs]), np.array([4104 + h for h in hs])])
        w1s = relay_tok(w_in[:, scol])
        wmkv = I["w_mem_kv"][0]
        wmk = relay(wmkv[:, g * 256:(g + 1) * 256])
        wmv = relay_tok(wmkv[:, 1024 + g * 256:1024 + (g + 1) * 256])
        prm = np.zeros((128, NPRM), np.float32)
        prm[:, PR_GMIX:PR_GMIX + 16] = I["g_mix"][0].reshape(16, 128).T
        prm[:, PR_GMLP:PR_GMLP + 16] = I["g_mlp"][0].reshape(16, 128).T
        prm[:, PR_GMEM:PR_GMEM + 16] = I["g_mem"][0].reshape(16, 128).T
        cw = I["conv_w"][0]
        for gi, base in enumerate((0, 1024, 2048)):
            for hi, h in enumerate(hs):
                prm[:, PR_CONV + (gi * 2 + hi) * 4:PR_CONV + (gi * 2 + hi) * 4 + 4] = cw[:, base + h * 128:base + (h + 1) * 128].T
        prm[:, PR_GDNG] = I["gdn_norm_g"][0]
        prm[:, PR_FQN] = I["fox_q_norm"][0]; prm[:, PR_FKN] = I["fox_k_norm"][0]
        prm[:, PR_MQN:PR_MQN + 2] = I["mem_q_norm"][0].reshape(2, 128).T
        prm[:, PR_MKN:PR_MKN + 2] = I["mem_k_norm"][0].reshape(2, 128).T
        for hi, h in enumerate(hs):
            prm[:, PR_ALOG + hi] = I["a_log"][0, h]; prm[:, PR_DTB + hi] = I["dt_bias"][0, h]; prm[:, PR_BF + hi] = I["fox_b_f"][0, h]
        maps.append({
            "xT": xTb, "xq": np.ascontiguousarray(xTb[:, g * TQ:(g + 1) * TQ]), "memT": np.ascontiguousarray(I["mem"][b].T),
            "prm": prm, "cst": cst, "w1": w1, "w1s": w1s, "wmk": wmk, "wmv": wmv, **big,
        })
    return maps


def run(I, S, dbg=False, PH=(0, 1, 2, 3, 4, 5)):
    key = (S, dbg, PH)
    if key not in _CACHE:
        _CACHE[key] = build(S, dbg, PH)
    nc = _CACHE[key]
    maps = prep_inputs(I, S, PH)
    res = run_bass_kernel_spmd(nc, maps, core_ids=list(range(8)))
    return res


def kernel(**inputs):
    I = {k: np.asarray(v) for k, v in inputs.items()}
    S = I["x"].shape[1]
    res = run(I, S)
    TQ = S // 4
    out = np.empty((2, S, D), np.float32)
    for core in range(8):
        b, g = core // 4, core % 4
        out[b, g * TQ:(g + 1) * TQ, :] = res.results[core]["outT"].T
    return out
```

```python
from contextlib import ExitStack, contextmanager
import types
import os
import numpy as np
import concourse.bass as bass
import concourse.mybir as mybir
from concourse.bass_utils import run_bass_kernel_spmd

F32 = mybir.dt.float32
BF16 = mybir.dt.bfloat16
AF = mybir.ActivationFunctionType
ALU = mybir.AluOpType

ENGS = ("pe", "act", "dve", "pool", "sp")
SAME_ENG_SYNC = bool(int(os.environ.get("KSES", "1")))
NDMA_SEM = 10
EPS = 1e-6
D = 2048
DFF = 8192
NEG = -30000.0
ARENA_F32 = 47 * 1024


class T:
    __slots__ = ("t", "name", "lws", "rd", "excl")

    def __init__(self, t, name, fence=None):
        self.t = t
        self.name = name
        self.excl = False
        self.lws = {}
        self.rd = dict(fence) if fence else {}

    def __getitem__(self, idx):
        return self.t[idx]


class Ev:
    __slots__ = ("kind", "key", "idx", "op")

    def __init__(self, kind, key, idx, op=None):
        self.kind = kind; self.key = key; self.idx = idx; self.op = op


class Op:
    __slots__ = ("eng", "fn", "waits", "mark", "cnt")

    def __init__(self, eng, fn):
        self.eng = eng; self.fn = fn; self.waits = []; self.mark = False; self.cnt = 0


def _freeze(fn):
    if fn is None or fn.__closure__ is None:
        return fn
    cells = []
    for c in fn.__closure__:
        try:
            cells.append(types.CellType(c.cell_contents))
        except ValueError:
            cells.append(c)
    return types.FunctionType(fn.__code__, fn.__globals__, fn.__name__, fn.__defaults__, tuple(cells))


def _merge(d, ev):
    o = d.get(ev.key)
    if o is None or o.idx < ev.idx:
        d[ev.key] = ev


class Prog:
    def __init__(self, nc):
        self.nc = nc
        self.es = ExitStack()
        self.ops = {e: [] for e in ENGS}
        self.seen = {e: {} for e in ENGS}
        self.esem = {e: self.es.enter_context(nc.semaphore("s_" + e)) for e in ENGS}
        self.dsem = {}
        self.dcount = {}
        self.dnext = {}
        for q in ("sp", "pool", "act"):
            self.dsem[q] = [self.es.enter_context(nc.semaphore(f"d_{q}{i}")) for i in range(NDMA_SEM)]
            self.dcount[q] = [0] * NDMA_SEM
            self.dnext[q] = 0
        self.csem = {}
        self.arena = None
        self.sp = 0
        self.sp_max = 0
        self.last_ev = {}
        self.ntile = 0
        self.fence = {}
        self.scopes = []

    def sbuf(self, shape, dt, name=None):
        self.ntile += 1
        name = (name or "t") + f"_{self.ntile}"
        if self.arena is None:
            self.arena = self.es.enter_context(self.nc.sbuf_tensor("arena", [128, ARENA_F32], F32))
        shape = list(shape)
        esz = 4 if dt == F32 else 2
        nfree = 1
        for d in shape[1:]:
            nfree *= d
        nbytes = (nfree * esz + 31) // 32 * 32
        off = self.sp
        self.sp += nbytes
        assert self.sp <= ARENA_F32 * 4, f"SBUF arena overflow: {self.sp} > {ARENA_F32 * 4} ({name})"
        self.sp_max = max(self.sp_max, self.sp)
        v = self.arena[0:shape[0], off // 4:(off + nbytes) // 4]
        if dt != F32:
            v = v.bitcast(dt)
        v = v[:, 0:nfree]
        if len(shape) == 3:
            v = v.rearrange("p (a b) -> p a b", b=shape[2])
        t = T(v, name, self.fence)
        if self.scopes:
            self.scopes[-1][1].append(t)
        return t

    def psum(self, shape, dt, name=None):
        self.ntile += 1
        name = (name or "p") + f"_{self.ntile}"
        t = T(self.es.enter_context(self.nc.psum_tensor(name, list(shape), dt)), name)
        t.excl = True
        return t

    def dram(self, name, shape, dt, kind="Internal"):
        return T(self.nc.dram_tensor(name, list(shape), dt, kind=kind), name)

    @contextmanager
    def scope(self):
        tiles = []
        self.scopes.append((None, tiles))
        sp0 = self.sp
        try:
            yield
        finally:
            self.scopes.pop()
            f = dict(self.fence)
            for t in tiles:
                for ev in t.lws.values():
                    _merge(f, ev)
                for ev in t.rd.values():
                    _merge(f, ev)
            self.fence = f
            self.sp = sp0

    def _need(self, eng, ev, op):
        if ev.kind == "e":
            if ev.key[1] == eng and (eng == "pe" or not SAME_ENG_SYNC):
                return
            if self.seen[eng].get(ev.key, -1) >= ev.idx:
                return
            self.seen[eng][ev.key] = ev.idx
            ev.op.mark = True
            op.waits.append(ev)
        else:
            if self.seen[eng].get(ev.key, -1) >= ev.idx:
                return
            self.seen[eng][ev.key] = ev.idx
            op.waits.append(ev)

    def _deps(self, eng, op, reads, writes, pw):
        reads = [getattr(t, "b", t) for t in reads]; writes = [getattr(t, "b", t) for t in writes]; pw = [getattr(t, "b", t) for t in pw]
        need = {}
        for t in reads:
            for ev in t.lws.values():
                _merge(need, ev)
            if t.excl:
                for ev in t.rd.values():
                    if ev.kind != "e" or ev.key[1] != eng:
                        _merge(need, ev)
        for t in writes:
            for ev in t.lws.values():
                _merge(need, ev)
            for ev in t.rd.values():
                _merge(need, ev)
        for t in pw:
            for ev in t.rd.values():
                _merge(need, ev)
        for ev in need.values():
            self._need(eng, ev, op)

    def _record(self, ev, reads, writes, pw):
        reads = [getattr(t, "b", t) for t in reads]; writes = [getattr(t, "b", t) for t in writes]; pw = [getattr(t, "b", t) for t in pw]
        for t in reads:
            _merge(t.rd, ev)
        for t in writes:
            t.lws = {ev.key: ev}
            t.rd = {}
        for t in pw:
            _merge(t.lws, ev)

    def _cut(self):
        self.nops = getattr(self, "nops", 0) + 1
        if os.environ.get("KLIST"):
            import inspect
            fr = inspect.stack()[2]
            print("OP", self.nops, inspect.stack()[1].function, fr.lineno, (fr.code_context or [""])[0].strip()[:110])
        return self.nops > int(os.environ.get("KCUT", "1000000000"))

    def op(self, eng, fn, reads=(), writes=(), pw=()):
        if self._cut():
            return
        o = Op(eng, _freeze(fn))
        self._deps(eng, o, reads, writes, pw)
        idx = len(self.ops[eng])
        self.ops[eng].append(o)
        ev = Ev("e", ("e", eng), idx, o)
        self.last_ev[eng] = ev
        self._record(ev, reads, writes, pw)
        return o

    def dma(self, q, out_ap, in_ap, reads=(), writes=(), pw=(), **kw):
        if self._cut():
            return
        o = Op(q, None)
        self._deps(q, o, reads, writes, pw)
        si = self.dnext[q]
        self.dnext[q] = (si + 1) % NDMA_SEM
        prev = self.dcount[q][si]
        key = ("d", q, si)
        if prev > 0:
            self._need(q, Ev("d", key, prev), o)
        val = prev + 1
        self.dcount[q][si] = val
        sem = self.dsem[q][si]
        o.fn = lambda e: e.dma_start(out=out_ap, in_=in_ap, **kw).then_inc(sem, 16)
        self.ops[q].append(o)
        self._record(Ev("d", key, val), reads, writes, pw)

    def custom(self, q, fn, reads=(), writes=(), pw=()):
        o = Op(q, None)
        self._deps(q, o, reads, writes, pw)
        sem = self.es.enter_context(self.nc.semaphore(f"c_{len(self.csem)}"))
        key = ("c", len(self.csem))
        self.csem[key] = sem
        fn = _freeze(fn)
        o.fn = lambda e: fn(e, sem)
        self.ops[q].append(o)
        self._record(Ev("d", key, 1), reads, writes, pw)

    def wait_all(self, eng, tiles):
        o = Op(eng, None)
        need = {}
        for t in tiles:
            for ev in t.lws.values():
                _merge(need, ev)
            for ev in t.rd.values():
                _merge(need, ev)
        for ev in self.last_ev.values():
            _merge(need, ev)
        for q in self.dcount:
            for si, cnt in enumerate(self.dcount[q]):
                if cnt > 0:
                    _merge(need, Ev("d", ("d", q, si), cnt))
        for ev in need.values():
            self._need(eng, ev, o)
        self.ops[eng].append(o)

    def emit(self):
        nc = self.nc
        for e in ENGS:
            c = 0
            for o in self.ops[e]:
                if o.mark:
                    c += 1
                    o.cnt = c
        P = self

        def run(eng_name, e):
            for o in P.ops[eng_name]:
                for ev in o.waits:
                    if ev.kind == "e":
                        e.wait_ge(P.esem[ev.key[1]], ev.op.cnt)
                    elif ev.key[0] == "c":
                        e.wait_ge(P.csem[ev.key], 1)
                    else:
                        e.wait_ge(P.dsem[ev.key[1]][ev.key[2]], 16 * ev.idx)
                if o.fn is None:
                    continue
                r = o.fn(e)
                if o.mark:
                    r.then_inc(P.esem[eng_name], 1)

        with nc.Block() as block:
            @block.tensor
            def _(e):
                run("pe", e)

            @block.scalar
            def _(e):
                run("act", e)

            @block.vector
            def _(e):
                run("dve", e)

            @block.gpsimd
            def _(e):
                run("pool", e)

            @block.sync
            def _(e):
                run("sp", e)
        self.es.close()


class Rot:
    def __init__(self, P, n, shape, dt, name):
        self.b = [P.sbuf(shape, dt, name) for _ in range(n)]
        self.i = 0

    def get(self):
        t = self.b[self.i]
        self.i = (self.i + 1) % len(self.b)
        return t


PR_GMIX, PR_GMLP, PR_GMEM = 0, 16, 32
PR_CONV = 48
PR_GDNG = 72
PR_FQN, PR_FKN = 73, 74
PR_MQN, PR_MKN = 75, 77
PR_ALOG, PR_DTB, PR_BF = 79, 81, 83
NPRM = 85

C_ID, C_TRI, C_TRIS, C_ONES, C_SEL, C_NM = 0, 128, 256, 384, 512, 640
NCST = 768


def make_consts():
    c = np.zeros((128, NCST), np.float32)
    i = np.arange(128)
    c[:, C_ID:C_ID + 128] = np.eye(128)
    c[:, C_TRI:C_TRI + 128] = (i[:, None] <= i[None, :])
    c[:, C_TRIS:C_TRIS + 128] = (i[:, None] < i[None, :])
    c[:, C_ONES:C_ONES + 128] = 1.0
    c[64, C_SEL:C_SEL + 128] = 1.0
    j = np.arange(64)
    nm = np.zeros((128, 128), np.float32)
    nm[:64, 0:64] = np.where(j[None, :] > j[:, None], 0.0, NEG)
    nm[:64, 64:128] = np.where(j[None, :] >= j[:, None], 0.0, NEG)
    c[:, C_NM:C_NM + 128] = nm
    return c


def relay(W):
    K, N = W.shape
    return np.ascontiguousarray(W.reshape(K // 128, 128, N // 128, 128).transpose(2, 1, 0, 3)).reshape(N // 128, 128, K)


def relay_tok(W):
    K, n = W.shape
    return np.ascontiguousarray(W.reshape(K // 128, 128, n).transpose(1, 0, 2)).reshape(128, (K // 128) * n)


def build(S, dbg=False, PH=(0, 1, 2, 3, 4, 5)):
    nc = bass.Bass("TRN2", target_bir_lowering=False)
    NTILE = S // 512
    J = S // 128
    NCH = S // 64
    TQ = S // 4
    NT4 = min(512, TQ)
    NSUB = TQ // NT4

    def din(name, shape, dt=F32):
        return T(nc.dram_tensor(name, list(shape), dt, kind="ExternalInput"), name)

    xT = din("xT", [D, S]); xq = din("xq", [D, TQ]); memT = din("memT", [D, 256])
    prm_d = din("prm", [128, NPRM]); cst_d = din("cst", [128, NCST])
    w1 = din("w1", [14, 128, D]); w1s = din("w1s", [128, 16 * 262])
    wmk = din("wmk", [2, 128, D]); wmv = din("wmv", [128, 16 * 256])
    if 5 in PH:
        wg = din("wg", [48, 128, D]); wup = din("wup", [48, 128, 1024])
        wout = din("wout", [16, 128, D]); wff1 = din("wff1", [64, 128, D]); wff2 = din("wff2", [16, 128, DFF])
    outT = T(nc.dram_tensor("outT", [D, TQ], F32, kind="ExternalOutput"), "outT")

    P = Prog(nc)
    skind = "ExternalOutput" if dbg else "Internal"
    graw = [P.dram(f"graw{i}", [128, S], F32, skind) for i in range(8)]
    foxq = [P.dram(f"foxq{i}", [128, S], BF16, skind) for i in range(2)]
    foxk = [P.dram(f"foxk{i}", [128, S], BF16, skind) for i in range(2)]
    foxv = P.dram("foxv", [S, 256], BF16, skind)
    smallT = P.dram("smallT", [6, S], F32, skind)
    NCT = S // NT4
    omine = P.dram("omine_i", [NCT, 768, NT4], BF16)
    gath = P.dram("gath", [NCT, 4 * 768, NT4], BF16)

    def omine_write(row0, t0, ob, Dq=None):
        Dq = Dq or P
        for o in range(0, 512, NT4):
            ct = (t0 + o) // NT4
            Dq.dma("sp", omine[ct, row0:row0 + 128, :], ob[:, o:o + NT4], reads=[ob], pw=[omine])

    cst = P.sbuf([128, NCST], F32, "cst")
    prm = P.sbuf([128, NPRM], F32, "prm")
    cbf = P.sbuf([128, 512], BF16, "cbf")
    epsc = P.sbuf([128, 2], F32, "epsc")
    kTm = P.sbuf([128, 2, 256], BF16, "kTm")
    vm = P.sbuf([128, 2, 256], BF16, "vm")
    PS = [P.psum([128, 512], F32, f"ps{i}") for i in range(6)]
    PSB = [P.psum([128, 1024], BF16, f"psb{i}") for i in range(2)]

    P.dma("sp", cst[:, :], cst_d[:, :], writes=[cst])
    P.dma("sp", prm[:, :], prm_d[:, :], writes=[prm])
    P.op("dve", lambda e: e.tensor_copy(cbf[:, :], cst[:, 0:512]), reads=[cst], writes=[cbf])
    P.op("pool", lambda e: e.memset(epsc[:, 0:1], EPS), writes=[epsc])
    P.op("pool", lambda e: e.memset(epsc[:, 1:2], 1.0), pw=[epsc])
    ident_bf = cbf[:, 0:128]; tri_bf = cbf[:, 128:256]; ones_bf = cbf[:, 384:512]

    def rstd_from_ss(ss_ps, n, inv_d, out_t, np_=128):
        P.op("act", lambda e: e.activation(out_t[0:np_, 0:n], ss_ps[0:np_, 0:n], AF.Sqrt, bias=epsc[0:np_, 0:1], scale=inv_d),
             reads=[ss_ps, epsc], writes=[out_t])
        P.op("dve", lambda e: e.reciprocal(out_t[0:np_, 0:n], out_t[0:np_, 0:n]), reads=[out_t], writes=[out_t])

    tmp2 = Rot(P, 3, [128, 512], F32, "tmp2")

    def stt2(out_t, out_ap, in_t, in_ap, gc, rs_tile, rs_ap, n, first):
        tm = tmp2.get()
        P.op("act", lambda e: e.activation(tm[:, 0:n], in_ap, AF.Copy, scale=prm[:, gc:gc + 1]), reads=[in_t, prm], writes=[tm])
        P.op("dve", lambda e: e.tensor_tensor(out_ap, tm[:, 0:n], rs_ap, ALU.mult), reads=[tm, rs_tile],
             writes=[out_t] if first else (), pw=() if first else [out_t])

    def rmsnorm_tile(xt, n, gcol, hT, sqrot, rs_t):
        ss = PS[0]
        for c in range(16):
            sq = sqrot.get()
            P.op("act", lambda e, c=c, sq=sq: e.activation(sq[:, 0:n], xt[:, c, 0:n], AF.Square), reads=[xt], writes=[sq])
            P.op("pe", lambda e, c=c, sq=sq: e.matmul(ss[:, 0:n], ones_bf, sq[:, 0:n], start=(c == 0), stop=(c == 15)),
                 reads=[cbf, sq], writes=[ss] if c == 0 else (), pw=() if c == 0 else [ss])
        rstd_from_ss(ss, n, 1.0 / D, rs_t)
        for c in range(16):
            stt2(hT, hT[:, c, 0:n], xt, xt[:, c, 0:n], gcol + c, rs_t, rs_t[:, 0:n], n, c == 0)

    def wload(dst, src2d, n, stg=None):
        for o in range(0, n, 4096):
            m = min(4096, n - o)
            first = (o == 0)
            P.dma("pool", dst[:, o:o + m], src2d[:, o:o + m], writes=[dst] if first else (), pw=() if first else [dst])

    def phase(n):
        if n in PH:
            with P.scope():
                yield

    for _ in phase(0):
        mt = P.sbuf([128, 16, 256], F32, "mt")
        mh = P.sbuf([128, 16, 256], BF16, "mh")
        sqrot = Rot(P, 3, [128, 512], BF16, "sq")
        rs_t = P.sbuf([128, 512], F32, "rs")
        wk_b = [P.sbuf([128, 2048], BF16, "wk") for _ in range(2)]
        wv_b = P.sbuf([128, 4096], BF16, "wv")
        stg = Rot(P, 2, [128, 2048], F32, "stg")
        kraw = P.sbuf([128, 2, 256], F32, "kraw")
        P.dma("sp", mt[:, :, :], memT[:, :].rearrange("(c p) n -> p c n", p=128), writes=[mt])
        for dc in range(2):
            wload(wk_b[dc], wmk[dc], 2048, stg)
        wload(wv_b, wmv, 4096, stg)
        rmsnorm_tile(mt, 256, PR_GMEM, mh, sqrot, rs_t)
        ss2 = PS[3]
        for dc in range(2):
            ps = PS[1 + dc]
            for k in range(16):
                P.op("pe", lambda e, k=k, dc=dc, ps=ps: e.matmul(ps[:, 0:256], wk_b[dc][:, k * 128:(k + 1) * 128], mh[:, k, :], start=(k == 0), stop=(k == 15)),
                     reads=[wk_b[dc], mh], writes=[ps] if k == 0 else (), pw=() if k == 0 else [ps])
            P.op("dve", lambda e, dc=dc, ps=ps: e.tensor_copy(kraw[:, dc, :], ps[:, 0:256]), reads=[ps], writes=[kraw] if dc == 0 else (), pw=() if dc == 0 else [kraw])
            sq = sqrot.get()
            P.op("act", lambda e, sq=sq, dc=dc: e.activation(sq[:, 0:256], kraw[:, dc, :], AF.Square), reads=[kraw], writes=[sq])
            P.op("pe", lambda e, sq=sq, dc=dc: e.matmul(ss2[:, 0:256], ones_bf, sq[:, 0:256], start=(dc == 0), stop=(dc == 1)),
                 reads=[cbf, sq], writes=[ss2] if dc == 0 else (), pw=() if dc == 0 else [ss2])
        rstd_from_ss(ss2, 256, 1.0 / 256, rs_t)
        for dc in range(2):
            stt2(kTm, kTm[:, dc, :], kraw, kraw[:, dc, :], PR_MKN + dc, rs_t, rs_t[:, 0:256], 256, dc == 0)
        for kc in range(2):
            ps = PS[4 + kc]
            for k in range(16):
                P.op("pe", lambda e, k=k, kc=kc, ps=ps: e.matmul(ps[:, 0:256], mh[:, k, kc * 128:(kc + 1) * 128], wv_b[:, k * 256:(k + 1) * 256], start=(k == 0), stop=(k == 15)),
                     reads=[wv_b, mh], writes=[ps] if k == 0 else (), pw=() if k == 0 else [ps])
            P.op("act", lambda e, kc=kc, ps=ps: e.copy(vm[:, kc, :], ps[:, 0:256]), reads=[ps], writes=[vm] if kc == 0 else (), pw=() if kc == 0 else [vm])

    for _ in phase(1):
        w1b = [P.sbuf([128, 2048], BF16, "w1b") for _ in range(14)]
        w1sb = P.sbuf([128, 16 * 262], BF16, "w1sb")
        with P.scope():
            stg = None
            for cc in range(14):
                wload(w1b[cc], w1[cc], 2048, stg)
            wload(w1sb, w1s, 16 * 262, stg)
        xrot = Rot(P, 2, [128, 16, 512], F32, "xt")
        hT = P.sbuf([128, 16, 512], BF16, "hT")
        sqrot = Rot(P, 3, [128, 512], BF16, "sq")
        rs_t = P.sbuf([128, 512], F32, "rs")
        rs2 = P.sbuf([128, 512], F32, "rs2")
        rawrot = Rot(P, 3, [128, 512], F32, "raw")
        mqraw = P.sbuf([128, 2, 512], F32, "mqraw")
        mqn = P.sbuf([128, 2, 512], BF16, "mqn")
        obrot = Rot(P, 3, [128, 512], BF16, "ob")
        pTm = [P.sbuf([128, 512], BF16, "pTm") for _ in range(2)]
        vst = Rot(P, 2, [128, 256], BF16, "vst")
        sst = Rot(P, 2, [128, 6], F32, "sst")
        rden = P.sbuf([128, 512], F32, "rden")
        for t in range(NTILE):
            t0 = t * 512
            xt = xrot.get()
            P.dma("sp", xt[:, :, :], xT[:, t0:t0 + 512].rearrange("(c p) n -> p c n", p=128), writes=[xt])
            rmsnorm_tile(xt, 512, PR_GMIX, hT, sqrot, rs_t)
            for cc in range(14):
                ps = PS[1 + (cc % 2)]
                for k in range(16):
                    P.op("pe", lambda e, k=k, cc=cc, ps=ps: e.matmul(ps[:, :], w1b[cc][:, k * 128:(k + 1) * 128], hT[:, k, :], start=(k == 0), stop=(k == 15)),
                         reads=[w1b[cc], hT], writes=[ps] if k == 0 else (), pw=() if k == 0 else [ps])
                if cc < 8:
                    raw = rawrot.get()
                    P.op("act", lambda e, raw=raw, ps=ps: e.copy(raw[:, :], ps[:, :]), reads=[ps], writes=[raw])
                    P.dma("sp", graw[cc][:, t0:t0 + 512], raw[:, :], reads=[raw], pw=[graw[cc]])
                elif cc < 12:
                    raw = rawrot.get()
                    sq = sqrot.get()
                    P.op("dve", lambda e, raw=raw, ps=ps: e.tensor_copy(raw[:, :], ps[:, :]), reads=[ps], writes=[raw])
                    P.op("act", lambda e, sq=sq, ps=ps: e.activation(sq[:, :], ps[:, :], AF.Square), reads=[ps], writes=[sq])
                    P.op("pe", lambda e, sq=sq: e.matmul(PS[3][:, :], ones_bf, sq[:, :], start=True, stop=True), reads=[cbf, sq], writes=[PS[3]])
                    rstd_from_ss(PS[3], 512, 1.0 / 128, rs2)
                    ob = obrot.get()
                    gc = PR_FQN if cc < 10 else PR_FKN
                    stt2(ob, ob[:, :], raw, raw[:, :], gc, rs2, rs2[:, :], 512, True)
                    dst = (foxq if cc < 10 else foxk)[cc % 2]
                    P.dma("sp", dst[:, t0:t0 + 512], ob[:, :], reads=[ob], pw=[dst])
                else:
                    dc = cc - 12
                    sq = sqrot.get()
                    P.op("dve", lambda e, dc=dc, ps=ps: e.tensor_copy(mqraw[:, dc, :], ps[:, :]), reads=[ps], writes=[mqraw] if dc == 0 else (), pw=() if dc == 0 else [mqraw])
                    P.op("act", lambda e, sq=sq, ps=ps: e.activation(sq[:, :], ps[:, :], AF.Square), reads=[ps], writes=[sq])
                    P.op("pe", lambda e, sq=sq, dc=dc: e.matmul(PS[3][:, :], ones_bf, sq[:, :], start=(dc == 0), stop=(dc == 1)),
                         reads=[cbf, sq], writes=[PS[3]] if dc == 0 else (), pw=() if dc == 0 else [PS[3]])
            rstd_from_ss(PS[3], 512, 1.0 / 256, rs2)
            for dc in range(2):
                stt2(mqn, mqn[:, dc, :], mqraw, mqraw[:, dc, :], PR_MQN + dc, rs2, rs2[:, :], 512, dc == 0)
            for kc in range(2):
                ps = PS[1 + kc]
                for dc in range(2):
                    P.op("pe", lambda e, kc=kc, dc=dc, ps=ps: e.matmul(ps[:, :], kTm[:, dc, kc * 128:(kc + 1) * 128], mqn[:, dc, :], start=(dc == 0), stop=(dc == 1)),
                         reads=[kTm, mqn], writes=[ps] if dc == 0 else (), pw=() if dc == 0 else [ps])
                P.op("act", lambda e, kc=kc, ps=ps: e.activation(pTm[kc][:, :], ps[:, :], AF.Exp, scale=1.0 / 16.0), reads=[ps], writes=[pTm[kc]])
            for kc in range(2):
                P.op("pe", lambda e, kc=kc: e.matmul(PS[3][:, :], ones_bf, pTm[kc][:, :], start=(kc == 0), stop=(kc == 1)),
                     reads=[cbf, pTm[kc]], writes=[PS[3]] if kc == 0 else (), pw=() if kc == 0 else [PS[3]])
            P.op("dve", lambda e: e.reciprocal(rden[:, :], PS[3][:, :]), reads=[PS[3]], writes=[rden])
            for dvc in range(2):
                ps = PS[4 + dvc]
                for kc in range(2):
                    P.op("pe", lambda e, kc=kc, dvc=dvc, ps=ps: e.matmul(ps[:, :], vm[:, kc, dvc * 128:(dvc + 1) * 128], pTm[kc][:, :], start=(kc == 0), stop=(kc == 1)),
                         reads=[vm, pTm[kc]], writes=[ps] if kc == 0 else (), pw=() if kc == 0 else [ps])
                ob = obrot.get()
                P.op("dve", lambda e, ob=ob, ps=ps: e.tensor_tensor(ob[:, :], ps[:, :], rden[:, :], ALU.mult), reads=[ps, rden], writes=[ob])
                omine_write(512 + dvc * 128, t0, ob)
            for blk in range(4):
                ps = PS[0]
                for k in range(16):
                    P.op("pe", lambda e, k=k, blk=blk: e.matmul(ps[:, 0:262], hT[:, k, blk * 128:(blk + 1) * 128], w1sb[:, k * 262:(k + 1) * 262], start=(k == 0), stop=(k == 15)),
                         reads=[w1sb, hT], writes=[ps] if k == 0 else (), pw=() if k == 0 else [ps])
                v_s = vst.get(); s_s = sst.get()
                P.op("act", lambda e, v_s=v_s: e.copy(v_s[:, :], ps[:, 0:256]), reads=[ps], writes=[v_s])
                P.op("dve", lambda e, s_s=s_s: e.tensor_copy(s_s[:, :], ps[:, 256:262]), reads=[ps], writes=[s_s])
                r0 = t0 + blk * 128
                P.dma("sp", foxv[r0:r0 + 128, :], v_s[:, :], reads=[v_s], pw=[foxv])
                P.dma("sp", smallT[:, r0:r0 + 128].rearrange("c p -> p c"), s_s[:, :], reads=[s_s], pw=[smallT], allow_slow_non_contiguous=True)

    for _ in phase(2):
        kT = P.sbuf([128, S], BF16, "kT")
        V = P.sbuf([128, J, 128], BF16, "V")
        lf = P.sbuf([128, J], F32, "lf")
        lfJ = P.sbuf([J, 128], F32, "lfJ")
        totc = P.sbuf([J, 1], F32, "totc")
        totB = P.sbuf([J, 128], F32, "totB")
        cT = P.sbuf([128, J], F32, "cT")
        negc = P.sbuf([128, J], F32, "negc")
        cmidB = P.sbuf([128, J], F32, "cmidB")
        nbf = P.sbuf([128, 1], F32, "nbf")
        crow = P.sbuf([1, S], BF16, "crow")
        qrot = Rot(P, 2, [128, 512], BF16, "qg")
        prot = Rot(P, 3, [128, 512], BF16, "pT")
        rden = P.sbuf([128, 512], F32, "rden")
        obrot = Rot(P, 2, [128, 512], BF16, "ob")
        strot = [PS[2], PS[3], PS[4]]
        sti = 0
        for h in range(2):
            P.dma("sp", kT[:, :], foxk[h][:, :], reads=[foxk[h]], writes=[kT])
            P.dma("sp", V[:, :, :], foxv[:, h * 128:(h + 1) * 128].rearrange("(j p) c -> p j c", p=128), reads=[foxv], writes=[V])
            P.dma("sp", lf[:, :], smallT[h, :].rearrange("(j p) -> p j", p=128), reads=[smallT], writes=[lf], allow_slow_non_contiguous=True)
            P.dma("sp", lfJ[:, :], smallT[h, :].rearrange("(j p) -> j p", p=128), reads=[smallT], writes=[lfJ])
            P.op("dve", lambda e, h=h: e.tensor_scalar(nbf[:, :], prm[:, PR_BF + h:PR_BF + h + 1], -1.0, None, ALU.mult), reads=[prm], writes=[nbf])
            for tl, npart in ((lf, 128), (lfJ, J)):
                P.op("act", lambda e, tl=tl, npart=npart: e.activation(tl[:, :], tl[:, :], AF.Exp, bias=nbf[0:npart, 0:1], scale=-1.0), reads=[tl, nbf], writes=[tl])
                P.op("act", lambda e, tl=tl, npart=npart: e.activation(tl[:, :], tl[:, :], AF.Ln, bias=epsc[0:npart, 1:2], scale=1.0), reads=[tl, epsc], writes=[tl])
            P.op("dve", lambda e: e.reduce_sum(totc[:, :], lfJ[:, :], mybir.AxisListType.X), reads=[lfJ], writes=[totc])
            P.op("dve", lambda e: e.tensor_scalar(totB[:, :], cst[0:J, C_ONES:C_ONES + 128], totc[:, 0:1], None, ALU.mult), reads=[cst, totc], writes=[totB])
            ps = PS[5]
            P.op("pe", lambda e: e.matmul(ps[:, 0:J], cst[:, C_TRI:C_TRI + 128], lf[:, :], start=True, stop=False), reads=[cst, lf], writes=[ps])
            P.op("pe", lambda e: e.matmul(ps[:, 0:J], totB[:, :], cst[0:J, C_TRIS:C_TRIS + J], start=False, stop=True), reads=[cst, totB], pw=[ps])
            P.op("dve", lambda e: e.tensor_copy(negc[:, :], ps[:, 0:J]), reads=[ps], writes=[negc])
            P.op("dve", lambda e: e.tensor_scalar(cT[:, :], ps[:, 0:J], -1.0, None, ALU.mult), reads=[ps], writes=[cT])
            P.op("pe", lambda e: e.matmul(ps[:, 0:J], cst[:, C_SEL:C_SEL + 128], cT[:, :], start=True, stop=True), reads=[cst, cT], writes=[ps])
            P.op("dve", lambda e: e.tensor_copy(cmidB[:, :], ps[:, 0:J]), reads=[ps], writes=[cmidB])
            for i in range(J):
                P.op("dve", lambda e, i=i: e.tensor_scalar(crow[0:1, i * 128:(i + 1) * 128], cst[0:1, C_ONES:C_ONES + 128], cmidB[0:1, i:i + 1], 128.0 ** 0.5, ALU.mult, ALU.mult),
                     reads=[cst, cmidB], writes=[crow] if i == 0 else (), pw=() if i == 0 else [crow])
            for G in range(NTILE):
                qg = qrot.get()
                P.dma("sp", qg[:, :], foxq[h][:, G * 512:(G + 1) * 512], reads=[foxq[h]], writes=[qg])
                o_ps = PS[0]; d_ps = PS[1]
                order = list(range(4 * G, 4 * G + 4)) + list(range(0, 4 * G))
                for n, j in enumerate(order):
                    qlo = max(0, j - 4 * G)
                    c0 = qlo * 128
                    st = strot[sti]; sti = (sti + 1) % 3
                    P.op("pe", lambda e, j=j, c0=c0, st=st, qg=qg: e.matmul(st[:, c0:512], kT[:, j * 128:(j + 1) * 128], qg[:, c0:512], start=True, stop=False),
                         reads=[kT, qg], writes=[st])
                    P.op("pe", lambda e, c0=c0, st=st, G=G: e.matmul(st[:, c0:512], ones_bf[0:1, :], crow[0:1, G * 512 + c0:(G + 1) * 512], start=False, stop=True),
                         reads=[cbf, crow], pw=[st])
                    pT = prot.get()
                    P.op("act", lambda e, pT=pT, st=st, j=j, c0=c0: e.activation(pT[:, c0:512], st[:, c0:512], AF.Exp, bias=negc[:, j:j + 1], scale=128.0 ** -0.5),
                         reads=[st, negc], writes=[pT])
                    if j >= 4 * G:
                        P.op("pool", lambda e, pT=pT, c0=c0: e.tensor_tensor(pT[:, c0:c0 + 128], pT[:, c0:c0 + 128], tri_bf, ALU.mult), reads=[pT, cbf], writes=[pT])
                    first = (n == 0); last = (n == len(order) - 1)
                    P.op("pe", lambda e, j=j, c0=c0, pT=pT, first=first, last=last: e.matmul(o_ps[:, c0:512], V[:, j, :], pT[:, c0:512], start=first, stop=last, skip_group_check=True),
                         reads=[V, pT], writes=[o_ps] if first else (), pw=() if first else [o_ps])
                    P.op("pe", lambda e, c0=c0, pT=pT, first=first, last=last: e.matmul(d_ps[:, c0:512], ones_bf, pT[:, c0:512], start=first, stop=last, skip_group_check=True),
                         reads=[cbf, pT], writes=[d_ps] if first else (), pw=() if first else [d_ps])
                P.op("dve", lambda e: e.reciprocal(rden[:, :], d_ps[:, :]), reads=[d_ps], writes=[rden])
                ob = obrot.get()
                P.op("dve", lambda e, ob=ob: e.tensor_tensor(ob[:, :], o_ps[:, :], rden[:, :], ALU.mult), reads=[o_ps, rden], writes=[ob])
                omine_write(256 + h * 128, G * 512, ob)

    ag_done = [0]

    def ag_tile(Rq, t0):
        for o in range(0, 512, NT4):
            ct = (t0 + o) // NT4
            first = (ag_done[0] == 0)
            ag_done[0] += 1
            Rq.custom("pool", lambda e, sem, ct=ct: e.collective_compute("AllGather", ALU.bypass, replica_groups=[[0, 1, 2, 3], [4, 5, 6, 7]],
                                                                        ins=[omine.t[ct].opt()], outs=[gath.t[ct].opt()]).then_inc(sem, 1),
                      reads=[omine], writes=[gath] if first else (), pw=() if first else [gath])

    if 3 in PH:
      gdn_phase(P, nc, S, NTILE, NCH, PS, PSB, cst, cbf, prm, epsc, graw, smallT, omine_write, ag_tile if 4 in PH else None)

    pid = nc.partition_id()
    qid = pid % 4
    for _ in phase(5):
        xt = P.sbuf([128, 16, NT4], F32, "xt4")
        sqrot = Rot(P, 3, [128, 512], BF16, "sq")
        rs_t = P.sbuf([128, 512], F32, "rs")
        wrot = Rot(P, 6, [128, 2048], BF16, "w4")
        wurot = Rot(P, 4, [128, 1024], BF16, "wu4")
        w2rot = Rot(P, 3, [128, 4096], BF16, "wf2")
        stg = None
        sgrot = Rot(P, 2, [128, 512], F32, "sg")
        tmrot = Rot(P, 2, [128, 512], F32, "tm")
        yacc = P.sbuf([128, 512], F32, "yacc")
        orot = Rot(P, 2, [128, 512], F32, "o4")
        for sub in range(NSUB):
            c0 = sub * NT4
            P.dma("sp", xt[:, :, :], xq[:, c0:c0 + NT4].rearrange("(c p) n -> p c n", p=128), writes=[xt])
            with P.scope():
                hT = P.sbuf([128, 16, NT4], BF16, "h4")
                oT = P.sbuf([128, 24, NT4], BF16, "o4T")
                yT = P.sbuf([128, 16, NT4], BF16, "y4")
                rmsnorm_tile(xt, NT4, PR_GMIX, hT, sqrot, rs_t)
                src = gath.t[bass.ds(qid * NSUB + sub, 1), :, :].rearrange("o (r p) n -> p (o r) n", p=128)
                P.dma("sp", oT[:, :, :], src, reads=[gath], writes=[oT])
                for c in range(16):
                    for br in range(3):
                        wgb = wrot.get(); wub = wurot.get()
                        wload(wgb, wg[br * 16 + c], 2048, stg)
                        wload(wub, wup[br * 16 + c], 1024, stg)
                        gps = PS[1 + (br % 2)]; ups = PS[3 + (br % 2)]
                        for k in range(16):
                            P.op("pe", lambda e, k=k, wgb=wgb, gps=gps: e.matmul(gps[:, 0:NT4], wgb[:, k * 128:(k + 1) * 128], hT[:, k, :], start=(k == 0), stop=(k == 15)),
                                 reads=[wgb, hT], writes=[gps] if k == 0 else (), pw=() if k == 0 else [gps])
                        for k in range(8):
                            rk = (k // 2) * 6 + br * 2 + (k % 2)
                            P.op("pe", lambda e, k=k, rk=rk, wub=wub, ups=ups: e.matmul(ups[:, 0:NT4], wub[:, k * 128:(k + 1) * 128], oT[:, rk, :], start=(k == 0), stop=(k == 7)),
                                 reads=[wub, oT], writes=[ups] if k == 0 else (), pw=() if k == 0 else [ups])
                        sg = sgrot.get()
                        P.op("act", lambda e, sg=sg, gps=gps: e.activation(sg[:, 0:NT4], gps[:, 0:NT4], AF.Sigmoid), reads=[gps], writes=[sg])
                        if br == 0:
                            P.op("dve", lambda e, sg=sg, ups=ups: e.tensor_tensor(yacc[:, 0:NT4], sg[:, 0:NT4], ups[:, 0:NT4], ALU.mult), reads=[sg, ups], writes=[yacc])
                        else:
                            tm = tmrot.get()
                            P.op("dve", lambda e, sg=sg, ups=ups, tm=tm: e.tensor_tensor(tm[:, 0:NT4], sg[:, 0:NT4], ups[:, 0:NT4], ALU.mult), reads=[sg, ups], writes=[tm])
                            if br == 1:
                                P.op("dve", lambda e, tm=tm: e.tensor_tensor(yacc[:, 0:NT4], yacc[:, 0:NT4], tm[:, 0:NT4], ALU.add), reads=[tm, yacc], writes=[yacc])
                            else:
                                P.op("dve", lambda e, tm=tm, c=c: e.tensor_tensor(yT[:, c, :], yacc[:, 0:NT4], tm[:, 0:NT4], ALU.add), reads=[tm, yacc],
                                     writes=[yT] if c == 0 else (), pw=() if c == 0 else [yT])
                for c in range(16):
                    wb = wrot.get()
                    wload(wb, wout[c], 2048, stg)
                    ps = PS[1 + (c % 2)]
                    for k in range(16):
                        P.op("pe", lambda e, k=k, wb=wb, ps=ps: e.matmul(ps[:, 0:NT4], wb[:, k * 128:(k + 1) * 128], yT[:, k, :], start=(k == 0), stop=(k == 15)),
                             reads=[wb, yT], writes=[ps] if k == 0 else (), pw=() if k == 0 else [ps])
                    P.op("dve", lambda e, c=c, ps=ps: e.tensor_tensor(xt[:, c, :], xt[:, c, :], ps[:, 0:NT4], ALU.add), reads=[ps, xt], pw=[xt])
            with P.scope():
                h2 = P.sbuf([128, 16, NT4], BF16, "h24")
                uT = P.sbuf([128, 32, NT4], BF16, "u4")
                rrot = Rot(P, 2, [128, 512], F32, "r4")
                rmsnorm_tile(xt, NT4, PR_GMLP, h2, sqrot, rs_t)
                for half in range(2):
                    for fi in range(32):
                        f = half * 32 + fi
                        wb = wrot.get()
                        wload(wb, wff1[f], 2048, stg)
                        ps = PS[1 + (f % 2)]
                        for k in range(16):
                            P.op("pe", lambda e, k=k, wb=wb, ps=ps: e.matmul(ps[:, 0:NT4], wb[:, k * 128:(k + 1) * 128], h2[:, k, :], start=(k == 0), stop=(k == 15)),
                                 reads=[wb, h2], writes=[ps] if k == 0 else (), pw=() if k == 0 else [ps])
                        r = rrot.get()
                        P.op("act", lambda e, r=r, ps=ps: e.activation(r[:, 0:NT4], ps[:, 0:NT4], AF.Relu), reads=[ps], writes=[r])
                        P.op("dve", lambda e, r=r, fi=fi: e.tensor_tensor(uT[:, fi, :], r[:, 0:NT4], r[:, 0:NT4], ALU.mult), reads=[r],
                             writes=[uT] if fi == 0 else (), pw=() if fi == 0 else [uT])
                    for c in range(16):
                        wb = w2rot.get()
                        wload(wb, wff2[c][:, half * 4096:(half + 1) * 4096], 4096, stg)
                        ps = PS[3 + (c % 2)]
                        for fi in range(32):
                            P.op("pe", lambda e, fi=fi, wb=wb, ps=ps: e.matmul(ps[:, 0:NT4], wb[:, fi * 128:(fi + 1) * 128], uT[:, fi, :], start=(fi == 0), stop=(fi == 31)),
                                 reads=[wb, uT], writes=[ps] if fi == 0 else (), pw=() if fi == 0 else [ps])
                        if half == 0:
                            P.op("dve", lambda e, c=c, ps=ps: e.tensor_tensor(xt[:, c, :], xt[:, c, :], ps[:, 0:NT4], ALU.add), reads=[ps, xt], pw=[xt])
                        else:
                            o = orot.get()
                            P.op("dve", lambda e, c=c, ps=ps, o=o: e.tensor_tensor(o[:, 0:NT4], xt[:, c, :], ps[:, 0:NT4], ALU.add), reads=[ps, xt], writes=[o])
                            P.dma("sp", outT[c * 128:(c + 1) * 128, c0:c0 + NT4], o[:, 0:NT4], reads=[o], pw=[outT])
    fin = [outT]
    if dbg and 3 in PH:
        omine_o = P.dram("omine", [NCT, 768, NT4], BF16, "ExternalOutput")
        for ct in range(NCT):
            P.dma("sp", omine_o[ct], omine[ct], reads=[omine], writes=[omine_o] if ct == 0 else (), pw=() if ct == 0 else [omine_o])
        fin += [omine_o]
    if dbg:
        fin += [foxv, smallT] + graw + foxq + foxk
    P.wait_all("sp", fin)
    P.emit()
    return nc


class View:
    __slots__ = ("ap", "b")

    def __init__(self, b, ap):
        self.b = b; self.ap = ap

    def __getitem__(self, idx):
        return self.ap[idx]


class Rec:
    def __init__(self):
        self.l = []

    def op(self, eng, fn, reads=(), writes=(), pw=()):
        self.l.append(("op", eng, _freeze(fn), tuple(reads), tuple(writes), tuple(pw)))

    def dma(self, q, out_ap, in_ap, reads=(), writes=(), pw=(), **kw):
        self.l.append(("dma", q, out_ap, in_ap, tuple(reads), tuple(writes), tuple(pw), kw))

    def custom(self, q, fn, reads=(), writes=(), pw=()):
        self.l.append(("custom", q, _freeze(fn), tuple(reads), tuple(writes), tuple(pw)))


def replay(P, recs):
    n = max(len(r.l) for r in recs)
    for k in range(n):
        for r in recs:
            if k < len(r.l):
                it = r.l[k]
                if it[0] == "op":
                    P.op(it[1], it[2], reads=it[3], writes=it[4], pw=it[5])
                elif it[0] == "custom":
                    P.custom(it[1], it[2], reads=it[3], writes=it[4], pw=it[5])
                else:
                    P.dma(it[1], it[2], it[3], reads=it[4], writes=it[5], pw=it[6], **it[7])


def gdn_phase(P, nc, S, NTILE, NCH, PS, PSB, cst, cbf, prm, epsc, graw, smallT, omine_write, ag_tile=None):
    ident_bf = cbf[:, 0:128]; ones_bf = cbf[:, 384:512]
    I64 = cbf[0:64, 0:64]
    UT64 = cst[0:64, C_TRI:C_TRI + 64]
    ID64 = cst[0:64, C_ID:C_ID + 64]

    def head(R, hi):
        ss = PS[hi]
        X = PS[2 + 2 * hi]; Yb = PS[3 + 2 * hi]; PB = PSB[hi]
        A = View(X, X[:, 0:129]); B = View(X, X[0:64, 129:257]); C = View(X, X[0:64, 257:385]); Dp = View(X, X[0:64, 385:449])
        E1 = View(Yb, Yb[:, 0:64]); E2 = View(Yb, Yb[:, 64:128]); E3 = View(Yb, Yb[:, 128:256])
        F1 = View(Yb, Yb[0:64, 256:384]); F2 = View(Yb, Yb[0:64, 384:512])
        PB1 = View(PB, PB[0:64, 0:64]); PB2 = View(PB, PB[0:64, 128:384])
        gall = P.sbuf([64, NCH], F32, "gall"); ball = P.sbuf([64, NCH], F32, "ball")
        nea = P.sbuf([64, 1], F32, "nea")
        xb = [P.sbuf([128, 515], F32, f"xb{i}") for i in range(4)]
        y = P.sbuf([128, 512], F32, "gy"); tmp = [P.sbuf([128, 512], F32, "gtmp") for _ in range(2)]; sg = P.sbuf([128, 512], F32, "gsg")
        yc = P.sbuf([128, 512], F32, "gyc")
        sq = P.sbuf([128, 512], BF16, "gsq"); rs = P.sbuf([128, 512], F32, "grs")
        qh = P.sbuf([128, 512], BF16, "qh"); kh = P.sbuf([128, 512], BF16, "kh"); vb = P.sbuf([128, 512], BF16, "vb")
        zs = P.sbuf([128, 512], F32, "zs"); oraw = P.sbuf([128, 512], F32, "oraw"); on = P.sbuf([128, 512], F32, "on")
        ob = P.sbuf([128, 512], BF16, "gob")
        gB = P.sbuf([64, 128], F32, "gB"); bB = P.sbuf([64, 128], F32, "bB")
        gamRow = P.sbuf([128, 64], F32, "gamRow"); betaRow = P.sbuf([128, 64], F32, "betaRow"); aRow = P.sbuf([128, 64], F32, "aRow")
        gamCol = P.sbuf([64, 1], F32, "gamCol"); acol = P.sbuf([64, 1], F32, "acol"); rcol = P.sbuf([64, 1], F32, "rcol"); bacol = P.sbuf([64, 1], F32, "bacol")
        t0d = P.sbuf([64, 64], F32, "t0d"); d2 = P.sbuf([64, 128], F32, "d2")
        kbq = P.sbuf([128, 128], BF16, "kbq"); NQ = P.sbuf([64, 128], BF16, "NQ")
        PP = [P.sbuf([64, 128], BF16, f"PP{i}") for i in range(2)]
        Y = [P.sbuf([64, 64], BF16, f"Y{i}") for i in range(2)]
        RHSw = P.sbuf([64, 128], BF16, "RHSw"); kdec = P.sbuf([64, 128], BF16, "kdec"); RHSv = P.sbuf([64, 128], BF16, "RHSv")
        nwT = P.sbuf([128, 64], BF16, "nwT"); uc = P.sbuf([64, 128], F32, "uc"); qdT = P.sbuf([128, 64], BF16, "qdT")
        u = P.sbuf([64, 128], BF16, "u"); S32 = P.sbuf([128, 128], F32, "S32"); Sbf = P.sbuf([128, 128], BF16, "Sbf"); St = P.sbuf([128, 128], F32, "St")
        R.dma("sp", gall[:, :], smallT[4 + hi, :].rearrange("(n p) -> p n", p=64), reads=[smallT], writes=[gall], allow_slow_non_contiguous=True)
        R.dma("sp", ball[:, :], smallT[2 + hi, :].rearrange("(n p) -> p n", p=64), reads=[smallT], writes=[ball], allow_slow_non_contiguous=True)
        R.op("act", lambda e: e.activation(nea[:, :], prm[0:64, PR_ALOG + hi:PR_ALOG + hi + 1], AF.Exp), reads=[prm], writes=[nea])
        R.op("dve", lambda e: e.tensor_scalar(nea[:, :], nea[:, :], -1.0, None, ALU.mult), reads=[nea], writes=[nea])
        R.op("act", lambda e: e.activation(gall[:, :], gall[:, :], AF.Exp, bias=prm[0:64, PR_DTB + hi:PR_DTB + hi + 1], scale=1.0), reads=[gall, prm], writes=[gall])
        R.op("act", lambda e: e.activation(gall[:, :], gall[:, :], AF.Ln, bias=epsc[0:64, 1:2], scale=1.0), reads=[gall, epsc], writes=[gall])
        R.op("dve", lambda e: e.tensor_scalar(gall[:, :], gall[:, :], nea[:, 0:1], None, ALU.mult), reads=[gall, nea], writes=[gall])
        R.op("act", lambda e: e.activation(ball[:, :], ball[:, :], AF.Sigmoid), reads=[ball], writes=[ball])
        R.op("pool", lambda e: e.memset(S32[:, :], 0.0), writes=[S32])
        R.op("pool", lambda e: e.memset(Sbf[:, :], 0.0), writes=[Sbf])
        for t in range(NTILE):
            c0t = t * 512
            for gi in range(4):
                src = graw[gi * 2 + hi]
                if t == 0:
                    R.op("pool", lambda e, gi=gi: e.memset(xb[gi][:, 0:3], 0.0), writes=[xb[gi]])
                    R.dma("sp", xb[gi][:, 3:515], src[:, 0:512], reads=[src], pw=[xb[gi]])
                else:
                    R.dma("sp", xb[gi][:, 0:515], src[:, c0t - 3:c0t + 512], reads=[src], writes=[xb[gi]])
            for gi in range(3):
                wc = PR_CONV + (gi * 2 + hi) * 4
                R.op("act", lambda e, gi=gi, wc=wc: e.activation(y[:, :], xb[gi][:, 0:512], AF.Copy, scale=prm[:, wc:wc + 1]), reads=[xb[gi], prm], writes=[y])
                for i in range(1, 4):
                    tm = tmp[i % 2]
                    R.op("act", lambda e, gi=gi, wc=wc, i=i, tm=tm: e.activation(tm[:, :], xb[gi][:, i:i + 512], AF.Copy, scale=prm[:, wc + i:wc + i + 1]), reads=[xb[gi], prm], writes=[tm])
                    R.op("dve", lambda e, tm=tm: e.tensor_tensor(y[:, :], y[:, :], tm[:, :], ALU.add), reads=[y, tm], writes=[y])
                R.op("act", lambda e: e.activation(sg[:, :], y[:, :], AF.Sigmoid), reads=[y], writes=[sg])
                if gi == 2:
                    R.op("dve", lambda e: e.tensor_tensor(vb[:, :], y[:, :], sg[:, :], ALU.mult), reads=[y, sg], writes=[vb])
                else:
                    R.op("dve", lambda e: e.tensor_tensor(yc[:, :], y[:, :], sg[:, :], ALU.mult), reads=[y, sg], writes=[yc])
                    R.op("act", lambda e: e.activation(sq[:, :], yc[:, :], AF.Square), reads=[yc], writes=[sq])
                    R.op("pe", lambda e: e.matmul(ss[:, :], ones_bf, sq[:, :], start=True, stop=True), reads=[cbf, sq], writes=[ss])
                    R.op("act", lambda e: e.activation(rs[:, :], ss[:, :], AF.Sqrt, bias=epsc[:, 0:1], scale=1.0), reads=[ss, epsc], writes=[rs])
                    R.op("dve", lambda e: e.reciprocal(rs[:, :], rs[:, :]), reads=[rs], writes=[rs])
                    if gi == 0:
                        R.op("dve", lambda e: e.tensor_scalar(rs[:, :], rs[:, :], 128.0 ** -0.5, None, ALU.mult), reads=[rs], writes=[rs])
                    dst = qh if gi == 0 else kh
                    R.op("dve", lambda e, dst=dst: e.tensor_tensor(dst[:, :], yc[:, :], rs[:, :], ALU.mult), reads=[yc, rs], writes=[dst])
            R.op("act", lambda e: e.activation(sg[:, :], xb[3][:, 3:515], AF.Sigmoid), reads=[xb[3]], writes=[sg])
            R.op("dve", lambda e: e.tensor_tensor(zs[:, :], xb[3][:, 3:515], sg[:, :], ALU.mult), reads=[xb[3], sg], writes=[zs])
            for ci in range(8):
                n = t * 8 + ci
                c0 = ci * 64
                R.op("act", lambda e, n=n: e.activation(gB[:, :], cst[0:64, C_ONES:C_ONES + 128], AF.Copy, scale=gall[:, n:n + 1]), reads=[cst, gall], writes=[gB])
                R.op("act", lambda e, n=n: e.activation(bB[:, :], cst[0:64, C_ONES:C_ONES + 128], AF.Copy, scale=ball[:, n:n + 1]), reads=[cst, ball], writes=[bB])
                R.op("pe", lambda e: e.matmul(A[:, 0:64], gB[:, :], UT64, start=True, stop=True), reads=[gB, cst], writes=[A])
                R.op("pe", lambda e: e.matmul(A[:, 64:128], bB[:, :], ID64, start=True, stop=True), reads=[bB, cst], pw=[A])
                R.op("pe", lambda e, n=n: e.matmul(A[0:64, 128:129], UT64, gall[:, n:n + 1], start=True, stop=True), reads=[gall, cst], pw=[A])
                R.op("dve", lambda e: e.tensor_copy(gamRow[:, :], A[:, 0:64]), reads=[A], writes=[gamRow])
                R.op("dve", lambda e: e.tensor_copy(betaRow[:, :], A[:, 64:128]), reads=[A], writes=[betaRow])
                R.op("dve", lambda e: e.tensor_copy(gamCol[:, :], A[0:64, 128:129]), reads=[A], writes=[gamCol])
                R.op("act", lambda e: e.activation(aRow[:, :], gamRow[:, :], AF.Exp), reads=[gamRow], writes=[aRow])
                R.op("act", lambda e: e.activation(acol[:, :], gamCol[:, :], AF.Exp), reads=[gamCol], writes=[acol])
                R.op("act", lambda e: e.activation(rcol[:, :], gamCol[:, :], AF.Exp, bias=gamRow[0:64, 63:64], scale=-1.0), reads=[gamCol, gamRow], writes=[rcol])
                R.op("dve", lambda e, n=n: e.tensor_tensor(bacol[:, :], acol[:, :], ball[:, n:n + 1], ALU.mult), reads=[acol, ball], writes=[bacol])
                R.op("dve", lambda e: e.tensor_scalar(t0d[:, :], gamRow[0:64, :], gamCol[:, 0:1], None, ALU.subtract), reads=[gamRow, gamCol], writes=[t0d])
                R.op("dve", lambda e: e.tensor_tensor(d2[:, 0:64], t0d[:, :], cst[0:64, C_NM:C_NM + 64], ALU.add), reads=[t0d, cst], writes=[d2])
                R.op("dve", lambda e: e.tensor_tensor(d2[:, 64:128], t0d[:, :], cst[0:64, C_NM + 64:C_NM + 128], ALU.add), reads=[t0d, cst], pw=[d2])
                R.op("act", lambda e: e.activation(d2[:, :], d2[:, :], AF.Exp), reads=[d2], writes=[d2])
                R.op("dve", lambda e, c0=c0: e.tensor_tensor(kbq[:, 0:64], kh[:, c0:c0 + 64], betaRow[:, :], ALU.mult), reads=[kh, betaRow], writes=[kbq])
                R.op("act", lambda e, c0=c0: e.copy(kbq[:, 64:128], qh[:, c0:c0 + 64]), reads=[qh], pw=[kbq])
                R.op("pe", lambda e, c0=c0: e.matmul(B[:, :], kh[:, c0:c0 + 64], kbq[:, :], start=True, stop=True), reads=[kh, kbq], writes=[B])
                R.op("dve", lambda e: e.tensor_tensor(NQ[:, :], B[:, :], d2[:, :], ALU.mult), reads=[B, d2], writes=[NQ])
                R.op("pe", lambda e: e.transpose(PB1[:, :], NQ[:, 0:64], I64), reads=[NQ, cbf], writes=[PB1])
                R.op("dve", lambda e: e.tensor_copy(PP[0][:, 64:128], PB1[:, :]), reads=[PB1], writes=[PP[0]])
                R.op("act", lambda e: e.copy(PP[0][:, 0:64], NQ[:, 0:64]), reads=[NQ], pw=[PP[0]])
                R.op("dve", lambda e: e.tensor_tensor(Y[0][:, :], I64, NQ[:, 0:64], ALU.subtract), reads=[cbf, NQ], writes=[Y[0]])
                pk = 0; yk = 0
                for lvl in range(5):
                    R.op("pe", lambda e, pk=pk: e.matmul(C[:, 0:64], PP[pk][:, 64:128], PP[pk][:, 0:64], start=True, stop=True), reads=[PP[pk]], writes=[C])
                    R.op("pe", lambda e, pk=pk: e.matmul(C[:, 64:128], PP[pk][:, 0:64], PP[pk][:, 64:128], start=True, stop=True), reads=[PP[pk]], pw=[C])
                    R.op("dve", lambda e, pk=pk: e.tensor_copy(PP[1 - pk][:, :], C[:, :]), reads=[C], writes=[PP[1 - pk]])
                    pk = 1 - pk
                    R.op("pe", lambda e, pk=pk, yk=yk: e.matmul(Dp[:, :], PP[pk][:, 64:128], Y[yk][:, :], start=True, stop=True), reads=[PP[pk], Y[yk]], writes=[Dp])
                    R.op("dve", lambda e, yk=yk: e.tensor_tensor(Y[1 - yk][:, :], Dp[:, :], Y[yk][:, :], ALU.add), reads=[Dp, Y[yk]], writes=[Y[1 - yk]])
                    yk = 1 - yk
                Tt = Y[yk]
                R.op("pe", lambda e, c0=c0: e.transpose(PB2[:, 0:128], kh[:, c0:c0 + 64], ident_bf), reads=[kh, cbf], writes=[PB2])
                R.op("pe", lambda e, c0=c0: e.transpose(PB2[:, 128:256], vb[:, c0:c0 + 64], ident_bf), reads=[vb, cbf], pw=[PB2])
                R.op("dve", lambda e: e.tensor_scalar(RHSw[:, :], PB2[:, 0:128], bacol[:, 0:1], None, ALU.mult), reads=[PB2, bacol], writes=[RHSw])
                R.op("dve", lambda e: e.tensor_scalar(kdec[:, :], PB2[:, 0:128], rcol[:, 0:1], None, ALU.mult), reads=[PB2, rcol], writes=[kdec])
                R.op("dve", lambda e, n=n: e.tensor_scalar(RHSv[:, :], PB2[:, 128:256], ball[:, n:n + 1], None, ALU.mult), reads=[PB2, ball], writes=[RHSv])
                R.op("pe", lambda e, Tt=Tt: e.matmul(E1[:, :], RHSw[:, :], Tt[:, :], start=True, stop=True), reads=[RHSw, Tt], writes=[E1])
                R.op("dve", lambda e: e.tensor_scalar(nwT[:, :], E1[:, :], -1.0, None, ALU.mult), reads=[E1], writes=[nwT])
                R.op("pe", lambda e, Tt=Tt: e.matmul(F1[:, :], Tt[:, :], RHSv[:, :], start=True, stop=True), reads=[RHSv, Tt], writes=[F1])
                R.op("dve", lambda e: e.tensor_copy(uc[:, :], F1[:, :]), reads=[F1], writes=[uc])
                R.op("dve", lambda e, c0=c0: e.tensor_tensor(qdT[:, :], qh[:, c0:c0 + 64], aRow[:, :], ALU.mult), reads=[qh, aRow], writes=[qdT])
                R.op("pe", lambda e: e.matmul(F2[:, :], nwT[:, :], Sbf[:, :], start=True, stop=True), reads=[nwT, Sbf], writes=[F2])
                R.op("dve", lambda e: e.tensor_tensor(u[:, :], F2[:, :], uc[:, :], ALU.add), reads=[F2, uc], writes=[u])
                R.op("pe", lambda e: e.matmul(E2[:, :], Sbf[:, :], qdT[:, :], start=True, stop=False), reads=[Sbf, qdT], writes=[E2])
                R.op("pe", lambda e: e.matmul(E2[:, :], u[:, :], NQ[:, 64:128], start=False, stop=True), reads=[u, NQ], pw=[E2])
                R.op("dve", lambda e, c0=c0: e.tensor_copy(oraw[:, c0:c0 + 64], E2[:, :]), reads=[E2], writes=[oraw] if ci == 0 else (), pw=() if ci == 0 else [oraw])
                R.op("pe", lambda e: e.matmul(E3[:, :], kdec[:, :], u[:, :], start=True, stop=True), reads=[kdec, u], writes=[E3])
                R.op("act", lambda e: e.activation(St[:, :], S32[:, :], AF.Copy, scale=aRow[:, 63:64]), reads=[S32, aRow], writes=[St])
                R.op("dve", lambda e: e.tensor_tensor(S32[:, :], St[:, :], E3[:, :], ALU.add), reads=[St, E3], writes=[S32])
                R.op("act", lambda e: e.copy(Sbf[:, :], S32[:, :]), reads=[S32], writes=[Sbf])
            R.op("act", lambda e: e.activation(sq[:, :], oraw[:, :], AF.Square), reads=[oraw], writes=[sq])
            R.op("pe", lambda e: e.matmul(ss[:, :], ones_bf, sq[:, :], start=True, stop=True), reads=[cbf, sq], writes=[ss])
            R.op("act", lambda e: e.activation(rs[:, :], ss[:, :], AF.Sqrt, bias=epsc[:, 0:1], scale=1.0 / 128), reads=[ss, epsc], writes=[rs])
            R.op("dve", lambda e: e.reciprocal(rs[:, :], rs[:, :]), reads=[rs], writes=[rs])
            R.op("dve", lambda e: e.tensor_tensor(on[:, :], oraw[:, :], rs[:, :], ALU.mult), reads=[oraw, rs], writes=[on])
            R.op("dve", lambda e: e.tensor_scalar(on[:, :], on[:, :], prm[:, PR_GDNG:PR_GDNG + 1], None, ALU.mult), reads=[on, prm], writes=[on])
            R.op("dve", lambda e: e.tensor_tensor(ob[:, :], on[:, :], zs[:, :], ALU.mult), reads=[on, zs], writes=[ob])
            omine_write(hi * 128, c0t, ob, R)
            if hi == 1 and ag_tile is not None:
                ag_tile(R, c0t)

    with P.scope():
        recs = []
        for hi in range(2):
            R = Rec()
            head(R, hi)
            recs.append(R)
        replay(P, recs)


_CACHE = {}


def prep_inputs(I, S, PH=(0, 1, 2, 3, 4, 5)):
    w_in = I["w_in"][0]
    TQ = S // 4
    cst = make_consts()
    gates0 = 8216
    big = {}
    if 5 in PH:
        big["wg"] = np.concatenate([relay(w_in[:, gates0 + br * D:gates0 + (br + 1) * D]) for br in range(3)], axis=0)
        big["wup"] = np.concatenate([relay(I[n][0]) for n in ("w_up_gdn", "w_up_fox", "w_up_mem")], axis=0)
        big["wout"] = relay(I["w_out"][0]); big["wff1"] = relay(I["w_ff1"][0]); big["wff2"] = relay(I["w_ff2"][0])
    maps = []
    for core in range(8):
        b, g = core // 4, core % 4
        hs = [2 * g, 2 * g + 1]
        xTb = np.ascontiguousarray(I["x"][b, :S].T)
        cols = []
        for base in (0, 1024, 2048, 3072, 4112, 5136):
            for h in hs:
                cols.append(np.arange(base + h * 128, base + (h + 1) * 128))
        cols.append(np.arange(7192 + g * 256, 7192 + (g + 1) * 256))
        w1 = relay(w_in[:, np.concatenate(cols)])
        scol = np.concatenate([np.arange(6160 + h * 128, 6160 + (h + 1) * 128) for h in hs] + [np.array([7184 + h for h in hs]),
                              np.array([4096 + h for h in hs]), np.array([4104 + h for h in hs])])
        w1s = relay_tok(w_in[:, scol])
        wmkv = I["w_mem_kv"][0]
        wmk = relay(wmkv[:, g * 256:(g + 1) * 256])
        wmv = relay_tok(wmkv[:, 1024 + g * 256:1024 + (g + 1) * 256])
        prm = np.zeros((128, NPRM), np.float32)
        prm[:, PR_GMIX:PR_GMIX + 16] = I["g_mix"][0].reshape(16, 128).T
        prm[:, PR_GMLP:PR_GMLP + 16] = I["g_mlp"][0].reshape(16, 128).T
        prm[:, PR_GMEM:PR_GMEM + 16] = I["g_mem"][0].reshape(16, 128).T
        cw = I["conv_w"][0]
        for gi, base in enumerate((0, 1024, 2048)):
            for hi, h in enumerate(hs):
                prm[:, PR_CONV + (gi * 2 + hi) * 4:PR_CONV + (gi * 2 + hi) * 4 + 4] = cw[:, base + h * 128:base + (h + 1) * 128].T
        prm[:, PR_GDNG] = I["gdn_norm_g"][0]
        prm[:, PR_FQN] = I["fox_q_norm"][0]; prm[:, PR_FKN] = I["fox_k_norm"][0]
        prm[:, PR_MQN:PR_MQN + 2] = I["mem_q_norm"][0].reshape(2, 128).T
        prm[:, PR_MKN:PR_MKN + 2] = I["mem_k_norm"][0].reshape(2, 128).T
        for hi, h in enumerate(hs):
            prm[:, PR_ALOG + hi] = I["a_log"][0, h]; prm[:, PR_DTB + hi] = I["dt_bias"][0, h]; prm[:, PR_BF + hi] = I["fox_b_f"][0, h]
        maps.append({
            "xT": xTb, "xq": np.ascontiguousarray(xTb[:, g * TQ:(g + 1) * TQ]), "memT": np.ascontiguousarray(I["mem"][b].T),
            "prm": prm, "cst": cst, "w1": w1, "w1s": w1s, "wmk": wmk, "wmv": wmv, **big,
        })
    return maps


def run(I, S, dbg=False, PH=(0, 1, 2, 3, 4, 5)):
    key = (S, dbg, PH)
    if key not in _CACHE:
        _CACHE[key] = build(S, dbg, PH)
    nc = _CACHE[key]
    maps = prep_inputs(I, S, PH)
    res = run_bass_kernel_spmd(nc, maps, core_ids=list(range(8)))
    return res


def kernel(**inputs):
    I = {k: np.asarray(v) for k, v in inputs.items()}
    S = I["x"].shape[1]
    res = run(I, S)
    TQ = S // 4
    out = np.empty((2, S, D), np.float32)
    for core in range(8):
        b, g = core // 4, core % 4
        out[b, g * TQ:(g + 1) * TQ, :] = res.results[core]["outT"].T
    return out
```

```python
from contextlib import ExitStack, contextmanager
import types
import os
import numpy as np
import concourse.bass as bass
import concourse.mybir as mybir
from concourse.bass_utils import run_bass_kernel_spmd

F32 = mybir.dt.float32
BF16 = mybir.dt.bfloat16
AF = mybir.ActivationFunctionType
ALU = mybir.AluOpType

ENGS = ("pe", "act", "dve", "pool", "sp")
SAME_ENG_SYNC = bool(int(os.environ.get("KSES", "1")))
NDMA_SEM = 10
EPS = 1e-6
D = 2048
DFF = 8192
NEG = -30000.0
ARENA_F32 = 47 * 1024


class T:
    __slots__ = ("t", "name", "lws", "rd", "excl")

    def __init__(self, t, name, fence=None):
        self.t = t
        self.name = name
        self.excl = False
        self.lws = {}
        self.rd = dict(fence) if fence else {}

    def __getitem__(self, idx):
        return self.t[idx]


class Ev:
    __slots__ = ("kind", "key", "idx", "op")

    def __init__(self, kind, key, idx, op=None):
        self.kind = kind; self.key = key; self.idx = idx; self.op = op


class Op:
    __slots__ = ("eng", "fn", "waits", "mark", "cnt")

    def __init__(self, eng, fn):
        self.eng = eng; self.fn = fn; self.waits = []; self.mark = False; self.cnt = 0


def _freeze(fn):
    if fn is None or fn.__closure__ is None:
        return fn
    cells = []
    for c in fn.__closure__:
        try:
            cells.append(types.CellType(c.cell_contents))
        except ValueError:
            cells.append(c)
    return types.FunctionType(fn.__code__, fn.__globals__, fn.__name__, fn.__defaults__, tuple(cells))


def _merge(d, ev):
    o = d.get(ev.key)
    if o is None or o.idx < ev.idx:
        d[ev.key] = ev


class Prog:
    def __init__(self, nc):
        self.nc = nc
        self.es = ExitStack()
        self.ops = {e: [] for e in ENGS}
        self.seen = {e: {} for e in ENGS}
        self.esem = {e: self.es.enter_context(nc.semaphore("s_" + e)) for e in ENGS}
        self.dsem = {}
        self.dcount = {}
        self.dnext = {}
        for q in ("sp", "pool", "act"):
            self.dsem[q] = [self.es.enter_context(nc.semaphore(f"d_{q}{i}")) for i in range(NDMA_SEM)]
            self.dcount[q] = [0] * NDMA_SEM
            self.dnext[q] = 0
        self.csem = {}
        self.arena = None
        self.sp = 0
        self.sp_max = 0
        self.last_ev = {}
        self.ntile = 0
        self.fence = {}
        self.scopes = []

    def sbuf(self, shape, dt, name=None):
        self.ntile += 1
        name = (name or "t") + f"_{self.ntile}"
        if self.arena is None:
            self.arena = self.es.enter_context(self.nc.sbuf_tensor("arena", [128, ARENA_F32], F32))
        shape = list(shape)
        esz = 4 if dt == F32 else 2
        nfree = 1
        for d in shape[1:]:
            nfree *= d
        nbytes = (nfree * esz + 31) // 32 * 32
        off = self.sp
        self.sp += nbytes
        assert self.sp <= ARENA_F32 * 4, f"SBUF arena overflow: {self.sp} > {ARENA_F32 * 4} ({name})"
        self.sp_max = max(self.sp_max, self.sp)
        v = self.arena[0:shape[0], off // 4:(off + nbytes) // 4]
        if dt != F32:
            v = v.bitcast(dt)
        v = v[:, 0:nfree]
        if len(shape) == 3:
            v = v.rearrange("p (a b) -> p a b", b=shape[2])
        t = T(v, name, self.fence)
        if self.scopes:
            self.scopes[-1][1].append(t)
        return t

    def psum(self, shape, dt, name=None):
        self.ntile += 1
        name = (name or "p") + f"_{self.ntile}"
        t = T(self.es.enter_context(self.nc.psum_tensor(name, list(shape), dt)), name)
        t.excl = True
        return t

    def dram(self, name, shape, dt, kind="Internal"):
        return T(self.nc.dram_tensor(name, list(shape), dt, kind=kind), name)

    @contextmanager
    def scope(self):
        tiles = []
        self.scopes.append((None, tiles))
        sp0 = self.sp
        try:
            yield
        finally:
            self.scopes.pop()
            f = dict(self.fence)
            for t in tiles:
                for ev in t.lws.values():
                    _merge(f, ev)
                for ev in t.rd.values():
                    _merge(f, ev)
            self.fence = f
            self.sp = sp0

    def _need(self, eng, ev, op):
        if ev.kind == "e":
            if ev.key[1] == eng and (eng == "pe" or not SAME_ENG_SYNC):
                return
            if self.seen[eng].get(ev.key, -1) >= ev.idx:
                return
            self.seen[eng][ev.key] = ev.idx
            ev.op.mark = True
            op.waits.append(ev)
        else:
            if self.seen[eng].get(ev.key, -1) >= ev.idx:
                return
            self.seen[eng][ev.key] = ev.idx
            op.waits.append(ev)

    def _deps(self, eng, op, reads, writes, pw):
        reads = [getattr(t, "b", t) for t in reads]; writes = [getattr(t, "b", t) for t in writes]; pw = [getattr(t, "b", t) for t in pw]
        need = {}
        for t in reads:
            for ev in t.lws.values():
                _merge(need, ev)
            if t.excl:
                for ev in t.rd.values():
                    if ev.kind != "e" or ev.key[1] != eng:
                        _merge(need, ev)
        for t in writes:
            for ev in t.lws.values():
                _merge(need, ev)
            for ev in t.rd.values():
                _merge(need, ev)
        for t in pw:
            for ev in t.rd.values():
                _merge(need, ev)
        for ev in need.values():
            self._need(eng, ev, op)

    def _record(self, ev, reads, writes, pw):
        reads = [getattr(t, "b", t) for t in reads]; writes = [getattr(t, "b", t) for t in writes]; pw = [getattr(t, "b", t) for t in pw]
        for t in reads:
            _merge(t.rd, ev)
        for t in writes:
            t.lws = {ev.key: ev}
            t.rd = {}
        for t in pw:
            _merge(t.lws, ev)

    def _cut(self):
        self.nops = getattr(self, "nops", 0) + 1
        if os.environ.get("KLIST"):
            import inspect
            fr = inspect.stack()[2]
            print("OP", self.nops, inspect.stack()[1].function, fr.lineno, (fr.code_context or [""])[0].strip()[:110])
        return self.nops > int(os.environ.get("KCUT", "1000000000"))

    def op(self, eng, fn, reads=(), writes=(), pw=()):
        if self._cut():
            return
        o = Op(eng, _freeze(fn))
        self._deps(eng, o, reads, writes, pw)
        idx = len(self.ops[eng])
        self.ops[eng].append(o)
        ev = Ev("e", ("e", eng), idx, o)
        self.last_ev[eng] = ev
        self._record(ev, reads, writes, pw)
        return o

    def dma(self, q, out_ap, in_ap, reads=(), writes=(), pw=(), **kw):
        if self._cut():
            return
        o = Op(q, None)
        self._deps(q, o, reads, writes, pw)
        si = self.dnext[q]
        self.dnext[q] = (si + 1) % NDMA_SEM
        prev = self.dcount[q][si]
        key = ("d", q, si)
        if prev > 0:
            self._need(q, Ev("d", key, prev), o)
        val = prev + 1
        self.dcount[q][si] = val
        sem = self.dsem[q][si]
        o.fn = lambda e: e.dma_start(out=out_ap, in_=in_ap, **kw).then_inc(sem, 16)
        self.ops[q].append(o)
        self._record(Ev("d", key, val), reads, writes, pw)

    def custom(self, q, fn, reads=(), writes=(), pw=()):
        o = Op(q, None)
        self._deps(q, o, reads, writes, pw)
        sem = self.es.enter_context(self.nc.semaphore(f"c_{len(self.csem)}"))
        key = ("c", len(self.csem))
        self.csem[key] = sem
        fn = _freeze(fn)
        o.fn = lambda e: fn(e, sem)
        self.ops[q].append(o)
        self._record(Ev("d", key, 1), reads, writes, pw)

    def wait_all(self, eng, tiles):
        o = Op(eng, None)
        need = {}
        for t in tiles:
            for ev in t.lws.values():
                _merge(need, ev)
            for ev in t.rd.values():
                _merge(need, ev)
        for ev in self.last_ev.values():
            _merge(need, ev)
        for q in self.dcount:
            for si, cnt in enumerate(self.dcount[q]):
                if cnt > 0:
                    _merge(need, Ev("d", ("d", q, si), cnt))
        for ev in need.values():
            self._need(eng, ev, o)
        self.ops[eng].append(o)

    def emit(self):
        nc = self.nc
        for e in ENGS:
            c = 0
            for o in self.ops[e]:
                if o.mark:
                    c += 1
                    o.cnt = c
        P = self

        def run(eng_name, e):
            for o in P.ops[eng_name]:
                for ev in o.waits:
                    if ev.kind == "e":
                        e.wait_ge(P.esem[ev.key[1]], ev.op.cnt)
                    elif ev.key[0] == "c":
                        e.wait_ge(P.csem[ev.key], 1)
                    else:
                        e.wait_ge(P.dsem[ev.key[1]][ev.key[2]], 16 * ev.idx)
                if o.fn is None:
                    continue
                r = o.fn(e)
                if o.mark:
                    r.then_inc(P.esem[eng_name], 1)

        with nc.Block() as block:
            @block.tensor
            def _(e):
                run("pe", e)

            @block.scalar
            def _(e):
                run("act", e)

            @block.vector
            def _(e):
                run("dve", e)

            @block.gpsimd
            def _(e):
                run("pool", e)

            @block.sync
            def _(e):
                run("sp", e)
        self.es.close()


class Rot:
    def __init__(self, P, n, shape, dt, name):
        self.b = [P.sbuf(shape, dt, name) for _ in range(n)]
        self.i = 0

    def get(self):
        t = self.b[self.i]
        self.i = (self.i + 1) % len(self.b)
        return t


PR_GMIX, PR_GMLP, PR_GMEM = 0, 16, 32
PR_CONV = 48
PR_GDNG = 72
PR_FQN, PR_FKN = 73, 74
PR_MQN, PR_MKN = 75, 77
PR_ALOG, PR_DTB, PR_BF = 79, 81, 83
NPRM = 85

C_ID, C_TRI, C_TRIS, C_ONES, C_SEL, C_NM = 0, 128, 256, 384, 512, 640
NCST = 768


def make_consts():
    c = np.zeros((128, NCST), np.float32)
    i = np.arange(128)
    c[:, C_ID:C_ID + 128] = np.eye(128)
    c[:, C_TRI:C_TRI + 128] = (i[:, None] <= i[None, :])
    c[:, C_TRIS:C_TRIS + 128] = (i[:, None] < i[None, :])
    c[:, C_ONES:C_ONES + 128] = 1.0
    c[64, C_SEL:C_SEL + 128] = 1.0
    j = np.arange(64)
    nm = np.zeros((128, 128), np.float32)
    nm[:64, 0:64] = np.where(j[None, :] > j[:, None], 0.0, NEG)
    nm[:64, 64:128] = np.where(j[None, :] >= j[:, None], 0.0, NEG)
    c[:, C_NM:C_NM + 128] = nm
    return c


def relay(W):
    K, N = W.shape
    return np.ascontiguousarray(W.reshape(K // 128, 128, N // 128, 128).transpose(2, 1, 0, 3)).reshape(N // 128, 128, K)


def relay_tok(W):
    K, n = W.shape
    return np.ascontiguousarray(W.reshape(K // 128, 128, n).transpose(1, 0, 2)).reshape(128, (K // 128) * n)


def build(S, dbg=False, PH=(0, 1, 2, 3, 4, 5)):
    nc = bass.Bass("TRN2", target_bir_lowering=False)
    NTILE = S // 512
    J = S // 128
    NCH = S // 64
    TQ = S // 4
    NT4 = min(512, TQ)
    NSUB = TQ // NT4

    def din(name, shape, dt=F32):
        return T(nc.dram_tensor(name, list(shape), dt, kind="ExternalInput"), name)

    xT = din("xT", [D, S]); xq = din("xq", [D, TQ]); memT = din("memT", [D, 256])
    prm_d = din("prm", [128, NPRM]); cst_d = din("cst", [128, NCST])
    w1 = din("w1", [14, 128, D]); w1s = din("w1s", [128, 16 * 262])
    wmk = din("wmk", [2, 128, D]); wmv = din("wmv", [128, 16 * 256])
    if 5 in PH:
        wg = din("wg", [48, 128, D]); wup = din("wup", [48, 128, 1024])
        wout = din("wout", [16, 128, D]); wff1 = din("wff1", [64, 128, D]); wff2 = din("wff2", [16, 128, DFF])
    outT = T(nc.dram_tensor("outT", [D, TQ], F32, kind="ExternalOutput"), "outT")

    P = Prog(nc)
    skind = "ExternalOutput" if dbg else "Internal"
    graw = [P.dram(f"graw{i}", [128, S], F32, skind) for i in range(8)]
    foxq = [P.dram(f"foxq{i}", [128, S], BF16, skind) for i in range(2)]
    foxk = [P.dram(f"foxk{i}", [128, S], BF16, skind) for i in range(2)]
    foxv = P.dram("foxv", [S, 256], BF16, skind)
    smallT = P.dram("smallT", [6, S], F32, skind)
    NCT = S // NT4
    omine = P.dram("omine_i", [NCT, 768, NT4], BF16)
    gath = P.dram("gath", [NCT, 4 * 768, NT4], BF16)

    def omine_write(row0, t0, ob, Dq=None):
        Dq = Dq or P
        for o in range(0, 512, NT4):
            ct = (t0 + o) // NT4
            Dq.dma("sp", omine[ct, row0:row0 + 128, :], ob[:, o:o + NT4], reads=[ob], pw=[omine])

    cst = P.sbuf([128, NCST], F32, "cst")
    prm = P.sbuf([128, NPRM], F32, "prm")
    cbf = P.sbuf([128, 512], BF16, "cbf")
    epsc = P.sbuf([128, 2], F32, "epsc")
    kTm = P.sbuf([128, 2, 256], BF16, "kTm")
    vm = P.sbuf([128, 2, 256], BF16, "vm")
    PS = [P.psum([128, 512], F32, f"ps{i}") for i in range(6)]
    PSB = [P.psum([128, 1024], BF16, f"psb{i}") for i in range(2)]

    P.dma("sp", cst[:, :], cst_d[:, :], writes=[cst])
    P.dma("sp", prm[:, :], prm_d[:, :], writes=[prm])
    P.op("dve", lambda e: e.tensor_copy(cbf[:, :], cst[:, 0:512]), reads=[cst], writes=[cbf])
    P.op("pool", lambda e: e.memset(epsc[:, 0:1], EPS), writes=[epsc])
    P.op("pool", lambda e: e.memset(epsc[:, 1:2], 1.0), pw=[epsc])
    ident_bf = cbf[:, 0:128]; tri_bf = cbf[:, 128:256]; ones_bf = cbf[:, 384:512]

    def rstd_from_ss(ss_ps, n, inv_d, out_t, np_=128):
        P.op("act", lambda e: e.activation(out_t[0:np_, 0:n], ss_ps[0:np_, 0:n], AF.Sqrt, bias=epsc[0:np_, 0:1], scale=inv_d),
             reads=[ss_ps, epsc], writes=[out_t])
        P.op("dve", lambda e: e.reciprocal(out_t[0:np_, 0:n], out_t[0:np_, 0:n]), reads=[out_t], writes=[out_t])

    tmp2 = Rot(P, 3, [128, 512], F32, "tmp2")

    def stt2(out_t, out_ap, in_t, in_ap, gc, rs_tile, rs_ap, n, first):
        tm = tmp2.get()
        P.op("act", lambda e: e.activation(tm[:, 0:n], in_ap, AF.Copy, scale=prm[:, gc:gc + 1]), reads=[in_t, prm], writes=[tm])
        P.op("dve", lambda e: e.tensor_tensor(out_ap, tm[:, 0:n], rs_ap, ALU.mult), reads=[tm, rs_tile],
             writes=[out_t] if first else (), pw=() if first else [out_t])

    def rmsnorm_tile(xt, n, gcol, hT, sqrot, rs_t):
        ss = PS[0]
        for c in range(16):
            sq = sqrot.get()
            P.op("act", lambda e, c=c, sq=sq: e.activation(sq[:, 0:n], xt[:, c, 0:n], AF.Square), reads=[xt], writes=[sq])
            P.op("pe", lambda e, c=c, sq=sq: e.matmul(ss[:, 0:n], ones_bf, sq[:, 0:n], start=(c == 0), stop=(c == 15)),
                 reads=[cbf, sq], writes=[ss] if c == 0 else (), pw=() if c == 0 else [ss])
        rstd_from_ss(ss, n, 1.0 / D, rs_t)
        for c in range(16):
            stt2(hT, hT[:, c, 0:n], xt, xt[:, c, 0:n], gcol + c, rs_t, rs_t[:, 0:n], n, c == 0)

    def wload(dst, src2d, n, stg=None):
        for o in range(0, n, 4096):
            m = min(4096, n - o)
            first = (o == 0)
            P.dma("pool", dst[:, o:o + m], src2d[:, o:o + m], writes=[dst] if first else (), pw=() if first else [dst])

    def phase(n):
        if n in PH:
            with P.scope():
                yield

    for _ in phase(0):
        mt = P.sbuf([128, 16, 256], F32, "mt")
        mh = P.sbuf([128, 16, 256], BF16, "mh")
        sqrot = Rot(P, 3, [128, 512], BF16, "sq")
        rs_t = P.sbuf([128, 512], F32, "rs")
        wk_b = [P.sbuf([128, 2048], BF16, "wk") for _ in range(2)]
        wv_b = P.sbuf([128, 4096], BF16, "wv")
        stg = Rot(P, 2, [128, 2048], F32, "stg")
        kraw = P.sbuf([128, 2, 256], F32, "kraw")
        P.dma("sp", mt[:, :, :], memT[:, :].rearrange("(c p) n -> p c n", p=128), writes=[mt])
        for dc in range(2):
            wload(wk_b[dc], wmk[dc], 2048, stg)
        wload(wv_b, wmv, 4096, stg)
        rmsnorm_tile(mt, 256, PR_GMEM, mh, sqrot, rs_t)
        ss2 = PS[3]
        for dc in range(2):
            ps = PS[1 + dc]
            for k in range(16):
                P.op("pe", lambda e, k=k, dc=dc, ps=ps: e.matmul(ps[:, 0:256], wk_b[dc][:, k * 128:(k + 1) * 128], mh[:, k, :], start=(k == 0), stop=(k == 15)),
                     reads=[wk_b[dc], mh], writes=[ps] if k == 0 else (), pw=() if k == 0 else [ps])
            P.op("dve", lambda e, dc=dc, ps=ps: e.tensor_copy(kraw[:, dc, :], ps[:, 0:256]), reads=[ps], writes=[kraw] if dc == 0 else (), pw=() if dc == 0 else [kraw])
            sq = sqrot.get()
            P.op("act", lambda e, sq=sq, dc=dc: e.activation(sq[:, 0:256], kraw[:, dc, :], AF.Square), reads=[kraw], writes=[sq])
            P.op("pe", lambda e, sq=sq, dc=dc: e.matmul(ss2[:, 0:256], ones_bf, sq[:, 0:256], start=(dc == 0), stop=(dc == 1)),
                 reads=[cbf, sq], writes=[ss2] if dc == 0 else (), pw=() if dc == 0 else [ss2])
        rstd_from_ss(ss2, 256, 1.0 / 256, rs_t)
        for dc in range(2):
            stt2(kTm, kTm[:, dc, :], kraw, kraw[:, dc, :], PR_MKN + dc, rs_t, rs_t[:, 0:256], 256, dc == 0)
        for kc in range(2):
            ps = PS[4 + kc]
            for k in range(16):
                P.op("pe", lambda e, k=k, kc=kc, ps=ps: e.matmul(ps[:, 0:256], mh[:, k, kc * 128:(kc + 1) * 128], wv_b[:, k * 256:(k + 1) * 256], start=(k == 0), stop=(k == 15)),
                     reads=[wv_b, mh], writes=[ps] if k == 0 else (), pw=() if k == 0 else [ps])
            P.op("act", lambda e, kc=kc, ps=ps: e.copy(vm[:, kc, :], ps[:, 0:256]), reads=[ps], writes=[vm] if kc == 0 else (), pw=() if kc == 0 else [vm])

    for _ in phase(1):
        w1b = [P.sbuf([128, 2048], BF16, "w1b") for _ in range(14)]
        w1sb = P.sbuf([128, 16 * 262], BF16, "w1sb")
        with P.scope():
            stg = None
            for cc in range(14):
                wload(w1b[cc], w1[cc], 2048, stg)
            wload(w1sb, w1s, 16 * 262, stg)
        xrot = Rot(P, 2, [128, 16, 512], F32, "xt")
        hT = P.sbuf([128, 16, 512], BF16, "hT")
        sqrot = Rot(P, 3, [128, 512], BF16, "sq")
        rs_t = P.sbuf([128, 512], F32, "rs")
        rs2 = P.sbuf([128, 512], F32, "rs2")
        rawrot = Rot(P, 3, [128, 512], F32, "raw")
        mqraw = P.sbuf([128, 2, 512], F32, "mqraw")
        mqn = P.sbuf([128, 2, 512], BF16, "mqn")
        obrot = Rot(P, 3, [128, 512], BF16, "ob")
        pTm = [P.sbuf([128, 512], BF16, "pTm") for _ in range(2)]
        vst = Rot(P, 2, [128, 256], BF16, "vst")
        sst = Rot(P, 2, [128, 6], F32, "sst")
        rden = P.sbuf([128, 512], F32, "rden")
        def xload(t):
            xt_ = xrot.get()
            P.dma("sp", xt_[:, :, :], xT[:, t * 512:(t + 1) * 512].rearrange("(c p) n -> p c n", p=128), writes=[xt_])
            return xt_

        xnext = xload(0)
        for t in range(NTILE):
            t0 = t * 512
            xt = xnext
            rmsnorm_tile(xt, 512, PR_GMIX, hT, sqrot, rs_t)
            if t + 1 < NTILE:
                xnext = xload(t + 1)
            for cc in range(14):
                ps = PS[1 + (cc % 2)]
                for k in range(16):
                    P.op("pe", lambda e, k=k, cc=cc, ps=ps: e.matmul(ps[:, :], w1b[cc][:, k * 128:(k + 1) * 128], hT[:, k, :], start=(k == 0), stop=(k == 15)),
                         reads=[w1b[cc], hT], writes=[ps] if k == 0 else (), pw=() if k == 0 else [ps])
                if cc < 8:
                    raw = rawrot.get()
                    P.op("act", lambda e, raw=raw, ps=ps: e.copy(raw[:, :], ps[:, :]), reads=[ps], writes=[raw])
                    P.dma("sp", graw[cc][:, t0:t0 + 512], raw[:, :], reads=[raw], pw=[graw[cc]])
                elif cc < 12:
                    raw = rawrot.get()
                    sq = sqrot.get()
                    P.op("dve", lambda e, raw=raw, ps=ps: e.tensor_copy(raw[:, :], ps[:, :]), reads=[ps], writes=[raw])
                    P.op("act", lambda e, sq=sq, ps=ps: e.activation(sq[:, :], ps[:, :], AF.Square), reads=[ps], writes=[sq])
                    P.op("pe", lambda e, sq=sq: e.matmul(PS[3][:, :], ones_bf, sq[:, :], start=True, stop=True), reads=[cbf, sq], writes=[PS[3]])
                    rstd_from_ss(PS[3], 512, 1.0 / 128, rs2)
                    ob = obrot.get()
                    gc = PR_FQN if cc < 10 else PR_FKN
                    stt2(ob, ob[:, :], raw, raw[:, :], gc, rs2, rs2[:, :], 512, True)
                    dst = (foxq if cc < 10 else foxk)[cc % 2]
                    P.dma("sp", dst[:, t0:t0 + 512], ob[:, :], reads=[ob], pw=[dst])
                else:
                    dc = cc - 12
                    sq = sqrot.get()
                    P.op("dve", lambda e, dc=dc, ps=ps: e.tensor_copy(mqraw[:, dc, :], ps[:, :]), reads=[ps], writes=[mqraw] if dc == 0 else (), pw=() if dc == 0 else [mqraw])
                    P.op("act", lambda e, sq=sq, ps=ps: e.activation(sq[:, :], ps[:, :], AF.Square), reads=[ps], writes=[sq])
                    P.op("pe", lambda e, sq=sq, dc=dc: e.matmul(PS[3][:, :], ones_bf, sq[:, :], start=(dc == 0), stop=(dc == 1)),
                         reads=[cbf, sq], writes=[PS[3]] if dc == 0 else (), pw=() if dc == 0 else [PS[3]])
            rstd_from_ss(PS[3], 512, 1.0 / 256, rs2)
            for dc in range(2):
                stt2(mqn, mqn[:, dc, :], mqraw, mqraw[:, dc, :], PR_MQN + dc, rs2, rs2[:, :], 512, dc == 0)
            for kc in range(2):
                ps = PS[1 + kc]
                for dc in range(2):
                    P.op("pe", lambda e, kc=kc, dc=dc, ps=ps: e.matmul(ps[:, :], kTm[:, dc, kc * 128:(kc + 1) * 128], mqn[:, dc, :], start=(dc == 0), stop=(dc == 1)),
                         reads=[kTm, mqn], writes=[ps] if dc == 0 else (), pw=() if dc == 0 else [ps])
                P.op("act", lambda e, kc=kc, ps=ps: e.activation(pTm[kc][:, :], ps[:, :], AF.Exp, scale=1.0 / 16.0), reads=[ps], writes=[pTm[kc]])
            for kc in range(2):
                P.op("pe", lambda e, kc=kc: e.matmul(PS[3][:, :], ones_bf, pTm[kc][:, :], start=(kc == 0), stop=(kc == 1)),
                     reads=[cbf, pTm[kc]], writes=[PS[3]] if kc == 0 else (), pw=() if kc == 0 else [PS[3]])
            P.op("dve", lambda e: e.reciprocal(rden[:, :], PS[3][:, :]), reads=[PS[3]], writes=[rden])
            for dvc in range(2):
                ps = PS[4 + dvc]
                for kc in range(2):
                    P.op("pe", lambda e, kc=kc, dvc=dvc, ps=ps: e.matmul(ps[:, :], vm[:, kc, dvc * 128:(dvc + 1) * 128], pTm[kc][:, :], start=(kc == 0), stop=(kc == 1)),
                         reads=[vm, pTm[kc]], writes=[ps] if kc == 0 else (), pw=() if kc == 0 else [ps])
                ob = obrot.get()
                P.op("dve", lambda e, ob=ob, ps=ps: e.tensor_tensor(ob[:, :], ps[:, :], rden[:, :], ALU.mult), reads=[ps, rden], writes=[ob])
                omine_write(512 + dvc * 128, t0, ob)
            for blk in range(4):
                ps = PS[0]
                for k in range(16):
                    P.op("pe", lambda e, k=k, blk=blk: e.matmul(ps[:, 0:262], hT[:, k, blk * 128:(blk + 1) * 128], w1sb[:, k * 262:(k + 1) * 262], start=(k == 0), stop=(k == 15)),
                         reads=[w1sb, hT], writes=[ps] if k == 0 else (), pw=() if k == 0 else [ps])
                v_s = vst.get(); s_s = sst.get()
                P.op("act", lambda e, v_s=v_s: e.copy(v_s[:, :], ps[:, 0:256]), reads=[ps], writes=[v_s])
                P.op("dve", lambda e, s_s=s_s: e.tensor_copy(s_s[:, :], ps[:, 256:262]), reads=[ps], writes=[s_s])
                r0 = t0 + blk * 128
                P.dma("sp", foxv[r0:r0 + 128, :], v_s[:, :], reads=[v_s], pw=[foxv])
                P.dma("sp", smallT[:, r0:r0 + 128].rearrange("c p -> p c"), s_s[:, :], reads=[s_s], pw=[smallT], allow_slow_non_contiguous=True)

    for _ in phase(2):
        kT = P.sbuf([128, S], BF16, "kT")
        V = P.sbuf([128, J, 128], BF16, "V")
        lf = P.sbuf([128, J], F32, "lf")
        lfJ = P.sbuf([J, 128], F32, "lfJ")
        totc = P.sbuf([J, 1], F32, "totc")
        totB = P.sbuf([J, 128], F32, "totB")
        cT = P.sbuf([128, J], F32, "cT")
        negc = P.sbuf([128, J], F32, "negc")
        cmidB = P.sbuf([128, J], F32, "cmidB")
        nbf = P.sbuf([128, 1], F32, "nbf")
        crow = P.sbuf([1, S], BF16, "crow")
        qrot = Rot(P, 2, [128, 512], BF16, "qg")
        prot = Rot(P, 3, [128, 512], BF16, "pT")
        rden = P.sbuf([128, 512], F32, "rden")
        obrot = Rot(P, 2, [128, 512], BF16, "ob")
        strot = [PS[2], PS[3], PS[4]]
        sti = 0
        for h in range(2):
            P.dma("sp", kT[:, :], foxk[h][:, :], reads=[foxk[h]], writes=[kT])
            P.dma("sp", V[:, :, :], foxv[:, h * 128:(h + 1) * 128].rearrange("(j p) c -> p j c", p=128), reads=[foxv], writes=[V])
            P.dma("sp", lf[:, :], smallT[h, :].rearrange("(j p) -> p j", p=128), reads=[smallT], writes=[lf], allow_slow_non_contiguous=True)
            P.dma("sp", lfJ[:, :], smallT[h, :].rearrange("(j p) -> j p", p=128), reads=[smallT], writes=[lfJ])
            P.op("dve", lambda e, h=h: e.tensor_scalar(nbf[:, :], prm[:, PR_BF + h:PR_BF + h + 1], -1.0, None, ALU.mult), reads=[prm], writes=[nbf])
            for tl, npart in ((lf, 128), (lfJ, J)):
                P.op("act", lambda e, tl=tl, npart=npart: e.activation(tl[:, :], tl[:, :], AF.Exp, bias=nbf[0:npart, 0:1], scale=-1.0), reads=[tl, nbf], writes=[tl])
                P.op("act", lambda e, tl=tl, npart=npart: e.activation(tl[:, :], tl[:, :], AF.Ln, bias=epsc[0:npart, 1:2], scale=1.0), reads=[tl, epsc], writes=[tl])
            P.op("dve", lambda e: e.reduce_sum(totc[:, :], lfJ[:, :], mybir.AxisListType.X), reads=[lfJ], writes=[totc])
            P.op("dve", lambda e: e.tensor_scalar(totB[:, :], cst[0:J, C_ONES:C_ONES + 128], totc[:, 0:1], None, ALU.mult), reads=[cst, totc], writes=[totB])
            ps = PS[5]
            P.op("pe", lambda e: e.matmul(ps[:, 0:J], cst[:, C_TRI:C_TRI + 128], lf[:, :], start=True, stop=False), reads=[cst, lf], writes=[ps])
            P.op("pe", lambda e: e.matmul(ps[:, 0:J], totB[:, :], cst[0:J, C_TRIS:C_TRIS + J], start=False, stop=True), reads=[cst, totB], pw=[ps])
            P.op("dve", lambda e: e.tensor_copy(negc[:, :], ps[:, 0:J]), reads=[ps], writes=[negc])
            P.op("dve", lambda e: e.tensor_scalar(cT[:, :], ps[:, 0:J], -1.0, None, ALU.mult), reads=[ps], writes=[cT])
            P.op("pe", lambda e: e.matmul(ps[:, 0:J], cst[:, C_SEL:C_SEL + 128], cT[:, :], start=True, stop=True), reads=[cst, cT], writes=[ps])
            P.op("dve", lambda e: e.tensor_copy(cmidB[:, :], ps[:, 0:J]), reads=[ps], writes=[cmidB])
            for i in range(J):
                P.op("dve", lambda e, i=i: e.tensor_scalar(crow[0:1, i * 128:(i + 1) * 128], cst[0:1, C_ONES:C_ONES + 128], cmidB[0:1, i:i + 1], 128.0 ** 0.5, ALU.mult, ALU.mult),
                     reads=[cst, cmidB], writes=[crow] if i == 0 else (), pw=() if i == 0 else [crow])
            for G in range(NTILE):
                qg = qrot.get()
                P.dma("sp", qg[:, :], foxq[h][:, G * 512:(G + 1) * 512], reads=[foxq[h]], writes=[qg])
                o_ps = PS[0]; d_ps = PS[1]
                order = list(range(4 * G, 4 * G + 4)) + list(range(0, 4 * G))
                for n, j in enumerate(order):
                    qlo = max(0, j - 4 * G)
                    c0 = qlo * 128
                    st = strot[sti]; sti = (sti + 1) % 3
                    P.op("pe", lambda e, j=j, c0=c0, st=st, qg=qg: e.matmul(st[:, c0:512], kT[:, j * 128:(j + 1) * 128], qg[:, c0:512], start=True, stop=False),
                         reads=[kT, qg], writes=[st])
                    P.op("pe", lambda e, c0=c0, st=st, G=G: e.matmul(st[:, c0:512], ones_bf[0:1, :], crow[0:1, G * 512 + c0:(G + 1) * 512], start=False, stop=True),
                         reads=[cbf, crow], pw=[st])
                    pT = prot.get()
                    P.op("act", lambda e, pT=pT, st=st, j=j, c0=c0: e.activation(pT[:, c0:512], st[:, c0:512], AF.Exp, bias=negc[:, j:j + 1], scale=128.0 ** -0.5),
                         reads=[st, negc], writes=[pT])
                    if j >= 4 * G:
                        P.op("pool", lambda e, pT=pT, c0=c0: e.tensor_tensor(pT[:, c0:c0 + 128], pT[:, c0:c0 + 128], tri_bf, ALU.mult), reads=[pT, cbf], writes=[pT])
                    first = (n == 0); last = (n == len(order) - 1)
                    P.op("pe", lambda e, j=j, c0=c0, pT=pT, first=first, last=last: e.matmul(o_ps[:, c0:512], V[:, j, :], pT[:, c0:512], start=first, stop=last, skip_group_check=True),
                         reads=[V, pT], writes=[o_ps] if first else (), pw=() if first else [o_ps])
                    P.op("pe", lambda e, c0=c0, pT=pT, first=first, last=last: e.matmul(d_ps[:, c0:512], ones_bf, pT[:, c0:512], start=first, stop=last, skip_group_check=True),
                         reads=[cbf, pT], writes=[d_ps] if first else (), pw=() if first else [d_ps])
                P.op("dve", lambda e: e.reciprocal(rden[:, :], d_ps[:, :]), reads=[d_ps], writes=[rden])
                ob = obrot.get()
                P.op("dve", lambda e, ob=ob: e.tensor_tensor(ob[:, :], o_ps[:, :], rden[:, :], ALU.mult), reads=[o_ps, rden], writes=[ob])
                omine_write(256 + h * 128, G * 512, ob)

    ag_done = [0]

    def ag_tile(Rq, t0):
        for o in range(0, 512, NT4):
            ct = (t0 + o) // NT4
            first = (ag_done[0] == 0)
            ag_done[0] += 1
            Rq.custom("pool", lambda e, sem, ct=ct: e.collective_compute("AllGather", ALU.bypass, replica_groups=[[0, 1, 2, 3], [4, 5, 6, 7]],
                                                                        ins=[omine.t[ct].opt()], outs=[gath.t[ct].opt()]).then_inc(sem, 1),
                      reads=[omine], writes=[gath] if first else (), pw=() if first else [gath])

    if 3 in PH:
      gdn_phase(P, nc, S, NTILE, NCH, PS, PSB, cst, cbf, prm, epsc, graw, smallT, omine_write, ag_tile if 4 in PH else None)

    pid = nc.partition_id()
    qid = pid % 4
    for _ in phase(5):
        xt = P.sbuf([128, 16, NT4], F32, "xt4")
        sqrot = Rot(P, 3, [128, 512], BF16, "sq")
        rs_t = P.sbuf([128, 512], F32, "rs")
        wrot = Rot(P, 6, [128, 2048], BF16, "w4")
        wurot = Rot(P, 4, [128, 1024], BF16, "wu4")
        w2rot = Rot(P, 3, [128, 4096], BF16, "wf2")
        stg = None
        sgrot = Rot(P, 2, [128, 512], F32, "sg")
        tmrot = Rot(P, 2, [128, 512], F32, "tm")
        yacc = P.sbuf([128, 512], F32, "yacc")
        orot = Rot(P, 2, [128, 512], F32, "o4")
        for sub in range(NSUB):
            c0 = sub * NT4
            P.dma("sp", xt[:, :, :], xq[:, c0:c0 + NT4].rearrange("(c p) n -> p c n", p=128), writes=[xt])
            with P.scope():
                hT = P.sbuf([128, 16, NT4], BF16, "h4")
                oT = P.sbuf([128, 24, NT4], BF16, "o4T")
                yT = P.sbuf([128, 16, NT4], BF16, "y4")
                rmsnorm_tile(xt, NT4, PR_GMIX, hT, sqrot, rs_t)
                src = gath.t[bass.ds(qid * NSUB + sub, 1), :, :].rearrange("o (r p) n -> p (o r) n", p=128)
                P.dma("sp", oT[:, :, :], src, reads=[gath], writes=[oT])
                for c in range(16):
                    for br in range(3):
                        wgb = wrot.get(); wub = wurot.get()
                        wload(wgb, wg[br * 16 + c], 2048, stg)
                        wload(wub, wup[br * 16 + c], 1024, stg)
                        gps = PS[1 + (br % 2)]; ups = PS[3 + (br % 2)]
                        for k in range(16):
                            P.op("pe", lambda e, k=k, wgb=wgb, gps=gps: e.matmul(gps[:, 0:NT4], wgb[:, k * 128:(k + 1) * 128], hT[:, k, :], start=(k == 0), stop=(k == 15)),
                                 reads=[wgb, hT], writes=[gps] if k == 0 else (), pw=() if k == 0 else [gps])
                        for k in range(8):
                            rk = (k // 2) * 6 + br * 2 + (k % 2)
                            P.op("pe", lambda e, k=k, rk=rk, wub=wub, ups=ups: e.matmul(ups[:, 0:NT4], wub[:, k * 128:(k + 1) * 128], oT[:, rk, :], start=(k == 0), stop=(k == 7)),
                                 reads=[wub, oT], writes=[ups] if k == 0 else (), pw=() if k == 0 else [ups])
                        sg = sgrot.get()
                        P.op("act", lambda e, sg=sg, gps=gps: e.activation(sg[:, 0:NT4], gps[:, 0:NT4], AF.Sigmoid), reads=[gps], writes=[sg])
                        if br == 0:
                            P.op("dve", lambda e, sg=sg, ups=ups: e.tensor_tensor(yacc[:, 0:NT4], sg[:, 0:NT4], ups[:, 0:NT4], ALU.mult), reads=[sg, ups], writes=[yacc])
                        else:
                            tm = tmrot.get()
                            P.op("dve", lambda e, sg=sg, ups=ups, tm=tm: e.tensor_tensor(tm[:, 0:NT4], sg[:, 0:NT4], ups[:, 0:NT4], ALU.mult), reads=[sg, ups], writes=[tm])
                            if br == 1:
                                P.op("dve", lambda e, tm=tm: e.tensor_tensor(yacc[:, 0:NT4], yacc[:, 0:NT4], tm[:, 0:NT4], ALU.add), reads=[tm, yacc], writes=[yacc])
                            else:
                                P.op("dve", lambda e, tm=tm, c=c: e.tensor_tensor(yT[:, c, :], yacc[:, 0:NT4], tm[:, 0:NT4], ALU.add), reads=[tm, yacc],
                                     writes=[yT] if c == 0 else (), pw=() if c == 0 else [yT])
                for c in range(16):
                    wb = wrot.get()
                    wload(wb, wout[c], 2048, stg)
                    ps = PS[1 + (c % 2)]
                    for k in range(16):
                        P.op("pe", lambda e, k=k, wb=wb, ps=ps: e.matmul(ps[:, 0:NT4], wb[:, k * 128:(k + 1) * 128], yT[:, k, :], start=(k == 0), stop=(k == 15)),
                             reads=[wb, yT], writes=[ps] if k == 0 else (), pw=() if k == 0 else [ps])
                    P.op("dve", lambda e, c=c, ps=ps: e.tensor_tensor(xt[:, c, :], xt[:, c, :], ps[:, 0:NT4], ALU.add), reads=[ps, xt], pw=[xt])
            with P.scope():
                h2 = P.sbuf([128, 16, NT4], BF16, "h24")
                uT = P.sbuf([128, 32, NT4], BF16, "u4")
                rrot = Rot(P, 2, [128, 512], F32, "r4")
                rmsnorm_tile(xt, NT4, PR_GMLP, h2, sqrot, rs_t)
                for half in range(2):
                    for fi in range(32):
                        f = half * 32 + fi
                        wb = wrot.get()
                        wload(wb, wff1[f], 2048, stg)
                        ps = PS[1 + (f % 2)]
                        for k in range(16):
                            P.op("pe", lambda e, k=k, wb=wb, ps=ps: e.matmul(ps[:, 0:NT4], wb[:, k * 128:(k + 1) * 128], h2[:, k, :], start=(k == 0), stop=(k == 15)),
                                 reads=[wb, h2], writes=[ps] if k == 0 else (), pw=() if k == 0 else [ps])
                        r = rrot.get()
                        P.op("act", lambda e, r=r, ps=ps: e.activation(r[:, 0:NT4], ps[:, 0:NT4], AF.Relu), reads=[ps], writes=[r])
                        P.op("dve", lambda e, r=r, fi=fi: e.tensor_tensor(uT[:, fi, :], r[:, 0:NT4], r[:, 0:NT4], ALU.mult), reads=[r],
                             writes=[uT] if fi == 0 else (), pw=() if fi == 0 else [uT])
                    for c in range(16):
                        wb = w2rot.get()
                        wload(wb, wff2[c][:, half * 4096:(half + 1) * 4096], 4096, stg)
                        ps = PS[3 + (c % 2)]
                        for fi in range(32):
                            P.op("pe", lambda e, fi=fi, wb=wb, ps=ps: e.matmul(ps[:, 0:NT4], wb[:, fi * 128:(fi + 1) * 128], uT[:, fi, :], start=(fi == 0), stop=(fi == 31)),
                                 reads=[wb, uT], writes=[ps] if fi == 0 else (), pw=() if fi == 0 else [ps])
                        if half == 0:
                            P.op("dve", lambda e, c=c, ps=ps: e.tensor_tensor(xt[:, c, :], xt[:, c, :], ps[:, 0:NT4], ALU.add), reads=[ps, xt], pw=[xt])
                        else:
                            o = orot.get()
                            P.op("dve", lambda e, c=c, ps=ps, o=o: e.tensor_tensor(o[:, 0:NT4], xt[:, c, :], ps[:, 0:NT4], ALU.add), reads=[ps, xt], writes=[o])
                            P.dma("sp", outT[c * 128:(c + 1) * 128, c0:c0 + NT4], o[:, 0:NT4], reads=[o], pw=[outT])
    fin = [outT]
    if dbg and 3 in PH:
        omine_o = P.dram("omine", [NCT, 768, NT4], BF16, "ExternalOutput")
        for ct in range(NCT):
            P.dma("sp", omine_o[ct], omine[ct], reads=[omine], writes=[omine_o] if ct == 0 else (), pw=() if ct == 0 else [omine_o])
        fin += [omine_o]
    if dbg:
        fin += [foxv, smallT] + graw + foxq + foxk
    P.wait_all("sp", fin)
    P.emit()
    return nc


class View:
    __slots__ = ("ap", "b")

    def __init__(self, b, ap):
        self.b = b; self.ap = ap

    def __getitem__(self, idx):
        return self.ap[idx]


class Rec:
    def __init__(self):
        self.l = []

    def op(self, eng, fn, reads=(), writes=(), pw=()):
        self.l.append(("op", eng, _freeze(fn), tuple(reads), tuple(writes), tuple(pw)))

    def dma(self, q, out_ap, in_ap, reads=(), writes=(), pw=(), **kw):
        self.l.append(("dma", q, out_ap, in_ap, tuple(reads), tuple(writes), tuple(pw), kw))

    def custom(self, q, fn, reads=(), writes=(), pw=()):
        self.l.append(("custom", q, _freeze(fn), tuple(reads), tuple(writes), tuple(pw)))


def replay(P, recs):
    n = max(len(r.l) for r in recs)
    for k in range(n):
        for r in recs:
            if k < len(r.l):
                it = r.l[k]
                if it[0] == "op":
                    P.op(it[1], it[2], reads=it[3], writes=it[4], pw=it[5])
                elif it[0] == "custom":
                    P.custom(it[1], it[2], reads=it[3], writes=it[4], pw=it[5])
                else:
                    P.dma(it[1], it[2], it[3], reads=it[4], writes=it[5], pw=it[6], **it[7])


def gdn_phase(P, nc, S, NTILE, NCH, PS, PSB, cst, cbf, prm, epsc, graw, smallT, omine_write, ag_tile=None):
    ident_bf = cbf[:, 0:128]; ones_bf = cbf[:, 384:512]
    I64 = cbf[0:64, 0:64]
    UT64 = cst[0:64, C_TRI:C_TRI + 64]
    ID64 = cst[0:64, C_ID:C_ID + 64]

    def head(R, hi):
        ss = PS[hi]
        X = PS[2 + 2 * hi]; Yb = PS[3 + 2 * hi]; PB = PSB[hi]
        A = View(X, X[:, 0:129]); B = View(X, X[0:64, 129:257]); C = View(X, X[0:64, 257:385]); Dp = View(X, X[0:64, 385:449])
        E1 = View(Yb, Yb[:, 0:64]); E2 = View(Yb, Yb[:, 64:128]); E3 = View(Yb, Yb[:, 128:256])
        F1 = View(Yb, Yb[0:64, 256:384]); F2 = View(Yb, Yb[0:64, 384:512])
        PB1 = View(PB, PB[0:64, 0:64]); PB2 = View(PB, PB[0:64, 128:384])
        gall = P.sbuf([64, NCH], F32, "gall"); ball = P.sbuf([64, NCH], F32, "ball")
        nea = P.sbuf([64, 1], F32, "nea")
        xb = [P.sbuf([128, 515], F32, f"xb{i}") for i in range(4)]
        y = P.sbuf([128, 512], F32, "gy"); tmp = [P.sbuf([128, 512], F32, "gtmp") for _ in range(2)]; sg = P.sbuf([128, 512], F32, "gsg")
        yc = P.sbuf([128, 512], F32, "gyc")
        sq = P.sbuf([128, 512], BF16, "gsq"); rs = P.sbuf([128, 512], F32, "grs")
        qh = P.sbuf([128, 512], BF16, "qh"); kh = P.sbuf([128, 512], BF16, "kh"); vb = P.sbuf([128, 512], BF16, "vb")
        zs = P.sbuf([128, 512], F32, "zs"); oraw = P.sbuf([128, 512], F32, "oraw"); on = P.sbuf([128, 512], F32, "on")
        ob = P.sbuf([128, 512], BF16, "gob")
        gB = P.sbuf([64, 128], F32, "gB"); bB = P.sbuf([64, 128], F32, "bB")
        gamRow = P.sbuf([128, 64], F32, "gamRow"); betaRow = P.sbuf([128, 64], F32, "betaRow"); aRow = P.sbuf([128, 64], F32, "aRow")
        gamCol = P.sbuf([64, 1], F32, "gamCol"); acol = P.sbuf([64, 1], F32, "acol"); rcol = P.sbuf([64, 1], F32, "rcol"); bacol = P.sbuf([64, 1], F32, "bacol")
        t0d = P.sbuf([64, 64], F32, "t0d"); d2 = P.sbuf([64, 128], F32, "d2")
        kbq = P.sbuf([128, 128], BF16, "kbq"); NQ = P.sbuf([64, 128], BF16, "NQ")
        PP = [P.sbuf([64, 128], BF16, f"PP{i}") for i in range(2)]
        Y = [P.sbuf([64, 64], BF16, f"Y{i}") for i in range(2)]
        RHSw = P.sbuf([64, 128], BF16, "RHSw"); kdec = P.sbuf([64, 128], BF16, "kdec"); RHSv = P.sbuf([64, 128], BF16, "RHSv")
        nwT = P.sbuf([128, 64], BF16, "nwT"); uc = P.sbuf([64, 128], F32, "uc"); qdT = P.sbuf([128, 64], BF16, "qdT")
        u = P.sbuf([64, 128], BF16, "u"); S32 = P.sbuf([128, 128], F32, "S32"); Sbf = P.sbuf([128, 128], BF16, "Sbf"); St = P.sbuf([128, 128], F32, "St")
        R.dma("sp", gall[:, :], smallT[4 + hi, :].rearrange("(n p) -> p n", p=64), reads=[smallT], writes=[gall], allow_slow_non_contiguous=True)
        R.dma("sp", ball[:, :], smallT[2 + hi, :].rearrange("(n p) -> p n", p=64), reads=[smallT], writes=[ball], allow_slow_non_contiguous=True)
        R.op("act", lambda e: e.activation(nea[:, :], prm[0:64, PR_ALOG + hi:PR_ALOG + hi + 1], AF.Exp), reads=[prm], writes=[nea])
        R.op("dve", lambda e: e.tensor_scalar(nea[:, :], nea[:, :], -1.0, None, ALU.mult), reads=[nea], writes=[nea])
        R.op("act", lambda e: e.activation(gall[:, :], gall[:, :], AF.Exp, bias=prm[0:64, PR_DTB + hi:PR_DTB + hi + 1], scale=1.0), reads=[gall, prm], writes=[gall])
        R.op("act", lambda e: e.activation(gall[:, :], gall[:, :], AF.Ln, bias=epsc[0:64, 1:2], scale=1.0), reads=[gall, epsc], writes=[gall])
        R.op("dve", lambda e: e.tensor_scalar(gall[:, :], gall[:, :], nea[:, 0:1], None, ALU.mult), reads=[gall, nea], writes=[gall])
        R.op("act", lambda e: e.activation(ball[:, :], ball[:, :], AF.Sigmoid), reads=[ball], writes=[ball])
        R.op("pool", lambda e: e.memset(S32[:, :], 0.0), writes=[S32])
        R.op("pool", lambda e: e.memset(Sbf[:, :], 0.0), writes=[Sbf])
        for t in range(NTILE):
            c0t = t * 512
            for gi in range(4):
                src = graw[gi * 2 + hi]
                if t == 0:
                    R.op("pool", lambda e, gi=gi: e.memset(xb[gi][:, 0:3], 0.0), writes=[xb[gi]])
                    R.dma("sp", xb[gi][:, 3:515], src[:, 0:512], reads=[src], pw=[xb[gi]])
                else:
                    R.dma("sp", xb[gi][:, 0:515], src[:, c0t - 3:c0t + 512], reads=[src], writes=[xb[gi]])
            for gi in range(3):
                wc = PR_CONV + (gi * 2 + hi) * 4
                R.op("act", lambda e, gi=gi, wc=wc: e.activation(y[:, :], xb[gi][:, 0:512], AF.Copy, scale=prm[:, wc:wc + 1]), reads=[xb[gi], prm], writes=[y])
                for i in range(1, 4):
                    tm = tmp[i % 2]
                    R.op("act", lambda e, gi=gi, wc=wc, i=i, tm=tm: e.activation(tm[:, :], xb[gi][:, i:i + 512], AF.Copy, scale=prm[:, wc + i:wc + i + 1]), reads=[xb[gi], prm], writes=[tm])
                    R.op("dve", lambda e, tm=tm: e.tensor_tensor(y[:, :], y[:, :], tm[:, :], ALU.add), reads=[y, tm], writes=[y])
                R.op("act", lambda e: e.activation(sg[:, :], y[:, :], AF.Sigmoid), reads=[y], writes=[sg])
                if gi == 2:
                    R.op("dve", lambda e: e.tensor_tensor(vb[:, :], y[:, :], sg[:, :], ALU.mult), reads=[y, sg], writes=[vb])
                else:
                    R.op("dve", lambda e: e.tensor_tensor(yc[:, :], y[:, :], sg[:, :], ALU.mult), reads=[y, sg], writes=[yc])
                    R.op("act", lambda e: e.activation(sq[:, :], yc[:, :], AF.Square), reads=[yc], writes=[sq])
                    R.op("pe", lambda e: e.matmul(ss[:, :], ones_bf, sq[:, :], start=True, stop=True), reads=[cbf, sq], writes=[ss])
                    R.op("act", lambda e: e.activation(rs[:, :], ss[:, :], AF.Sqrt, bias=epsc[:, 0:1], scale=1.0), reads=[ss, epsc], writes=[rs])
                    R.op("dve", lambda e: e.reciprocal(rs[:, :], rs[:, :]), reads=[rs], writes=[rs])
                    if gi == 0:
                        R.op("dve", lambda e: e.tensor_scalar(rs[:, :], rs[:, :], 128.0 ** -0.5, None, ALU.mult), reads=[rs], writes=[rs])
                    dst = qh if gi == 0 else kh
                    R.op("dve", lambda e, dst=dst: e.tensor_tensor(dst[:, :], yc[:, :], rs[:, :], ALU.mult), reads=[yc, rs], writes=[dst])
            R.op("act", lambda e: e.activation(sg[:, :], xb[3][:, 3:515], AF.Sigmoid), reads=[xb[3]], writes=[sg])
            R.op("dve", lambda e: e.tensor_tensor(zs[:, :], xb[3][:, 3:515], sg[:, :], ALU.mult), reads=[xb[3], sg], writes=[zs])
            for ci in range(8):
                n = t * 8 + ci
                c0 = ci * 64
                R.op("act", lambda e, n=n: e.activation(gB[:, :], cst[0:64, C_ONES:C_ONES + 128], AF.Copy, scale=gall[:, n:n + 1]), reads=[cst, gall], writes=[gB])
                R.op("act", lambda e, n=n: e.activation(bB[:, :], cst[0:64, C_ONES:C_ONES + 128], AF.Copy, scale=ball[:, n:n + 1]), reads=[cst, ball], writes=[bB])
                R.op("pe", lambda e: e.matmul(A[:, 0:64], gB[:, :], UT64, start=True, stop=True), reads=[gB, cst], writes=[A])
                R.op("pe", lambda e: e.matmul(A[:, 64:128], bB[:, :], ID64, start=True, stop=True), reads=[bB, cst], pw=[A])
                R.op("pe", lambda e, n=n: e.matmul(A[0:64, 128:129], UT64, gall[:, n:n + 1], start=True, stop=True), reads=[gall, cst], pw=[A])
                R.op("dve", lambda e: e.tensor_copy(gamRow[:, :], A[:, 0:64]), reads=[A], writes=[gamRow])
                R.op("dve", lambda e: e.tensor_copy(betaRow[:, :], A[:, 64:128]), reads=[A], writes=[betaRow])
                R.op("dve", lambda e: e.tensor_copy(gamCol[:, :], A[0:64, 128:129]), reads=[A], writes=[gamCol])
                R.op("act", lambda e: e.activation(aRow[:, :], gamRow[:, :], AF.Exp), reads=[gamRow], writes=[aRow])
                R.op("act", lambda e: e.activation(acol[:, :], gamCol[:, :], AF.Exp), reads=[gamCol], writes=[acol])
                R.op("act", lambda e: e.activation(rcol[:, :], gamCol[:, :], AF.Exp, bias=gamRow[0:64, 63:64], scale=-1.0), reads=[gamCol, gamRow], writes=[rcol])
                R.op("dve", lambda e, n=n: e.tensor_tensor(bacol[:, :], acol[:, :], ball[:, n:n + 1], ALU.mult), reads=[acol, ball], writes=[bacol])
                R.op("dve", lambda e: e.tensor_scalar(t0d[:, :], gamRow[0:64, :], gamCol[:, 0:1], None, ALU.subtract), reads=[gamRow, gamCol], writes=[t0d])
                R.op("dve", lambda e: e.tensor_tensor(d2[:, 0:64], t0d[:, :], cst[0:64, C_NM:C_NM + 64], ALU.add), reads=[t0d, cst], writes=[d2])
                R.op("dve", lambda e: e.tensor_tensor(d2[:, 64:128], t0d[:, :], cst[0:64, C_NM + 64:C_NM + 128], ALU.add), reads=[t0d, cst], pw=[d2])
                R.op("act", lambda e: e.activation(d2[:, :], d2[:, :], AF.Exp), reads=[d2], writes=[d2])
                R.op("dve", lambda e, c0=c0: e.tensor_tensor(kbq[:, 0:64], kh[:, c0:c0 + 64], betaRow[:, :], ALU.mult), reads=[kh, betaRow], writes=[kbq])
                R.op("act", lambda e, c0=c0: e.copy(kbq[:, 64:128], qh[:, c0:c0 + 64]), reads=[qh], pw=[kbq])
                R.op("pe", lambda e, c0=c0: e.matmul(B[:, :], kh[:, c0:c0 + 64], kbq[:, :], start=True, stop=True), reads=[kh, kbq], writes=[B])
                R.op("dve", lambda e: e.tensor_tensor(NQ[:, :], B[:, :], d2[:, :], ALU.mult), reads=[B, d2], writes=[NQ])
                R.op("pe", lambda e: e.transpose(PB1[:, :], NQ[:, 0:64], I64), reads=[NQ, cbf], writes=[PB1])
                R.op("dve", lambda e: e.tensor_copy(PP[0][:, 64:128], PB1[:, :]), reads=[PB1], writes=[PP[0]])
                R.op("act", lambda e: e.copy(PP[0][:, 0:64], NQ[:, 0:64]), reads=[NQ], pw=[PP[0]])
                R.op("dve", lambda e: e.tensor_tensor(Y[0][:, :], I64, NQ[:, 0:64], ALU.subtract), reads=[cbf, NQ], writes=[Y[0]])
                pk = 0; yk = 0
                for lvl in range(5):
                    R.op("pe", lambda e, pk=pk: e.matmul(C[:, 0:64], PP[pk][:, 64:128], PP[pk][:, 0:64], start=True, stop=True), reads=[PP[pk]], writes=[C])
                    R.op("pe", lambda e, pk=pk: e.matmul(C[:, 64:128], PP[pk][:, 0:64], PP[pk][:, 64:128], start=True, stop=True), reads=[PP[pk]], pw=[C])
                    R.op("dve", lambda e, pk=pk: e.tensor_copy(PP[1 - pk][:, :], C[:, :]), reads=[C], writes=[PP[1 - pk]])
                    pk = 1 - pk
                    R.op("pe", lambda e, pk=pk, yk=yk: e.matmul(Dp[:, :], PP[pk][:, 64:128], Y[yk][:, :], start=True, stop=True), reads=[PP[pk], Y[yk]], writes=[Dp])
                    R.op("dve", lambda e, yk=yk: e.tensor_tensor(Y[1 - yk][:, :], Dp[:, :], Y[yk][:, :], ALU.add), reads=[Dp, Y[yk]], writes=[Y[1 - yk]])
                    yk = 1 - yk
                Tt = Y[yk]
                R.op("pe", lambda e, c0=c0: e.transpose(PB2[:, 0:128], kh[:, c0:c0 + 64], ident_bf), reads=[kh, cbf], writes=[PB2])
                R.op("pe", lambda e, c0=c0: e.transpose(PB2[:, 128:256], vb[:, c0:c0 + 64], ident_bf), reads=[vb, cbf], pw=[PB2])
                R.op("dve", lambda e: e.tensor_scalar(RHSw[:, :], PB2[:, 0:128], bacol[:, 0:1], None, ALU.mult), reads=[PB2, bacol], writes=[RHSw])
                R.op("dve", lambda e: e.tensor_scalar(kdec[:, :], PB2[:, 0:128], rcol[:, 0:1], None, ALU.mult), reads=[PB2, rcol], writes=[kdec])
                R.op("dve", lambda e, n=n: e.tensor_scalar(RHSv[:, :], PB2[:, 128:256], ball[:, n:n + 1], None, ALU.mult), reads=[PB2, ball], writes=[RHSv])
                R.op("pe", lambda e, Tt=Tt: e.matmul(E1[:, :], RHSw[:, :], Tt[:, :], start=True, stop=True), reads=[RHSw, Tt], writes=[E1])
                R.op("dve", lambda e: e.tensor_scalar(nwT[:, :], E1[:, :], -1.0, None, ALU.mult), reads=[E1], writes=[nwT])
                R.op("pe", lambda e, Tt=Tt: e.matmul(F1[:, :], Tt[:, :], RHSv[:, :], start=True, stop=True), reads=[RHSv, Tt], writes=[F1])
                R.op("dve", lambda e: e.tensor_copy(uc[:, :], F1[:, :]), reads=[F1], writes=[uc])
                R.op("dve", lambda e, c0=c0: e.tensor_tensor(qdT[:, :], qh[:, c0:c0 + 64], aRow[:, :], ALU.mult), reads=[qh, aRow], writes=[qdT])
                R.op("pe", lambda e: e.matmul(F2[:, :], nwT[:, :], Sbf[:, :], start=True, stop=True), reads=[nwT, Sbf], writes=[F2])
                R.op("dve", lambda e: e.tensor_tensor(u[:, :], F2[:, :], uc[:, :], ALU.add), reads=[F2, uc], writes=[u])
                R.op("pe", lambda e: e.matmul(E2[:, :], Sbf[:, :], qdT[:, :], start=True, stop=False), reads=[Sbf, qdT], writes=[E2])
                R.op("pe", lambda e: e.matmul(E2[:, :], u[:, :], NQ[:, 64:128], start=False, stop=True), reads=[u, NQ], pw=[E2])
                R.op("dve", lambda e, c0=c0: e.tensor_copy(oraw[:, c0:c0 + 64], E2[:, :]), reads=[E2], writes=[oraw] if ci == 0 else (), pw=() if ci == 0 else [oraw])
                R.op("pe", lambda e: e.matmul(E3[:, :], kdec[:, :], u[:, :], start=True, stop=True), reads=[kdec, u], writes=[E3])
                R.op("act", lambda e: e.activation(St[:, :], S32[:, :], AF.Copy, scale=aRow[:, 63:64]), reads=[S32, aRow], writes=[St])
                R.op("dve", lambda e: e.tensor_tensor(S32[:, :], St[:, :], E3[:, :], ALU.add), reads=[St, E3], writes=[S32])
                R.op("act", lambda e: e.copy(Sbf[:, :], S32[:, :]), reads=[S32], writes=[Sbf])
            R.op("act", lambda e: e.activation(sq[:, :], oraw[:, :], AF.Square), reads=[oraw], writes=[sq])
            R.op("pe", lambda e: e.matmul(ss[:, :], ones_bf, sq[:, :], start=True, stop=True), reads=[cbf, sq], writes=[ss])
            R.op("act", lambda e: e.activation(rs[:, :], ss[:, :], AF.Sqrt, bias=epsc[:, 0:1], scale=1.0 / 128), reads=[ss, epsc], writes=[rs])
            R.op("dve", lambda e: e.reciprocal(rs[:, :], rs[:, :]), reads=[rs], writes=[rs])
            R.op("dve", lambda e: e.tensor_tensor(on[:, :], oraw[:, :], rs[:, :], ALU.mult), reads=[oraw, rs], writes=[on])
            R.op("dve", lambda e: e.tensor_scalar(on[:, :], on[:, :], prm[:, PR_GDNG:PR_GDNG + 1], None, ALU.mult), reads=[on, prm], writes=[on])
            R.op("dve", lambda e: e.tensor_tensor(ob[:, :], on[:, :], zs[:, :], ALU.mult), reads=[on, zs], writes=[ob])
            omine_write(hi * 128, c0t, ob, R)
            if hi == 1 and ag_tile is not None:
                ag_tile(R, c0t)

    with P.scope():
        recs = []
        for hi in range(2):
            R = Rec()
            head(R, hi)
            recs.append(R)
        replay(P, recs)


_CACHE = {}


def prep_inputs(I, S, PH=(0, 1, 2, 3, 4, 5)):
    w_in = I["w_in"][0]
    TQ = S // 4
    cst = make_consts()
    gates0 = 8216
    big = {}
    if 5 in PH:
        big["wg"] = np.concatenate([relay(w_in[:, gates0 + br * D:gates0 + (br + 1) * D]) for br in range(3)], axis=0)
        big["wup"] = np.concatenate([relay(I[n][0]) for n in ("w_up_gdn", "w_up_fox", "w_up_mem")], axis=0)
        big["wout"] = relay(I["w_out"][0]); big["wff1"] = relay(I["w_ff1"][0]); big["wff2"] = relay(I["w_ff2"][0])
    maps = []
    for core in range(8):
        b, g = core // 4, core % 4
        hs = [2 * g, 2 * g + 1]
        xTb = np.ascontiguousarray(I["x"][b, :S].T)
        cols = []
        for base in (0, 1024, 2048, 3072, 4112, 5136):
            for h in hs:
                cols.append(np.arange(base + h * 128, base + (h + 1) * 128))
        cols.append(np.arange(7192 + g * 256, 7192 + (g + 1) * 256))
        w1 = relay(w_in[:, np.concatenate(cols)])
        scol = np.concatenate([np.arange(6160 + h * 128, 6160 + (h + 1) * 128) for h in hs] + [np.array([7184 + h for h in hs]),
                              np.array([4096 + h for h in hs]), np.array([4104 + h for h in hs])])
        w1s = relay_tok(w_in[:, scol])
        wmkv = I["w_mem_kv"][0]
        wmk = relay(wmkv[:, g * 256:(g + 1) * 256])
        wmv = relay_tok(wmkv[:, 1024 + g * 256:1024 + (g + 1) * 256])
        prm = np.zeros((128, NPRM), np.float32)
        prm[:, PR_GMIX:PR_GMIX + 16] = I["g_mix"][0].reshape(16, 128).T
        prm[:, PR_GMLP:PR_GMLP + 16] = I["g_mlp"][0].reshape(16, 128).T
        prm[:, PR_GMEM:PR_GMEM + 16] = I["g_mem"][0].reshape(16, 128).T
        cw = I["conv_w"][0]
        for gi, base in enumerate((0, 1024, 2048)):
            for hi, h in enumerate(hs):
                prm[:, PR_CONV + (gi * 2 + hi) * 4:PR_CONV + (gi * 2 + hi) * 4 + 4] = cw[:, base + h * 128:base + (h + 1) * 128].T
        prm[:, PR_GDNG] = I["gdn_norm_g"][0]
        prm[:, PR_FQN] = I["fox_q_norm"][0]; prm[:, PR_FKN] = I["fox_k_norm"][0]
        prm[:, PR_MQN:PR_MQN + 2] = I["mem_q_norm"][0].reshape(2, 128).T
        prm[:, PR_MKN:PR_MKN + 2] = I["mem_k_norm"][0].reshape(2, 128).T
        for hi, h in enumerate(hs):
            prm[:, PR_ALOG + hi] = I["a_log"][0, h]; prm[:, PR_DTB + hi] = I["dt_bias"][0, h]; prm[:, PR_BF + hi] = I["fox_b_f"][0, h]
        maps.append({
            "xT": xTb, "xq": np.ascontiguousarray(xTb[:, g * TQ:(g + 1) * TQ]), "memT": np.ascontiguousarray(I["mem"][b].T),
            "prm": prm, "cst": cst, "w1": w1, "w1s": w1s, "wmk": wmk, "wmv": wmv, **big,
        })
    return maps


def run(I, S, dbg=False, PH=(0, 1, 2, 3, 4, 5)):
    key = (S, dbg, PH)
    if key not in _CACHE:
        _CACHE[key] = build(S, dbg, PH)
    nc = _CACHE[key]
    maps = prep_inputs(I, S, PH)
    res = run_bass_kernel_spmd(nc, maps, core_ids=list(range(8)))
    return res


def kernel(**inputs):
    I = {k: np.asarray(v) for k, v in inputs.items()}
    S = I["x"].shape[1]
    res = run(I, S)
    TQ = S // 4
    out = np.empty((2, S, D), np.float32)
    for core in range(8):
        b, g = core // 4, core % 4
        out[b, g * TQ:(g + 1) * TQ, :] = res.results[core]["outT"].T
    return out
```

```python
from contextlib import ExitStack, contextmanager
import types
import os
import numpy as np
import concourse.bass as bass
import concourse.mybir as mybir
from concourse.bass_utils import run_bass_kernel_spmd

F32 = mybir.dt.float32
BF16 = mybir.dt.bfloat16
AF = mybir.ActivationFunctionType
ALU = mybir.AluOpType

ENGS = ("pe", "act", "dve", "pool", "sp")
SAME_ENG_SYNC = bool(int(os.environ.get("KSES", "1")))
NDMA_SEM = 10
EPS = 1e-6
D = 2048
DFF = 8192
NEG = -30000.0
ARENA_F32 = 47 * 1024


class T:
    __slots__ = ("t", "name", "lws", "rd", "excl")

    def __init__(self, t, name, fence=None):
        self.t = t
        self.name = name
        self.excl = False
        self.lws = {}
        self.rd = dict(fence) if fence else {}

    def __getitem__(self, idx):
        return self.t[idx]


class Ev:
    __slots__ = ("kind", "key", "idx", "op")

    def __init__(self, kind, key, idx, op=None):
        self.kind = kind; self.key = key; self.idx = idx; self.op = op


class Op:
    __slots__ = ("eng", "fn", "waits", "mark", "cnt")

    def __init__(self, eng, fn):
        self.eng = eng; self.fn = fn; self.waits = []; self.mark = False; self.cnt = 0


def _freeze(fn):
    if fn is None or fn.__closure__ is None:
        return fn
    cells = []
    for c in fn.__closure__:
        try:
            cells.append(types.CellType(c.cell_contents))
        except ValueError:
            cells.append(c)
    return types.FunctionType(fn.__code__, fn.__globals__, fn.__name__, fn.__defaults__, tuple(cells))


def _merge(d, ev):
    o = d.get(ev.key)
    if o is None or o.idx < ev.idx:
        d[ev.key] = ev


class Prog:
    def __init__(self, nc):
        self.nc = nc
        self.es = ExitStack()
        self.ops = {e: [] for e in ENGS}
        self.seen = {e: {} for e in ENGS}
        self.esem = {e: self.es.enter_context(nc.semaphore("s_" + e)) for e in ENGS}
        self.dsem = {}
        self.dcount = {}
        self.dnext = {}
        for q in ("sp", "pool", "act"):
            self.dsem[q] = [self.es.enter_context(nc.semaphore(f"d_{q}{i}")) for i in range(NDMA_SEM)]
            self.dcount[q] = [0] * NDMA_SEM
            self.dnext[q] = 0
        self.csem = {}
        self.arena = None
        self.sp = 0
        self.sp_max = 0
        self.last_ev = {}
        self.ntile = 0
        self.fence = {}
        self.scopes = []

    def sbuf(self, shape, dt, name=None):
        self.ntile += 1
        name = (name or "t") + f"_{self.ntile}"
        if self.arena is None:
            self.arena = self.es.enter_context(self.nc.sbuf_tensor("arena", [128, ARENA_F32], F32))
        shape = list(shape)
        esz = 4 if dt == F32 else 2
        nfree = 1
        for d in shape[1:]:
            nfree *= d
        nbytes = (nfree * esz + 31) // 32 * 32
        off = self.sp
        self.sp += nbytes
        assert self.sp <= ARENA_F32 * 4, f"SBUF arena overflow: {self.sp} > {ARENA_F32 * 4} ({name})"
        self.sp_max = max(self.sp_max, self.sp)
        v = self.arena[0:shape[0], off // 4:(off + nbytes) // 4]
        if dt != F32:
            v = v.bitcast(dt)
        v = v[:, 0:nfree]
        if len(shape) == 3:
            v = v.rearrange("p (a b) -> p a b", b=shape[2])
        t = T(v, name, self.fence)
        if self.scopes:
            self.scopes[-1][1].append(t)
        return t

    def psum(self, shape, dt, name=None):
        self.ntile += 1
        name = (name or "p") + f"_{self.ntile}"
        t = T(self.es.enter_context(self.nc.psum_tensor(name, list(shape), dt)), name)
        t.excl = True
        return t

    def dram(self, name, shape, dt, kind="Internal"):
        return T(self.nc.dram_tensor(name, list(shape), dt, kind=kind), name)

    @contextmanager
    def scope(self):
        tiles = []
        self.scopes.append((None, tiles))
        sp0 = self.sp
        try:
            yield
        finally:
            self.scopes.pop()
            f = dict(self.fence)
            for t in tiles:
                for ev in t.lws.values():
                    _merge(f, ev)
                for ev in t.rd.values():
                    _merge(f, ev)
            self.fence = f
            self.sp = sp0

    def _need(self, eng, ev, op):
        if ev.kind == "e":
            if ev.key[1] == eng and (eng == "pe" or not SAME_ENG_SYNC):
                return
            if self.seen[eng].get(ev.key, -1) >= ev.idx:
                return
            self.seen[eng][ev.key] = ev.idx
            ev.op.mark = True
            op.waits.append(ev)
        else:
            if self.seen[eng].get(ev.key, -1) >= ev.idx:
                return
            self.seen[eng][ev.key] = ev.idx
            op.waits.append(ev)

    def _deps(self, eng, op, reads, writes, pw):
        reads = [getattr(t, "b", t) for t in reads]; writes = [getattr(t, "b", t) for t in writes]; pw = [getattr(t, "b", t) for t in pw]
        need = {}
        for t in reads:
            for ev in t.lws.values():
                _merge(need, ev)
            if t.excl:
                for ev in t.rd.values():
                    if ev.kind != "e" or ev.key[1] != eng:
                        _merge(need, ev)
        for t in writes:
            for ev in t.lws.values():
                _merge(need, ev)
            for ev in t.rd.values():
                _merge(need, ev)
        for t in pw:
            for ev in t.rd.values():
                _merge(need, ev)
        for ev in need.values():
            self._need(eng, ev, op)

    def _record(self, ev, reads, writes, pw):
        reads = [getattr(t, "b", t) for t in reads]; writes = [getattr(t, "b", t) for t in writes]; pw = [getattr(t, "b", t) for t in pw]
        for t in reads:
            _merge(t.rd, ev)
        for t in writes:
            t.lws = {ev.key: ev}
            t.rd = {}
        for t in pw:
            _merge(t.lws, ev)

    def _cut(self):
        self.nops = getattr(self, "nops", 0) + 1
        if os.environ.get("KLIST"):
            import inspect
            fr = inspect.stack()[2]
            print("OP", self.nops, inspect.stack()[1].function, fr.lineno, (fr.code_context or [""])[0].strip()[:110])
        return self.nops > int(os.environ.get("KCUT", "1000000000"))

    def op(self, eng, fn, reads=(), writes=(), pw=()):
        if self._cut():
            return
        o = Op(eng, _freeze(fn))
        self._deps(eng, o, reads, writes, pw)
        idx = len(self.ops[eng])
        self.ops[eng].append(o)
        ev = Ev("e", ("e", eng), idx, o)
        self.last_ev[eng] = ev
        self._record(ev, reads, writes, pw)
        return o

    def dma(self, q, out_ap, in_ap, reads=(), writes=(), pw=(), **kw):
        if self._cut():
            return
        o = Op(q, None)
        self._deps(q, o, reads, writes, pw)
        si = self.dnext[q]
        self.dnext[q] = (si + 1) % NDMA_SEM
        prev = self.dcount[q][si]
        key = ("d", q, si)
        if prev > 0:
            self._need(q, Ev("d", key, prev), o)
        val = prev + 1
        self.dcount[q][si] = val
        sem = self.dsem[q][si]
        o.fn = lambda e: e.dma_start(out=out_ap, in_=in_ap, **kw).then_inc(sem, 16)
        self.ops[q].append(o)
        self._record(Ev("d", key, val), reads, writes, pw)

    def custom(self, q, fn, reads=(), writes=(), pw=()):
        o = Op(q, None)
        self._deps(q, o, reads, writes, pw)
        sem = self.es.enter_context(self.nc.semaphore(f"c_{len(self.csem)}"))
        key = ("c", len(self.csem))
        self.csem[key] = sem
        fn = _freeze(fn)
        o.fn = lambda e: fn(e, sem)
        self.ops[q].append(o)
        self._record(Ev("d", key, 1), reads, writes, pw)

    def wait_all(self, eng, tiles):
        o = Op(eng, None)
        need = {}
        for t in tiles:
            for ev in t.lws.values():
                _merge(need, ev)
            for ev in t.rd.values():
                _merge(need, ev)
        for ev in self.last_ev.values():
            _merge(need, ev)
        for q in self.dcount:
            for si, cnt in enumerate(self.dcount[q]):
                if cnt > 0:
                    _merge(need, Ev("d", ("d", q, si), cnt))
        for ev in need.values():
            self._need(eng, ev, o)
        self.ops[eng].append(o)

    def emit(self):
        nc = self.nc
        for e in ENGS:
            c = 0
            for o in self.ops[e]:
                if o.mark:
                    c += 1
                    o.cnt = c
        P = self

        def run(eng_name, e):
            for o in P.ops[eng_name]:
                for ev in o.waits:
                    if ev.kind == "e":
                        e.wait_ge(P.esem[ev.key[1]], ev.op.cnt)
                    elif ev.key[0] == "c":
                        e.wait_ge(P.csem[ev.key], 1)
                    else:
                        e.wait_ge(P.dsem[ev.key[1]][ev.key[2]], 16 * ev.idx)
                if o.fn is None:
                    continue
                r = o.fn(e)
                if o.mark:
                    r.then_inc(P.esem[eng_name], 1)

        with nc.Block() as block:
            @block.tensor
            def _(e):
                run("pe", e)

            @block.scalar
            def _(e):
                run("act", e)

            @block.vector
            def _(e):
                run("dve", e)

            @block.gpsimd
            def _(e):
                run("pool", e)

            @block.sync
            def _(e):
                run("sp", e)
        self.es.close()


class Rot:
    def __init__(self, P, n, shape, dt, name):
        self.b = [P.sbuf(shape, dt, name) for _ in range(n)]
        self.i = 0

    def get(self):
        t = self.b[self.i]
        self.i = (self.i + 1) % len(self.b)
        return t


PR_GMIX, PR_GMLP, PR_GMEM = 0, 16, 32
PR_CONV = 48
PR_GDNG = 72
PR_FQN, PR_FKN = 73, 74
PR_MQN, PR_MKN = 75, 77
PR_ALOG, PR_DTB, PR_BF = 79, 81, 83
NPRM = 85

C_ID, C_TRI, C_TRIS, C_ONES, C_SEL, C_NM = 0, 128, 256, 384, 512, 640
NCST = 768


def make_consts():
    c = np.zeros((128, NCST), np.float32)
    i = np.arange(128)
    c[:, C_ID:C_ID + 128] = np.eye(128)
    c[:, C_TRI:C_TRI + 128] = (i[:, None] <= i[None, :])
    c[:, C_TRIS:C_TRIS + 128] = (i[:, None] < i[None, :])
    c[:, C_ONES:C_ONES + 128] = 1.0
    c[64, C_SEL:C_SEL + 128] = 1.0
    j = np.arange(64)
    nm = np.zeros((128, 128), np.float32)
    nm[:64, 0:64] = np.where(j[None, :] > j[:, None], 0.0, NEG)
    nm[:64, 64:128] = np.where(j[None, :] >= j[:, None], 0.0, NEG)
    c[:, C_NM:C_NM + 128] = nm
    return c


def relay(W):
    K, N = W.shape
    return np.ascontiguousarray(W.reshape(K // 128, 128, N // 128, 128).transpose(2, 1, 0, 3)).reshape(N // 128, 128, K)


def relay_tok(W):
    K, n = W.shape
    return np.ascontiguousarray(W.reshape(K // 128, 128, n).transpose(1, 0, 2)).reshape(128, (K // 128) * n)


def build(S, dbg=False, PH=(0, 1, 2, 3, 4, 5)):
    nc = bass.Bass("TRN2", target_bir_lowering=False)
    NTILE = S // 512
    J = S // 128
    NCH = S // 64
    TQ = S // 4
    NT4 = min(512, TQ)
    NSUB = TQ // NT4

    def din(name, shape, dt=F32):
        return T(nc.dram_tensor(name, list(shape), dt, kind="ExternalInput"), name)

    xT = din("xT", [D, S]); xq = din("xq", [D, TQ]); memT = din("memT", [D, 256])
    prm_d = din("prm", [128, NPRM]); cst_d = din("cst", [128, NCST])
    w1 = din("w1", [14, 128, D]); w1s = din("w1s", [128, 16 * 262])
    wmk = din("wmk", [2, 128, D]); wmv = din("wmv", [128, 16 * 256])
    if 5 in PH:
        wg = din("wg", [48, 128, D]); wup = din("wup", [48, 128, 1024])
        wout = din("wout", [16, 128, D]); wff1 = din("wff1", [64, 128, D]); wff2 = din("wff2", [16, 128, DFF])
    outT = T(nc.dram_tensor("outT", [D, TQ], F32, kind="ExternalOutput"), "outT")

    P = Prog(nc)
    skind = "ExternalOutput" if dbg else "Internal"
    graw = [P.dram(f"graw{i}", [128, S], F32, skind) for i in range(8)]
    foxq = [P.dram(f"foxq{i}", [128, S], BF16, skind) for i in range(2)]
    foxk = [P.dram(f"foxk{i}", [128, S], BF16, skind) for i in range(2)]
    foxv = P.dram("foxv", [S, 256], BF16, skind)
    smallT = P.dram("smallT", [6, S], F32, skind)
    NCT = S // NT4
    omine = P.dram("omine_i", [NCT, 768, NT4], BF16)
    gath = P.dram("gath", [NCT, 4 * 768, NT4], BF16)

    def omine_write(row0, t0, ob, Dq=None):
        Dq = Dq or P
        for o in range(0, 512, NT4):
            ct = (t0 + o) // NT4
            Dq.dma("sp", omine[ct, row0:row0 + 128, :], ob[:, o:o + NT4], reads=[ob], pw=[omine])

    cst = P.sbuf([128, NCST], F32, "cst")
    prm = P.sbuf([128, NPRM], F32, "prm")
    cbf = P.sbuf([128, 512], BF16, "cbf")
    epsc = P.sbuf([128, 2], F32, "epsc")
    kTm = P.sbuf([128, 2, 256], BF16, "kTm")
    vm = P.sbuf([128, 2, 256], BF16, "vm")
    PS = [P.psum([128, 512], F32, f"ps{i}") for i in range(6)]
    PSB = [P.psum([128, 1024], BF16, f"psb{i}") for i in range(2)]

    P.dma("sp", cst[:, :], cst_d[:, :], writes=[cst])
    P.dma("sp", prm[:, :], prm_d[:, :], writes=[prm])
    P.op("dve", lambda e: e.tensor_copy(cbf[:, :], cst[:, 0:512]), reads=[cst], writes=[cbf])
    P.op("pool", lambda e: e.memset(epsc[:, 0:1], EPS), writes=[epsc])
    P.op("pool", lambda e: e.memset(epsc[:, 1:2], 1.0), pw=[epsc])
    ident_bf = cbf[:, 0:128]; tri_bf = cbf[:, 128:256]; ones_bf = cbf[:, 384:512]

    def rstd_from_ss(ss_ps, n, inv_d, out_t, np_=128):
        P.op("act", lambda e: e.activation(out_t[0:np_, 0:n], ss_ps[0:np_, 0:n], AF.Sqrt, bias=epsc[0:np_, 0:1], scale=inv_d),
             reads=[ss_ps, epsc], writes=[out_t])
        P.op("dve", lambda e: e.reciprocal(out_t[0:np_, 0:n], out_t[0:np_, 0:n]), reads=[out_t], writes=[out_t])

    tmp2 = Rot(P, 3, [128, 512], F32, "tmp2")

    def stt2(out_t, out_ap, in_t, in_ap, gc, rs_tile, rs_ap, n, first):
        tm = tmp2.get()
        P.op("act", lambda e: e.activation(tm[:, 0:n], in_ap, AF.Copy, scale=prm[:, gc:gc + 1]), reads=[in_t, prm], writes=[tm])
        P.op("dve", lambda e: e.tensor_tensor(out_ap, tm[:, 0:n], rs_ap, ALU.mult), reads=[tm, rs_tile],
             writes=[out_t] if first else (), pw=() if first else [out_t])

    def rmsnorm_tile(xt, n, gcol, hT, sqrot, rs_t):
        ss = PS[0]
        for c in range(16):
            sq = sqrot.get()
            P.op("act", lambda e, c=c, sq=sq: e.activation(sq[:, 0:n], xt[:, c, 0:n], AF.Square), reads=[xt], writes=[sq])
            P.op("pe", lambda e, c=c, sq=sq: e.matmul(ss[:, 0:n], ones_bf, sq[:, 0:n], start=(c == 0), stop=(c == 15)),
                 reads=[cbf, sq], writes=[ss] if c == 0 else (), pw=() if c == 0 else [ss])
        rstd_from_ss(ss, n, 1.0 / D, rs_t)
        for c in range(16):
            stt2(hT, hT[:, c, 0:n], xt, xt[:, c, 0:n], gcol + c, rs_t, rs_t[:, 0:n], n, c == 0)

    def wload(dst, src2d, n, stg=None):
        for o in range(0, n, 4096):
            m = min(4096, n - o)
            first = (o == 0)
            P.dma("pool", dst[:, o:o + m], src2d[:, o:o + m], writes=[dst] if first else (), pw=() if first else [dst])

    def phase(n):
        if n in PH:
            with P.scope():
                yield

    for _ in phase(0):
        mt = P.sbuf([128, 16, 256], F32, "mt")
        mh = P.sbuf([128, 16, 256], BF16, "mh")
        sqrot = Rot(P, 3, [128, 512], BF16, "sq")
        rs_t = P.sbuf([128, 512], F32, "rs")
        wk_b = [P.sbuf([128, 2048], BF16, "wk") for _ in range(2)]
        wv_b = P.sbuf([128, 4096], BF16, "wv")
        stg = Rot(P, 2, [128, 2048], F32, "stg")
        kraw = P.sbuf([128, 2, 256], F32, "kraw")
        P.dma("sp", mt[:, :, :], memT[:, :].rearrange("(c p) n -> p c n", p=128), writes=[mt])
        for dc in range(2):
            wload(wk_b[dc], wmk[dc], 2048, stg)
        wload(wv_b, wmv, 4096, stg)
        rmsnorm_tile(mt, 256, PR_GMEM, mh, sqrot, rs_t)
        ss2 = PS[3]
        for dc in range(2):
            ps = PS[1 + dc]
            for k in range(16):
                P.op("pe", lambda e, k=k, dc=dc, ps=ps: e.matmul(ps[:, 0:256], wk_b[dc][:, k * 128:(k + 1) * 128], mh[:, k, :], start=(k == 0), stop=(k == 15)),
                     reads=[wk_b[dc], mh], writes=[ps] if k == 0 else (), pw=() if k == 0 else [ps])
            P.op("dve", lambda e, dc=dc, ps=ps: e.tensor_copy(kraw[:, dc, :], ps[:, 0:256]), reads=[ps], writes=[kraw] if dc == 0 else (), pw=() if dc == 0 else [kraw])
            sq = sqrot.get()
            P.op("act", lambda e, sq=sq, dc=dc: e.activation(sq[:, 0:256], kraw[:, dc, :], AF.Square), reads=[kraw], writes=[sq])
            P.op("pe", lambda e, sq=sq, dc=dc: e.matmul(ss2[:, 0:256], ones_bf, sq[:, 0:256], start=(dc == 0), stop=(dc == 1)),
                 reads=[cbf, sq], writes=[ss2] if dc == 0 else (), pw=() if dc == 0 else [ss2])
        rstd_from_ss(ss2, 256, 1.0 / 256, rs_t)
        for dc in range(2):
            stt2(kTm, kTm[:, dc, :], kraw, kraw[:, dc, :], PR_MKN + dc, rs_t, rs_t[:, 0:256], 256, dc == 0)
        for kc in range(2):
            ps = PS[4 + kc]
            for k in range(16):
                P.op("pe", lambda e, k=k, kc=kc, ps=ps: e.matmul(ps[:, 0:256], mh[:, k, kc * 128:(kc + 1) * 128], wv_b[:, k * 256:(k + 1) * 256], start=(k == 0), stop=(k == 15)),
                     reads=[wv_b, mh], writes=[ps] if k == 0 else (), pw=() if k == 0 else [ps])
            P.op("act", lambda e, kc=kc, ps=ps: e.copy(vm[:, kc, :], ps[:, 0:256]), reads=[ps], writes=[vm] if kc == 0 else (), pw=() if kc == 0 else [vm])

    for _ in phase(1):
        w1b = [P.sbuf([128, 2048], BF16, "w1b") for _ in range(14)]
        w1sb = P.sbuf([128, 16 * 262], BF16, "w1sb")
        with P.scope():
            stg = None
            for cc in range(14):
                wload(w1b[cc], w1[cc], 2048, stg)
            wload(w1sb, w1s, 16 * 262, stg)
        xrot = Rot(P, 2, [128, 16, 512], F32, "xt")
        hT = P.sbuf([128, 16, 512], BF16, "hT")
        sqrot = Rot(P, 3, [128, 512], BF16, "sq")
        rs_t = P.sbuf([128, 512], F32, "rs")
        rs2 = P.sbuf([128, 512], F32, "rs2")
        rawrot = Rot(P, 3, [128, 512], F32, "raw")
        mqraw = P.sbuf([128, 2, 512], F32, "mqraw")
        mqn = P.sbuf([128, 2, 512], BF16, "mqn")
        obrot = Rot(P, 3, [128, 512], BF16, "ob")
        pTm = [P.sbuf([128, 512], BF16, "pTm") for _ in range(2)]
        vst = Rot(P, 2, [128, 256], BF16, "vst")
        sst = Rot(P, 2, [128, 6], F32, "sst")
        rden = P.sbuf([128, 512], F32, "rden")
        def xload(t):
            xt_ = xrot.get()
            P.dma("sp", xt_[:, :, :], xT[:, t * 512:(t + 1) * 512].rearrange("(c p) n -> p c n", p=128), writes=[xt_])
            return xt_

        xnext = xload(0)
        for t in range(NTILE):
            t0 = t * 512
            xt = xnext
            rmsnorm_tile(xt, 512, PR_GMIX, hT, sqrot, rs_t)
            if t + 1 < NTILE:
                xnext = xload(t + 1)
            for cc in range(14):
                ps = PS[1 + (cc % 2)]
                for k in range(16):
                    P.op("pe", lambda e, k=k, cc=cc, ps=ps: e.matmul(ps[:, :], w1b[cc][:, k * 128:(k + 1) * 128], hT[:, k, :], start=(k == 0), stop=(k == 15)),
                         reads=[w1b[cc], hT], writes=[ps] if k == 0 else (), pw=() if k == 0 else [ps])
                if cc < 8:
                    raw = rawrot.get()
                    P.op("act", lambda e, raw=raw, ps=ps: e.copy(raw[:, :], ps[:, :]), reads=[ps], writes=[raw])
                    P.dma("sp", graw[cc][:, t0:t0 + 512], raw[:, :], reads=[raw], pw=[graw[cc]])
                elif cc < 12:
                    raw = rawrot.get()
                    sq = sqrot.get()
                    P.op("dve", lambda e, raw=raw, ps=ps: e.tensor_copy(raw[:, :], ps[:, :]), reads=[ps], writes=[raw])
                    P.op("act", lambda e, sq=sq, ps=ps: e.activation(sq[:, :], ps[:, :], AF.Square), reads=[ps], writes=[sq])
                    P.op("pe", lambda e, sq=sq: e.matmul(PS[3][:, :], ones_bf, sq[:, :], start=True, stop=True), reads=[cbf, sq], writes=[PS[3]])
                    rstd_from_ss(PS[3], 512, 1.0 / 128, rs2)
                    ob = obrot.get()
                    gc = PR_FQN if cc < 10 else PR_FKN
                    stt2(ob, ob[:, :], raw, raw[:, :], gc, rs2, rs2[:, :], 512, True)
                    dst = (foxq if cc < 10 else foxk)[cc % 2]
                    P.dma("sp", dst[:, t0:t0 + 512], ob[:, :], reads=[ob], pw=[dst])
                else:
                    dc = cc - 12
                    sq = sqrot.get()
                    P.op("dve", lambda e, dc=dc, ps=ps: e.tensor_copy(mqraw[:, dc, :], ps[:, :]), reads=[ps], writes=[mqraw] if dc == 0 else (), pw=() if dc == 0 else [mqraw])
                    P.op("act", lambda e, sq=sq, ps=ps: e.activation(sq[:, :], ps[:, :], AF.Square), reads=[ps], writes=[sq])
                    P.op("pe", lambda e, sq=sq, dc=dc: e.matmul(PS[3][:, :], ones_bf, sq[:, :], start=(dc == 0), stop=(dc == 1)),
                         reads=[cbf, sq], writes=[PS[3]] if dc == 0 else (), pw=() if dc == 0 else [PS[3]])
            rstd_from_ss(PS[3], 512, 1.0 / 256, rs2)
            for dc in range(2):
                stt2(mqn, mqn[:, dc, :], mqraw, mqraw[:, dc, :], PR_MQN + dc, rs2, rs2[:, :], 512, dc == 0)
            for kc in range(2):
                ps = PS[1 + kc]
                for dc in range(2):
                    P.op("pe", lambda e, kc=kc, dc=dc, ps=ps: e.matmul(ps[:, :], kTm[:, dc, kc * 128:(kc + 1) * 128], mqn[:, dc, :], start=(dc == 0), stop=(dc == 1)),
                         reads=[kTm, mqn], writes=[ps] if dc == 0 else (), pw=() if dc == 0 else [ps])
                P.op("act", lambda e, kc=kc, ps=ps: e.activation(pTm[kc][:, :], ps[:, :], AF.Exp, scale=1.0 / 16.0), reads=[ps], writes=[pTm[kc]])
            for kc in range(2):
                P.op("pe", lambda e, kc=kc: e.matmul(PS[3][:, :], ones_bf, pTm[kc][:, :], start=(kc == 0), stop=(kc == 1)),
                     reads=[cbf, pTm[kc]], writes=[PS[3]] if kc == 0 else (), pw=() if kc == 0 else [PS[3]])
            P.op("dve", lambda e: e.reciprocal(rden[:, :], PS[3][:, :]), reads=[PS[3]], writes=[rden])
            for dvc in range(2):
                ps = PS[4 + dvc]
                for kc in range(2):
                    P.op("pe", lambda e, kc=kc, dvc=dvc, ps=ps: e.matmul(ps[:, :], vm[:, kc, dvc * 128:(dvc + 1) * 128], pTm[kc][:, :], start=(kc == 0), stop=(kc == 1)),
                         reads=[vm, pTm[kc]], writes=[ps] if kc == 0 else (), pw=() if kc == 0 else [ps])
                ob = obrot.get()
                P.op("dve", lambda e, ob=ob, ps=ps: e.tensor_tensor(ob[:, :], ps[:, :], rden[:, :], ALU.mult), reads=[ps, rden], writes=[ob])
                omine_write(512 + dvc * 128, t0, ob)
            for blk in range(4):
                ps = PS[0]
                for k in range(16):
                    P.op("pe", lambda e, k=k, blk=blk: e.matmul(ps[:, 0:262], hT[:, k, blk * 128:(blk + 1) * 128], w1sb[:, k * 262:(k + 1) * 262], start=(k == 0), stop=(k == 15)),
                         reads=[w1sb, hT], writes=[ps] if k == 0 else (), pw=() if k == 0 else [ps])
                v_s = vst.get(); s_s = sst.get()
                P.op("act", lambda e, v_s=v_s: e.copy(v_s[:, :], ps[:, 0:256]), reads=[ps], writes=[v_s])
                P.op("dve", lambda e, s_s=s_s: e.tensor_copy(s_s[:, :], ps[:, 256:262]), reads=[ps], writes=[s_s])
                r0 = t0 + blk * 128
                P.dma("sp", foxv[r0:r0 + 128, :], v_s[:, :], reads=[v_s], pw=[foxv])
                P.dma("sp", smallT[:, r0:r0 + 128].rearrange("c p -> p c"), s_s[:, :], reads=[s_s], pw=[smallT], allow_slow_non_contiguous=True)

    for _ in phase(2):
        kT = P.sbuf([128, S], BF16, "kT")
        V = P.sbuf([128, J, 128], BF16, "V")
        lf = P.sbuf([128, J], F32, "lf")
        lfJ = P.sbuf([J, 128], F32, "lfJ")
        totc = P.sbuf([J, 1], F32, "totc")
        totB = P.sbuf([J, 128], F32, "totB")
        cT = P.sbuf([128, J], F32, "cT")
        negc = P.sbuf([128, J], F32, "negc")
        cmidB = P.sbuf([128, J], F32, "cmidB")
        nbf = P.sbuf([128, 1], F32, "nbf")
        crow = P.sbuf([1, S], BF16, "crow")
        qrot = Rot(P, 2, [128, 512], BF16, "qg")
        prot = Rot(P, 3, [128, 512], BF16, "pT")
        rden = P.sbuf([128, 512], F32, "rden")
        obrot = Rot(P, 2, [128, 512], BF16, "ob")
        strot = [PS[2], PS[3], PS[4]]
        sti = 0
        for h in range(2):
            P.dma("sp", kT[:, :], foxk[h][:, :], reads=[foxk[h]], writes=[kT])
            P.dma("sp", V[:, :, :], foxv[:, h * 128:(h + 1) * 128].rearrange("(j p) c -> p j c", p=128), reads=[foxv], writes=[V])
            P.dma("sp", lf[:, :], smallT[h, :].rearrange("(j p) -> p j", p=128), reads=[smallT], writes=[lf], allow_slow_non_contiguous=True)
            P.dma("sp", lfJ[:, :], smallT[h, :].rearrange("(j p) -> j p", p=128), reads=[smallT], writes=[lfJ])
            P.op("dve", lambda e, h=h: e.tensor_scalar(nbf[:, :], prm[:, PR_BF + h:PR_BF + h + 1], -1.0, None, ALU.mult), reads=[prm], writes=[nbf])
            for tl, npart in ((lf, 128), (lfJ, J)):
                P.op("act", lambda e, tl=tl, npart=npart: e.activation(tl[:, :], tl[:, :], AF.Exp, bias=nbf[0:npart, 0:1], scale=-1.0), reads=[tl, nbf], writes=[tl])
                P.op("act", lambda e, tl=tl, npart=npart: e.activation(tl[:, :], tl[:, :], AF.Ln, bias=epsc[0:npart, 1:2], scale=1.0), reads=[tl, epsc], writes=[tl])
            P.op("dve", lambda e: e.reduce_sum(totc[:, :], lfJ[:, :], mybir.AxisListType.X), reads=[lfJ], writes=[totc])
            P.op("dve", lambda e: e.tensor_scalar(totB[:, :], cst[0:J, C_ONES:C_ONES + 128], totc[:, 0:1], None, ALU.mult), reads=[cst, totc], writes=[totB])
            ps = PS[5]
            P.op("pe", lambda e: e.matmul(ps[:, 0:J], cst[:, C_TRI:C_TRI + 128], lf[:, :], start=True, stop=False), reads=[cst, lf], writes=[ps])
            P.op("pe", lambda e: e.matmul(ps[:, 0:J], totB[:, :], cst[0:J, C_TRIS:C_TRIS + J], start=False, stop=True), reads=[cst, totB], pw=[ps])
            P.op("dve", lambda e: e.tensor_copy(negc[:, :], ps[:, 0:J]), reads=[ps], writes=[negc])
            P.op("dve", lambda e: e.tensor_scalar(cT[:, :], ps[:, 0:J], -1.0, None, ALU.mult), reads=[ps], writes=[cT])
            P.op("pe", lambda e: e.matmul(ps[:, 0:J], cst[:, C_SEL:C_SEL + 128], cT[:, :], start=True, stop=True), reads=[cst, cT], writes=[ps])
            P.op("dve", lambda e: e.tensor_copy(cmidB[:, :], ps[:, 0:J]), reads=[ps], writes=[cmidB])
            for i in range(J):
                P.op("dve", lambda e, i=i: e.tensor_scalar(crow[0:1, i * 128:(i + 1) * 128], cst[0:1, C_ONES:C_ONES + 128], cmidB[0:1, i:i + 1], 128.0 ** 0.5, ALU.mult, ALU.mult),
                     reads=[cst, cmidB], writes=[crow] if i == 0 else (), pw=() if i == 0 else [crow])
            for G in range(NTILE):
                qg = qrot.get()
                P.dma("pool", qg[:, :], foxq[h][:, G * 512:(G + 1) * 512], reads=[foxq[h]], writes=[qg])
                o_ps = PS[0]; d_ps = PS[1]
                order = list(range(4 * G, 4 * G + 4)) + list(range(0, 4 * G))
                for n, j in enumerate(order):
                    qlo = max(0, j - 4 * G)
                    c0 = qlo * 128
                    st = strot[sti]; sti = (sti + 1) % 3
                    P.op("pe", lambda e, j=j, c0=c0, st=st, qg=qg: e.matmul(st[:, c0:512], kT[:, j * 128:(j + 1) * 128], qg[:, c0:512], start=True, stop=False),
                         reads=[kT, qg], writes=[st])
                    P.op("pe", lambda e, c0=c0, st=st, G=G: e.matmul(st[:, c0:512], ones_bf[0:1, :], crow[0:1, G * 512 + c0:(G + 1) * 512], start=False, stop=True),
                         reads=[cbf, crow], pw=[st])
                    pT = prot.get()
                    P.op("act", lambda e, pT=pT, st=st, j=j, c0=c0: e.activation(pT[:, c0:512], st[:, c0:512], AF.Exp, bias=negc[:, j:j + 1], scale=128.0 ** -0.5),
                         reads=[st, negc], writes=[pT])
                    if j >= 4 * G:
                        P.op("pool", lambda e, pT=pT, c0=c0: e.tensor_tensor(pT[:, c0:c0 + 128], pT[:, c0:c0 + 128], tri_bf, ALU.mult), reads=[pT, cbf], writes=[pT])
                    first = (n == 0); last = (n == len(order) - 1)
                    P.op("pe", lambda e, j=j, c0=c0, pT=pT, first=first, last=last: e.matmul(o_ps[:, c0:512], V[:, j, :], pT[:, c0:512], start=first, stop=last, skip_group_check=True),
                         reads=[V, pT], writes=[o_ps] if first else (), pw=() if first else [o_ps])
                    P.op("pe", lambda e, c0=c0, pT=pT, first=first, last=last: e.matmul(d_ps[:, c0:512], ones_bf, pT[:, c0:512], start=first, stop=last, skip_group_check=True),
                         reads=[cbf, pT], writes=[d_ps] if first else (), pw=() if first else [d_ps])
                P.op("dve", lambda e: e.reciprocal(rden[:, :], d_ps[:, :]), reads=[d_ps], writes=[rden])
                ob = obrot.get()
                P.op("dve", lambda e, ob=ob: e.tensor_tensor(ob[:, :], o_ps[:, :], rden[:, :], ALU.mult), reads=[o_ps, rden], writes=[ob])
                omine_write(256 + h * 128, G * 512, ob)

    ag_done = [0]

    def ag_tile(Rq, t0):
        for o in range(0, 512, NT4):
            ct = (t0 + o) // NT4
            first = (ag_done[0] == 0)
            ag_done[0] += 1
            Rq.custom("pool", lambda e, sem, ct=ct: e.collective_compute("AllGather", ALU.bypass, replica_groups=[[0, 1, 2, 3], [4, 5, 6, 7]],
                                                                        ins=[omine.t[ct].opt()], outs=[gath.t[ct].opt()]).then_inc(sem, 1),
                      reads=[omine], writes=[gath] if first else (), pw=() if first else [gath])

    if 3 in PH:
      gdn_phase(P, nc, S, NTILE, NCH, PS, PSB, cst, cbf, prm, epsc, graw, smallT, omine_write, ag_tile if 4 in PH else None)

    pid = nc.partition_id()
    qid = pid % 4
    for _ in phase(5):
        xt = P.sbuf([128, 16, NT4], F32, "xt4")
        sqrot = Rot(P, 3, [128, 512], BF16, "sq")
        rs_t = P.sbuf([128, 512], F32, "rs")
        wrot = Rot(P, 6, [128, 2048], BF16, "w4")
        wurot = Rot(P, 4, [128, 1024], BF16, "wu4")
        w2rot = Rot(P, 3, [128, 4096], BF16, "wf2")
        stg = None
        sgrot = Rot(P, 2, [128, 512], F32, "sg")
        tmrot = Rot(P, 2, [128, 512], F32, "tm")
        yacc = P.sbuf([128, 512], F32, "yacc")
        orot = Rot(P, 2, [128, 512], F32, "o4")
        for sub in range(NSUB):
            c0 = sub * NT4
            P.dma("pool", xt[:, :, :], xq[:, c0:c0 + NT4].rearrange("(c p) n -> p c n", p=128), writes=[xt])
            with P.scope():
                hT = P.sbuf([128, 16, NT4], BF16, "h4")
                oT = P.sbuf([128, 24, NT4], BF16, "o4T")
                yT = P.sbuf([128, 16, NT4], BF16, "y4")
                rmsnorm_tile(xt, NT4, PR_GMIX, hT, sqrot, rs_t)
                src = gath.t[bass.ds(qid * NSUB + sub, 1), :, :].rearrange("o (r p) n -> p (o r) n", p=128)
                P.dma("pool", oT[:, :, :], src, reads=[gath], writes=[oT])
                for c in range(16):
                    for br in range(3):
                        wgb = wrot.get(); wub = wurot.get()
                        wload(wgb, wg[br * 16 + c], 2048, stg)
                        wload(wub, wup[br * 16 + c], 1024, stg)
                        gps = PS[1 + (br % 2)]; ups = PS[3 + (br % 2)]
                        for k in range(16):
                            P.op("pe", lambda e, k=k, wgb=wgb, gps=gps: e.matmul(gps[:, 0:NT4], wgb[:, k * 128:(k + 1) * 128], hT[:, k, :], start=(k == 0), stop=(k == 15)),
                                 reads=[wgb, hT], writes=[gps] if k == 0 else (), pw=() if k == 0 else [gps])
                        for k in range(8):
                            rk = (k // 2) * 6 + br * 2 + (k % 2)
                            P.op("pe", lambda e, k=k, rk=rk, wub=wub, ups=ups: e.matmul(ups[:, 0:NT4], wub[:, k * 128:(k + 1) * 128], oT[:, rk, :], start=(k == 0), stop=(k == 7)),
                                 reads=[wub, oT], writes=[ups] if k == 0 else (), pw=() if k == 0 else [ups])
                        sg = sgrot.get()
                        P.op("act", lambda e, sg=sg, gps=gps: e.activation(sg[:, 0:NT4], gps[:, 0:NT4], AF.Sigmoid), reads=[gps], writes=[sg])
                        if br == 0:
                            P.op("dve", lambda e, sg=sg, ups=ups: e.tensor_tensor(yacc[:, 0:NT4], sg[:, 0:NT4], ups[:, 0:NT4], ALU.mult), reads=[sg, ups], writes=[yacc])
                        else:
                            tm = tmrot.get()
                            P.op("dve", lambda e, sg=sg, ups=ups, tm=tm: e.tensor_tensor(tm[:, 0:NT4], sg[:, 0:NT4], ups[:, 0:NT4], ALU.mult), reads=[sg, ups], writes=[tm])
                            if br == 1:
                                P.op("dve", lambda e, tm=tm: e.tensor_tensor(yacc[:, 0:NT4], yacc[:, 0:NT4], tm[:, 0:NT4], ALU.add), reads=[tm, yacc], writes=[yacc])
                            else:
                                P.op("dve", lambda e, tm=tm, c=c: e.tensor_tensor(yT[:, c, :], yacc[:, 0:NT4], tm[:, 0:NT4], ALU.add), reads=[tm, yacc],
                                     writes=[yT] if c == 0 else (), pw=() if c == 0 else [yT])
                for c in range(16):
                    wb = wrot.get()
                    wload(wb, wout[c], 2048, stg)
                    ps = PS[1 + (c % 2)]
                    for k in range(16):
                        P.op("pe", lambda e, k=k, wb=wb, ps=ps: e.matmul(ps[:, 0:NT4], wb[:, k * 128:(k + 1) * 128], yT[:, k, :], start=(k == 0), stop=(k == 15)),
                             reads=[wb, yT], writes=[ps] if k == 0 else (), pw=() if k == 0 else [ps])
                    P.op("dve", lambda e, c=c, ps=ps: e.tensor_tensor(xt[:, c, :], xt[:, c, :], ps[:, 0:NT4], ALU.add), reads=[ps, xt], pw=[xt])
            with P.scope():
                h2 = P.sbuf([128, 16, NT4], BF16, "h24")
                uT = P.sbuf([128, 32, NT4], BF16, "u4")
                rrot = Rot(P, 2, [128, 512], F32, "r4")
                rmsnorm_tile(xt, NT4, PR_GMLP, h2, sqrot, rs_t)
                for half in range(2):
                    for fi in range(32):
                        f = half * 32 + fi
                        wb = wrot.get()
                        wload(wb, wff1[f], 2048, stg)
                        ps = PS[1 + (f % 2)]
                        for k in range(16):
                            P.op("pe", lambda e, k=k, wb=wb, ps=ps: e.matmul(ps[:, 0:NT4], wb[:, k * 128:(k + 1) * 128], h2[:, k, :], start=(k == 0), stop=(k == 15)),
                                 reads=[wb, h2], writes=[ps] if k == 0 else (), pw=() if k == 0 else [ps])
                        r = rrot.get()
                        P.op("act", lambda e, r=r, ps=ps: e.activation(r[:, 0:NT4], ps[:, 0:NT4], AF.Relu), reads=[ps], writes=[r])
                        P.op("dve", lambda e, r=r, fi=fi: e.tensor_tensor(uT[:, fi, :], r[:, 0:NT4], r[:, 0:NT4], ALU.mult), reads=[r],
                             writes=[uT] if fi == 0 else (), pw=() if fi == 0 else [uT])
                    for c in range(16):
                        wb = w2rot.get()
                        wload(wb, wff2[c][:, half * 4096:(half + 1) * 4096], 4096, stg)
                        ps = PS[3 + (c % 2)]
                        for fi in range(32):
                            P.op("pe", lambda e, fi=fi, wb=wb, ps=ps: e.matmul(ps[:, 0:NT4], wb[:, fi * 128:(fi + 1) * 128], uT[:, fi, :], start=(fi == 0), stop=(fi == 31)),
                                 reads=[wb, uT], writes=[ps] if fi == 0 else (), pw=() if fi == 0 else [ps])
                        if half == 0:
                            P.op("dve", lambda e, c=c, ps=ps: e.tensor_tensor(xt[:, c, :], xt[:, c, :], ps[:, 0:NT4], ALU.add), reads=[ps, xt], pw=[xt])
                        else:
                            o = orot.get()
                            P.op("dve", lambda e, c=c, ps=ps, o=o: e.tensor_tensor(o[:, 0:NT4], xt[:, c, :], ps[:, 0:NT4], ALU.add), reads=[ps, xt], writes=[o])
                            P.dma("sp", outT[c * 128:(c + 1) * 128, c0:c0 + NT4], o[:, 0:NT4], reads=[o], pw=[outT])
    fin = [outT]
    if dbg and 3 in PH:
        omine_o = P.dram("omine", [NCT, 768, NT4], BF16, "ExternalOutput")
        for ct in range(NCT):
            P.dma("sp", omine_o[ct], omine[ct], reads=[omine], writes=[omine_o] if ct == 0 else (), pw=() if ct == 0 else [omine_o])
        fin += [omine_o]
    if dbg:
        fin += [foxv, smallT] + graw + foxq + foxk
    P.wait_all("sp", fin)
    P.emit()
    return nc


class View:
    __slots__ = ("ap", "b")

    def __init__(self, b, ap):
        self.b = b; self.ap = ap

    def __getitem__(self, idx):
        return self.ap[idx]


class Rec:
    def __init__(self):
        self.l = []

    def op(self, eng, fn, reads=(), writes=(), pw=()):
        self.l.append(("op", eng, _freeze(fn), tuple(reads), tuple(writes), tuple(pw)))

    def dma(self, q, out_ap, in_ap, reads=(), writes=(), pw=(), **kw):
        self.l.append(("dma", q, out_ap, in_ap, tuple(reads), tuple(writes), tuple(pw), kw))

    def custom(self, q, fn, reads=(), writes=(), pw=()):
        self.l.append(("custom", q, _freeze(fn), tuple(reads), tuple(writes), tuple(pw)))


def replay(P, recs):
    n = max(len(r.l) for r in recs)
    for k in range(n):
        for r in recs:
            if k < len(r.l):
                it = r.l[k]
                if it[0] == "op":
                    P.op(it[1], it[2], reads=it[3], writes=it[4], pw=it[5])
                elif it[0] == "custom":
                    P.custom(it[1], it[2], reads=it[3], writes=it[4], pw=it[5])
                else:
                    P.dma(it[1], it[2], it[3], reads=it[4], writes=it[5], pw=it[6], **it[7])


def gdn_phase(P, nc, S, NTILE, NCH, PS, PSB, cst, cbf, prm, epsc, graw, smallT, omine_write, ag_tile=None):
    ident_bf = cbf[:, 0:128]; ones_bf = cbf[:, 384:512]
    I64 = cbf[0:64, 0:64]
    UT64 = cst[0:64, C_TRI:C_TRI + 64]
    ID64 = cst[0:64, C_ID:C_ID + 64]

    def head(R, hi):
        ss = PS[hi]
        X = PS[2 + 2 * hi]; Yb = PS[3 + 2 * hi]; PB = PSB[hi]
        A = View(X, X[:, 0:129]); B = View(X, X[0:64, 129:257]); C = View(X, X[0:64, 257:385]); Dp = View(X, X[0:64, 385:449])
        E1 = View(Yb, Yb[:, 0:64]); E2 = View(Yb, Yb[:, 64:128]); E3 = View(Yb, Yb[:, 128:256])
        F1 = View(Yb, Yb[0:64, 256:384]); F2 = View(Yb, Yb[0:64, 384:512])
        PB1 = View(PB, PB[0:64, 0:64]); PB2 = View(PB, PB[0:64, 128:384])
        gall = P.sbuf([64, NCH], F32, "gall"); ball = P.sbuf([64, NCH], F32, "ball")
        nea = P.sbuf([64, 1], F32, "nea")
        xb = [P.sbuf([128, 515], F32, f"xb{i}") for i in range(4)]
        y = P.sbuf([128, 512], F32, "gy"); tmp = [P.sbuf([128, 512], F32, "gtmp") for _ in range(2)]; sg = P.sbuf([128, 512], F32, "gsg")
        yc = P.sbuf([128, 512], F32, "gyc")
        sq = P.sbuf([128, 512], BF16, "gsq"); rs = P.sbuf([128, 512], F32, "grs")
        qh = P.sbuf([128, 512], BF16, "qh"); kh = P.sbuf([128, 512], BF16, "kh"); vb = P.sbuf([128, 512], BF16, "vb")
        zs = P.sbuf([128, 512], F32, "zs"); oraw = P.sbuf([128, 512], F32, "oraw"); on = P.sbuf([128, 512], F32, "on")
        ob = P.sbuf([128, 512], BF16, "gob")
        gB = P.sbuf([64, 128], F32, "gB"); bB = P.sbuf([64, 128], F32, "bB")
        gamRow = P.sbuf([128, 64], F32, "gamRow"); betaRow = P.sbuf([128, 64], F32, "betaRow"); aRow = P.sbuf([128, 64], F32, "aRow")
        gamCol = P.sbuf([64, 1], F32, "gamCol"); acol = P.sbuf([64, 1], F32, "acol"); rcol = P.sbuf([64, 1], F32, "rcol"); bacol = P.sbuf([64, 1], F32, "bacol")
        t0d = P.sbuf([64, 64], F32, "t0d"); d2 = P.sbuf([64, 128], F32, "d2")
        kbq = P.sbuf([128, 128], BF16, "kbq"); NQ = P.sbuf([64, 128], BF16, "NQ")
        PP = [P.sbuf([64, 128], BF16, f"PP{i}") for i in range(2)]
        Y = [P.sbuf([64, 64], BF16, f"Y{i}") for i in range(2)]
        RHSw = P.sbuf([64, 128], BF16, "RHSw"); kdec = P.sbuf([64, 128], BF16, "kdec"); RHSv = P.sbuf([64, 128], BF16, "RHSv")
        nwT = P.sbuf([128, 64], BF16, "nwT"); uc = P.sbuf([64, 128], F32, "uc"); qdT = P.sbuf([128, 64], BF16, "qdT")
        u = P.sbuf([64, 128], BF16, "u"); S32 = P.sbuf([128, 128], F32, "S32"); Sbf = P.sbuf([128, 128], BF16, "Sbf"); St = P.sbuf([128, 128], F32, "St")
        R.dma("sp", gall[:, :], smallT[4 + hi, :].rearrange("(n p) -> p n", p=64), reads=[smallT], writes=[gall], allow_slow_non_contiguous=True)
        R.dma("sp", ball[:, :], smallT[2 + hi, :].rearrange("(n p) -> p n", p=64), reads=[smallT], writes=[ball], allow_slow_non_contiguous=True)
        R.op("act", lambda e: e.activation(nea[:, :], prm[0:64, PR_ALOG + hi:PR_ALOG + hi + 1], AF.Exp), reads=[prm], writes=[nea])
        R.op("dve", lambda e: e.tensor_scalar(nea[:, :], nea[:, :], -1.0, None, ALU.mult), reads=[nea], writes=[nea])
        R.op("act", lambda e: e.activation(gall[:, :], gall[:, :], AF.Exp, bias=prm[0:64, PR_DTB + hi:PR_DTB + hi + 1], scale=1.0), reads=[gall, prm], writes=[gall])
        R.op("act", lambda e: e.activation(gall[:, :], gall[:, :], AF.Ln, bias=epsc[0:64, 1:2], scale=1.0), reads=[gall, epsc], writes=[gall])
        R.op("dve", lambda e: e.tensor_scalar(gall[:, :], gall[:, :], nea[:, 0:1], None, ALU.mult), reads=[gall, nea], writes=[gall])
        R.op("act", lambda e: e.activation(ball[:, :], ball[:, :], AF.Sigmoid), reads=[ball], writes=[ball])
        R.op("pool", lambda e: e.memset(S32[:, :], 0.0), writes=[S32])
        R.op("pool", lambda e: e.memset(Sbf[:, :], 0.0), writes=[Sbf])
        for t in range(NTILE):
            c0t = t * 512
            for gi in range(4):
                src = graw[gi * 2 + hi]
                if t == 0:
                    R.op("pool", lambda e, gi=gi: e.memset(xb[gi][:, 0:3], 0.0), writes=[xb[gi]])
                    R.dma("sp", xb[gi][:, 3:515], src[:, 0:512], reads=[src], pw=[xb[gi]])
                else:
                    R.dma("sp", xb[gi][:, 0:515], src[:, c0t - 3:c0t + 512], reads=[src], writes=[xb[gi]])
            for gi in range(3):
                wc = PR_CONV + (gi * 2 + hi) * 4
                R.op("act", lambda e, gi=gi, wc=wc: e.activation(y[:, :], xb[gi][:, 0:512], AF.Copy, scale=prm[:, wc:wc + 1]), reads=[xb[gi], prm], writes=[y])
                for i in range(1, 4):
                    tm = tmp[i % 2]
                    R.op("act", lambda e, gi=gi, wc=wc, i=i, tm=tm: e.activation(tm[:, :], xb[gi][:, i:i + 512], AF.Copy, scale=prm[:, wc + i:wc + i + 1]), reads=[xb[gi], prm], writes=[tm])
                    R.op("dve", lambda e, tm=tm: e.tensor_tensor(y[:, :], y[:, :], tm[:, :], ALU.add), reads=[y, tm], writes=[y])
                R.op("act", lambda e: e.activation(sg[:, :], y[:, :], AF.Sigmoid), reads=[y], writes=[sg])
                if gi == 2:
                    R.op("dve", lambda e: e.tensor_tensor(vb[:, :], y[:, :], sg[:, :], ALU.mult), reads=[y, sg], writes=[vb])
                else:
                    R.op("dve", lambda e: e.tensor_tensor(yc[:, :], y[:, :], sg[:, :], ALU.mult), reads=[y, sg], writes=[yc])
                    R.op("act", lambda e: e.activation(sq[:, :], yc[:, :], AF.Square), reads=[yc], writes=[sq])
                    R.op("pe", lambda e: e.matmul(ss[:, :], ones_bf, sq[:, :], start=True, stop=True), reads=[cbf, sq], writes=[ss])
                    R.op("act", lambda e: e.activation(rs[:, :], ss[:, :], AF.Sqrt, bias=epsc[:, 0:1], scale=1.0), reads=[ss, epsc], writes=[rs])
                    R.op("dve", lambda e: e.reciprocal(rs[:, :], rs[:, :]), reads=[rs], writes=[rs])
                    if gi == 0:
                        R.op("dve", lambda e: e.tensor_scalar(rs[:, :], rs[:, :], 128.0 ** -0.5, None, ALU.mult), reads=[rs], writes=[rs])
                    dst = qh if gi == 0 else kh
                    R.op("dve", lambda e, dst=dst: e.tensor_tensor(dst[:, :], yc[:, :], rs[:, :], ALU.mult), reads=[yc, rs], writes=[dst])
            R.op("act", lambda e: e.activation(sg[:, :], xb[3][:, 3:515], AF.Sigmoid), reads=[xb[3]], writes=[sg])
            R.op("dve", lambda e: e.tensor_tensor(zs[:, :], xb[3][:, 3:515], sg[:, :], ALU.mult), reads=[xb[3], sg], writes=[zs])
            for ci in range(8):
                n = t * 8 + ci
                c0 = ci * 64
                R.op("act", lambda e, n=n: e.activation(gB[:, :], cst[0:64, C_ONES:C_ONES + 128], AF.Copy, scale=gall[:, n:n + 1]), reads=[cst, gall], writes=[gB])
                R.op("act", lambda e, n=n: e.activation(bB[:, :], cst[0:64, C_ONES:C_ONES + 128], AF.Copy, scale=ball[:, n:n + 1]), reads=[cst, ball], writes=[bB])
                R.op("pe", lambda e: e.matmul(A[:, 0:64], gB[:, :], UT64, start=True, stop=True), reads=[gB, cst], writes=[A])
                R.op("pe", lambda e: e.matmul(A[:, 64:128], bB[:, :], ID64, start=True, stop=True), reads=[bB, cst], pw=[A])
                R.op("pe", lambda e, n=n: e.matmul(A[0:64, 128:129], UT64, gall[:, n:n + 1], start=True, stop=True), reads=[gall, cst], pw=[A])
                R.op("dve", lambda e: e.tensor_copy(gamRow[:, :], A[:, 0:64]), reads=[A], writes=[gamRow])
                R.op("dve", lambda e: e.tensor_copy(betaRow[:, :], A[:, 64:128]), reads=[A], writes=[betaRow])
                R.op("dve", lambda e: e.tensor_copy(gamCol[:, :], A[0:64, 128:129]), reads=[A], writes=[gamCol])
                R.op("act", lambda e: e.activation(aRow[:, :], gamRow[:, :], AF.Exp), reads=[gamRow], writes=[aRow])
                R.op("act", lambda e: e.activation(acol[:, :], gamCol[:, :], AF.Exp), reads=[gamCol], writes=[acol])
                R.op("act", lambda e: e.activation(rcol[:, :], gamCol[:, :], AF.Exp, bias=gamRow[0:64, 63:64], scale=-1.0), reads=[gamCol, gamRow], writes=[rcol])
                R.op("dve", lambda e, n=n: e.tensor_tensor(bacol[:, :], acol[:, :], ball[:, n:n + 1], ALU.mult), reads=[acol, ball], writes=[bacol])
                R.op("dve", lambda e: e.tensor_scalar(t0d[:, :], gamRow[0:64, :], gamCol[:, 0:1], None, ALU.subtract), reads=[gamRow, gamCol], writes=[t0d])
                R.op("dve", lambda e: e.tensor_tensor(d2[:, 0:64], t0d[:, :], cst[0:64, C_NM:C_NM + 64], ALU.add), reads=[t0d, cst], writes=[d2])
                R.op("dve", lambda e: e.tensor_tensor(d2[:, 64:128], t0d[:, :], cst[0:64, C_NM + 64:C_NM + 128], ALU.add), reads=[t0d, cst], pw=[d2])
                R.op("act", lambda e: e.activation(d2[:, :], d2[:, :], AF.Exp), reads=[d2], writes=[d2])
                R.op("dve", lambda e, c0=c0: e.tensor_tensor(kbq[:, 0:64], kh[:, c0:c0 + 64], betaRow[:, :], ALU.mult), reads=[kh, betaRow], writes=[kbq])
                R.op("act", lambda e, c0=c0: e.copy(kbq[:, 64:128], qh[:, c0:c0 + 64]), reads=[qh], pw=[kbq])
                R.op("pe", lambda e, c0=c0: e.matmul(B[:, :], kh[:, c0:c0 + 64], kbq[:, :], start=True, stop=True), reads=[kh, kbq], writes=[B])
                R.op("dve", lambda e: e.tensor_tensor(NQ[:, :], B[:, :], d2[:, :], ALU.mult), reads=[B, d2], writes=[NQ])
                R.op("pe", lambda e: e.transpose(PB1[:, :], NQ[:, 0:64], I64), reads=[NQ, cbf], writes=[PB1])
                R.op("dve", lambda e: e.tensor_copy(PP[0][:, 64:128], PB1[:, :]), reads=[PB1], writes=[PP[0]])
                R.op("act", lambda e: e.copy(PP[0][:, 0:64], NQ[:, 0:64]), reads=[NQ], pw=[PP[0]])
                R.op("dve", lambda e: e.tensor_tensor(Y[0][:, :], I64, NQ[:, 0:64], ALU.subtract), reads=[cbf, NQ], writes=[Y[0]])
                pk = 0; yk = 0
                for lvl in range(5):
                    R.op("pe", lambda e, pk=pk: e.matmul(C[:, 0:64], PP[pk][:, 64:128], PP[pk][:, 0:64], start=True, stop=True), reads=[PP[pk]], writes=[C])
                    R.op("pe", lambda e, pk=pk: e.matmul(C[:, 64:128], PP[pk][:, 0:64], PP[pk][:, 64:128], start=True, stop=True), reads=[PP[pk]], pw=[C])
                    R.op("dve", lambda e, pk=pk: e.tensor_copy(PP[1 - pk][:, :], C[:, :]), reads=[C], writes=[PP[1 - pk]])
                    pk = 1 - pk
                    R.op("pe", lambda e, pk=pk, yk=yk: e.matmul(Dp[:, :], PP[pk][:, 64:128], Y[yk][:, :], start=True, stop=True), reads=[PP[pk], Y[yk]], writes=[Dp])
                    R.op("dve", lambda e, yk=yk: e.tensor_tensor(Y[1 - yk][:, :], Dp[:, :], Y[yk][:, :], ALU.add), reads=[Dp, Y[yk]], writes=[Y[1 - yk]])
                    yk = 1 - yk
                Tt = Y[yk]
                R.op("pe", lambda e, c0=c0: e.transpose(PB2[:, 0:128], kh[:, c0:c0 + 64], ident_bf), reads=[kh, cbf], writes=[PB2])
                R.op("pe", lambda e, c0=c0: e.transpose(PB2[:, 128:256], vb[:, c0:c0 + 64], ident_bf), reads=[vb, cbf], pw=[PB2])
                R.op("dve", lambda e: e.tensor_scalar(RHSw[:, :], PB2[:, 0:128], bacol[:, 0:1], None, ALU.mult), reads=[PB2, bacol], writes=[RHSw])
                R.op("dve", lambda e: e.tensor_scalar(kdec[:, :], PB2[:, 0:128], rcol[:, 0:1], None, ALU.mult), reads=[PB2, rcol], writes=[kdec])
                R.op("dve", lambda e, n=n: e.tensor_scalar(RHSv[:, :], PB2[:, 128:256], ball[:, n:n + 1], None, ALU.mult), reads=[PB2, ball], writes=[RHSv])
                R.op("pe", lambda e, Tt=Tt: e.matmul(E1[:, :], RHSw[:, :], Tt[:, :], start=True, stop=True), reads=[RHSw, Tt], writes=[E1])
                R.op("dve", lambda e: e.tensor_scalar(nwT[:, :], E1[:, :], -1.0, None, ALU.mult), reads=[E1], writes=[nwT])
                R.op("pe", lambda e, Tt=Tt: e.matmul(F1[:, :], Tt[:, :], RHSv[:, :], start=True, stop=True), reads=[RHSv, Tt], writes=[F1])
                R.op("dve", lambda e: e.tensor_copy(uc[:, :], F1[:, :]), reads=[F1], writes=[uc])
                R.op("dve", lambda e, c0=c0: e.tensor_tensor(qdT[:, :], qh[:, c0:c0 + 64], aRow[:, :], ALU.mult), reads=[qh, aRow], writes=[qdT])
                R.op("pe", lambda e: e.matmul(F2[:, :], nwT[:, :], Sbf[:, :], start=True, stop=True), reads=[nwT, Sbf], writes=[F2])
                R.op("dve", lambda e: e.tensor_tensor(u[:, :], F2[:, :], uc[:, :], ALU.add), reads=[F2, uc], writes=[u])
                R.op("pe", lambda e: e.matmul(E2[:, :], Sbf[:, :], qdT[:, :], start=True, stop=False), reads=[Sbf, qdT], writes=[E2])
                R.op("pe", lambda e: e.matmul(E2[:, :], u[:, :], NQ[:, 64:128], start=False, stop=True), reads=[u, NQ], pw=[E2])
                R.op("dve", lambda e, c0=c0: e.tensor_copy(oraw[:, c0:c0 + 64], E2[:, :]), reads=[E2], writes=[oraw] if ci == 0 else (), pw=() if ci == 0 else [oraw])
                R.op("pe", lambda e: e.matmul(E3[:, :], kdec[:, :], u[:, :], start=True, stop=True), reads=[kdec, u], writes=[E3])
                R.op("act", lambda e: e.activation(St[:, :], S32[:, :], AF.Copy, scale=aRow[:, 63:64]), reads=[S32, aRow], writes=[St])
                R.op("dve", lambda e: e.tensor_tensor(S32[:, :], St[:, :], E3[:, :], ALU.add), reads=[St, E3], writes=[S32])
                R.op("act", lambda e: e.copy(Sbf[:, :], S32[:, :]), reads=[S32], writes=[Sbf])
            R.op("act", lambda e: e.activation(sq[:, :], oraw[:, :], AF.Square), reads=[oraw], writes=[sq])
            R.op("pe", lambda e: e.matmul(ss[:, :], ones_bf, sq[:, :], start=True, stop=True), reads=[cbf, sq], writes=[ss])
            R.op("act", lambda e: e.activation(rs[:, :], ss[:, :], AF.Sqrt, bias=epsc[:, 0:1], scale=1.0 / 128), reads=[ss, epsc], writes=[rs])
            R.op("dve", lambda e: e.reciprocal(rs[:, :], rs[:, :]), reads=[rs], writes=[rs])
            R.op("dve", lambda e: e.tensor_tensor(on[:, :], oraw[:, :], rs[:, :], ALU.mult), reads=[oraw, rs], writes=[on])
            R.op("dve", lambda e: e.tensor_scalar(on[:, :], on[:, :], prm[:, PR_GDNG:PR_GDNG + 1], None, ALU.mult), reads=[on, prm], writes=[on])
            R.op("dve", lambda e: e.tensor_tensor(ob[:, :], on[:, :], zs[:, :], ALU.mult), reads=[on, zs], writes=[ob])
            omine_write(hi * 128, c0t, ob, R)
            if hi == 1 and ag_tile is not None:
                ag_tile(R, c0t)

    with P.scope():
        recs = []
        for hi in range(2):
            R = Rec()
            head(R, hi)
            recs.append(R)
        replay(P, recs)


_CACHE = {}


def prep_inputs(I, S, PH=(0, 1, 2, 3, 4, 5)):
    w_in = I["w_in"][0]
    TQ = S // 4
    cst = make_consts()
    gates0 = 8216
    big = {}
    if 5 in PH:
        big["wg"] = np.concatenate([relay(w_in[:, gates0 + br * D:gates0 + (br + 1) * D]) for br in range(3)], axis=0)
        big["wup"] = np.concatenate([relay(I[n][0]) for n in ("w_up_gdn", "w_up_fox", "w_up_mem")], axis=0)
        big["wout"] = relay(I["w_out"][0]); big["wff1"] = relay(I["w_ff1"][0]); big["wff2"] = relay(I["w_ff2"][0])
    maps = []
    for core in range(8):
        b, g = core // 4, core % 4
        hs = [2 * g, 2 * g + 1]
        xTb = np.ascontiguousarray(I["x"][b, :S].T)
        cols = []
        for base in (0, 1024, 2048, 3072, 4112, 5136):
            for h in hs:
                cols.append(np.arange(base + h * 128, base + (h + 1) * 128))
        cols.append(np.arange(7192 + g * 256, 7192 + (g + 1) * 256))
        w1 = relay(w_in[:, np.concatenate(cols)])
        scol = np.concatenate([np.arange(6160 + h * 128, 6160 + (h + 1) * 128) for h in hs] + [np.array([7184 + h for h in hs]),
                              np.array([4096 + h for h in hs]), np.array([4104 + h for h in hs])])
        w1s = relay_tok(w_in[:, scol])
        wmkv = I["w_mem_kv"][0]
        wmk = relay(wmkv[:, g * 256:(g + 1) * 256])
        wmv = relay_tok(wmkv[:, 1024 + g * 256:1024 + (g + 1) * 256])
        prm = np.zeros((128, NPRM), np.float32)
        prm[:, PR_GMIX:PR_GMIX + 16] = I["g_mix"][0].reshape(16, 128).T
        prm[:, PR_GMLP:PR_GMLP + 16] = I["g_mlp"][0].reshape(16, 128).T
        prm[:, PR_GMEM:PR_GMEM + 16] = I["g_mem"][0].reshape(16, 128).T
        cw = I["conv_w"][0]
        for gi, base in enumerate((0, 1024, 2048)):
            for hi, h in enumerate(hs):
                prm[:, PR_CONV + (gi * 2 + hi) * 4:PR_CONV + (gi * 2 + hi) * 4 + 4] = cw[:, base + h * 128:base + (h + 1) * 128].T
        prm[:, PR_GDNG] = I["gdn_norm_g"][0]
        prm[:, PR_FQN] = I["fox_q_norm"][0]; prm[:, PR_FKN] = I["fox_k_norm"][0]
        prm[:, PR_MQN:PR_MQN + 2] = I["mem_q_norm"][0].reshape(2, 128).T
        prm[:, PR_MKN:PR_MKN + 2] = I["mem_k_norm"][0].reshape(2, 128).T
        for hi, h in enumerate(hs):
            prm[:, PR_ALOG + hi] = I["a_log"][0, h]; prm[:, PR_DTB + hi] = I["dt_bias"][0, h]; prm[:, PR_BF + hi] = I["fox_b_f"][0, h]
        maps.append({
            "xT": xTb, "xq": np.ascontiguousarray(xTb[:, g * TQ:(g + 1) * TQ]), "memT": np.ascontiguousarray(I["mem"][b].T),
            "prm": prm, "cst": cst, "w1": w1, "w1s": w1s, "wmk": wmk, "wmv": wmv, **big,
        })
    return maps


def run(I, S, dbg=False, PH=(0, 1, 2, 3, 4, 5)):
    key = (S, dbg, PH)
    if key not in _CACHE:
        _CACHE[key] = build(S, dbg, PH)
    nc = _CACHE[key]
    maps = prep_inputs(I, S, PH)
    res = run_bass_kernel_spmd(nc, maps, core_ids=list(range(8)))
    return res


def kernel(**inputs):
    I = {k: np.asarray(v) for k, v in inputs.items()}
    S = I["x"].shape[1]
    res = run(I, S)
    TQ = S // 4
    out = np.empty((2, S, D), np.float32)
    for core in range(8):
        b, g = core // 4, core % 4
        out[b, g * TQ:(g + 1) * TQ, :] = res.results[core]["outT"].T
    return out
```
